# Optimizing a Trainium2 kernel written in Bass

```python
import math
import jax, jax.numpy as jnp
from jax import lax
import numpy as np

D_MODEL = 2048
BATCH = 4
SEQ = 2048
DEPTH = 4
DEC_BATCH = 128
DEC_SEQ = 1
PAST_LEN = 16384
PAGE_SIZE = 128

N_AB = (DEPTH + 1) // 2
N_C = DEPTH // 2
D_A = D_MODEL // 2
GS_A = 16
G_A = D_A // GS_A
N_A = 64
D_B = D_MODEL // 2
K_B = 128
H_B = D_B // K_B
V_B = D_B // H_B
D_IN_AB = D_A + 4 * D_B
CHUNK_B = 64
D_RNN = (D_MODEL * 4 // 3) // 256 * 256
NB_C = 16
BS_C = D_RNN // NB_C
CONV_W = 4
LRU_C = 8.0
D_FF = (D_MODEL * 8 // 3) // 128 * 128
N_MOD = 9
EPS = 1e-6

kernel_name = "hybrid_s5_hgrn2_rglru_decode_step"


def rmsnorm(x, w):
    xf = x.astype(jnp.float32)
    y = xf * lax.rsqrt(jnp.mean(xf * xf, axis=-1, keepdims=True) + EPS)
    return (y * w.astype(jnp.float32)).astype(x.dtype)


def modulate(h, shift, scale):
    return h * (1.0 + scale) + shift


def swiglu(h, w_gu, w_d):
    gate, up = jnp.split(h @ w_gu, 2, axis=-1)
    return (jax.nn.silu(gate) * up) @ w_d


def cmul(ar, ai, br, bi):
    return ar * br - ai * bi, ar * bi + ai * br


def complex_linear_scan(ar, ai, br, bi):
    def combine(e1, e2):
        a1r, a1i, b1r, b1i = e1
        a2r, a2i, b2r, b2i = e2
        nar, nai = cmul(a2r, a2i, a1r, a1i)
        tr, ti = cmul(a2r, a2i, b1r, b1i)
        return nar, nai, tr + b2r, ti + b2i
    return lax.associative_scan(combine, (ar, ai, br, bi), axis=1)


def real_linear_scan(a, b):
    def combine(e1, e2):
        a1, b1 = e1
        a2, b2 = e2
        return a1 * a2, a2 * b1 + b2
    return lax.associative_scan(combine, (a, b), axis=1)[1]


def s5_mixer(u, h0_re, h0_im, lam_re, lam_im, b_re, b_im, c_re, c_im, d_skip, log_step, w_glu, b_glu):
    f32 = jnp.float32
    Bsz, L, _ = u.shape
    uf = u.astype(f32).reshape(Bsz, L, G_A, GS_A)
    step = jnp.exp(log_step.astype(f32))[:, None]
    lr, li = lam_re.astype(f32), lam_im.astype(f32)
    mag = jnp.exp(lr * step)
    abar_r, abar_i = mag * jnp.cos(li * step), mag * jnp.sin(li * step)
    den = lr * lr + li * li
    pr, pim = abar_r - 1.0, abar_i
    zr = (pr * lr + pim * li) / den
    zi = (pim * lr - pr * li) / den
    bbr, bbi = cmul(zr[..., None], zi[..., None], b_re.astype(f32), b_im.astype(f32))
    xr = jnp.einsum("blgk,gnk->blgn", uf, bbr)
    xi = jnp.einsum("blgk,gnk->blgn", uf, bbi)
    ir, ii = cmul(abar_r, abar_i, h0_re.astype(f32), h0_im.astype(f32))
    xr = xr.at[:, 0].add(ir)
    xi = xi.at[:, 0].add(ii)
    ar = jnp.broadcast_to(abar_r, xr.shape)
    ai = jnp.broadcast_to(abar_i, xr.shape)
    _, _, hr, hi = complex_linear_scan(ar, ai, xr, xi)
    y = (jnp.einsum("blgn,gkn->blgk", hr, c_re.astype(f32))
         - jnp.einsum("blgn,gkn->blgk", hi, c_im.astype(f32)))
    y = y.reshape(Bsz, L, D_A) + d_skip.astype(f32) * u.astype(f32)
    y = jax.nn.gelu(y).astype(u.dtype)
    out = y * jax.nn.sigmoid(y @ w_glu + b_glu)
    return out, hr[:, -1].astype(h0_re.dtype), hi[:, -1].astype(h0_im.dtype)


def hgrn2_mixer(q, fz, iv, g, s0, lb, gnorm_w):
    f32 = jnp.float32
    Bsz, L, _ = q.shape
    lbf = lb.astype(f32)
    log_f = jnp.logaddexp(jnp.log(lbf), jnp.log1p(-lbf) + jax.nn.log_sigmoid(fz.astype(f32)))
    k = -jnp.expm1(log_f)
    C = min(CHUNK_B, L)
    n_chunks = -(-L // C)
    pad = n_chunks * C - L

    def to_chunks(t):
        t = jnp.pad(t.astype(f32), ((0, 0), (0, pad), (0, 0)))
        return t.reshape(Bsz, n_chunks, C, H_B, -1).transpose(1, 0, 3, 2, 4)

    qc, kc, vc, gc = to_chunks(q), to_chunks(k), to_chunks(iv), to_chunks(log_f)
    causal = jnp.tril(jnp.ones((C, C), dtype=bool))[:, :, None]

    def chunk_step(S, inp):
        qb, kb, vb, lfb = inp
        G = jnp.cumsum(lfb, axis=2)
        diff = G[:, :, :, None, :] - G[:, :, None, :, :]
        decay = jnp.exp(jnp.where(causal, diff, -jnp.inf))
        scores = jnp.einsum("bhtk,bhsk,bhtsk->bhts", qb, kb, decay)
        o = (jnp.einsum("bhts,bhsv->bhtv", scores, vb)
             + jnp.einsum("bhtk,bhkv->bhtv", qb * jnp.exp(G), S))
        G_last = G[:, :, -1:, :]
        S_new = (jnp.exp(G_last[:, :, 0, :, None]) * S
                 + jnp.einsum("bhsk,bhsv->bhkv", kb * jnp.exp(G_last - G), vb))
        return S_new, o

    S_fin, o = lax.scan(chunk_step, s0.astype(f32), (qc, kc, vc, gc))
    o = o.transpose(1, 0, 3, 2, 4).reshape(Bsz, n_chunks * C, H_B, V_B)[:, :L]
    o = o * lax.rsqrt(jnp.mean(o * o, axis=-1, keepdims=True) + EPS) * gnorm_w.astype(f32)
    o = o * jax.nn.silu(g.astype(f32).reshape(Bsz, L, H_B, V_B))
    return o.reshape(Bsz, L, D_B).astype(q.dtype), S_fin.astype(s0.dtype)


def block_diag(x, w, b):
    xb = x.reshape(x.shape[0], x.shape[1], NB_C, BS_C)
    return jnp.einsum("blnj,njk->blnk", xb, w).reshape(x.shape) + b


def rglru_block(h, conv_buf, h0, w_in_c, conv_w, conv_b, w_ga, b_ga, w_gx, b_gx, lru_lambda, w_out_c):
    f32 = jnp.float32
    gate_br, xr = jnp.split(h @ w_in_c, 2, axis=-1)
    L = xr.shape[1]
    xp = jnp.concatenate([conv_buf.astype(xr.dtype), xr], axis=1)
    xc = conv_b + xp[:, 0:L] * conv_w[0]
    for j in range(1, CONV_W):
        xc = xc + xp[:, j:j + L] * conv_w[j]
    new_buf = xp[:, L:]
    r = jax.nn.sigmoid(block_diag(xc, w_ga, b_ga).astype(f32))
    ig = jax.nn.sigmoid(block_diag(xc, w_gx, b_gx).astype(f32))
    log_a = LRU_C * r * jax.nn.log_sigmoid(lru_lambda.astype(f32))
    a = jnp.exp(log_a)
    bterm = jnp.sqrt(-jnp.expm1(2.0 * log_a)) * ig * xc.astype(f32)
    bterm = bterm.at[:, 0].add(a[:, 0] * h0.astype(f32))
    hs = real_linear_scan(a, bterm)
    y = (jax.nn.gelu(gate_br.astype(f32)) * hs).astype(h.dtype) @ w_out_c
    return y, new_buf.astype(conv_buf.dtype), hs[:, -1].astype(h0.dtype)


def run_group(x, c, s5_re0, s5_im0, hgrn0, lru0, conv0, P):
    Bsz = x.shape[0]
    lb_all = jnp.cumsum(jax.nn.softmax(P["hg_lb_logits"].astype(jnp.float32), axis=0), axis=0)
    lb_all = lb_all - lb_all[0:1]
    sc = jax.nn.silu(c)
    s5r_l, s5i_l, hg_l, lru_l, conv_l = [], [], [], [], []
    for l in range(DEPTH):
        j = l // 2
        m = (sc @ P["w_ada"][l] + P["b_ada"][l]).reshape(Bsz, N_MOD, 1, D_MODEL)
        h = modulate(rmsnorm(x, P["norm_w"][l, 0]), m[:, 0], m[:, 1])
        x = x + 0.5 * m[:, 2] * swiglu(h, P["w_ffn_gu"][l, 0], P["w_ffn_d"][l, 0])
        h = modulate(rmsnorm(x, P["norm_w"][l, 1]), m[:, 3], m[:, 4])
        if l % 2 == 0:
            z = h @ P["w_in_ab"][j]
            u, q, fz, iv, g = jnp.split(z, [D_A, D_A + D_B, D_A + 2 * D_B, D_A + 3 * D_B], axis=-1)
            ya, s5r, s5i = s5_mixer(u, s5_re0[j], s5_im0[j], P["s5_lam_re"][j], P["s5_lam_im"][j],
                                    P["s5_b_re"][j], P["s5_b_im"][j], P["s5_c_re"][j], P["s5_c_im"][j],
                                    P["s5_d"][j], P["s5_log_step"][j], P["s5_w_glu"][j], P["s5_b_glu"][j])
            yb, hgS = hgrn2_mixer(q, fz, iv, g, hgrn0[j], lb_all[j], P["hg_norm_w"][j])
            mix = jnp.concatenate([ya, yb], axis=-1) @ P["w_out_ab"][j]
            s5r_l.append(s5r)
            s5i_l.append(s5i)
            hg_l.append(hgS)
        else:
            mix, cb, hl = rglru_block(h, conv0[j], lru0[j], P["w_in_c"][j], P["conv_w"][j], P["conv_b"][j],
                                      P["w_gate_a"][j], P["b_gate_a"][j], P["w_gate_x"][j], P["b_gate_x"][j],
                                      P["lru_lambda"][j], P["w_out_c"][j])
            conv_l.append(cb)
            lru_l.append(hl)
        x = x + m[:, 5] * mix
        h = modulate(rmsnorm(x, P["norm_w"][l, 2]), m[:, 6], m[:, 7])
        x = x + 0.5 * m[:, 8] * swiglu(h, P["w_ffn_gu"][l, 1], P["w_ffn_d"][l, 1])
    y = rmsnorm(x, P["final_norm_w"])
    return y, jnp.stack(s5r_l), jnp.stack(s5i_l), jnp.stack(hg_l), jnp.stack(lru_l), jnp.stack(conv_l)


def setup_inputs(seed: int = 0) -> dict:
    key = jax.random.key(seed)
    ks = iter(jax.random.split(key, 64))
    f32 = jnp.float32

    def nrm(shape, scale):
        return jax.random.normal(next(ks), shape, f32) * scale

    n_idx = jnp.arange(N_A, dtype=f32)
    a0 = jax.random.uniform(next(ks), (N_C, D_RNN), f32, 0.9, 0.999)
    p = a0 ** (1.0 / LRU_C)
    return {
        "x_prompt": nrm((BATCH, SEQ, D_MODEL), 1.0),
        "x_sample": nrm((DEC_BATCH, DEC_SEQ, D_MODEL), 1.0),
        "state_s5_re": nrm((N_AB, DEC_BATCH, G_A, N_A), 0.1),
        "state_s5_im": nrm((N_AB, DEC_BATCH, G_A, N_A), 0.1),
        "state_hgrn": nrm((N_AB, DEC_BATCH, H_B, K_B, V_B), 0.5),
        "state_lru": nrm((N_C, DEC_BATCH, D_RNN), 0.5),
        "state_conv": nrm((N_C, DEC_BATCH, CONV_W - 1, D_RNN), 1.0),
        "c_prompt": nrm((BATCH, D_MODEL), 1.0),
        "c_sample": nrm((DEC_BATCH, D_MODEL), 1.0),
        "norm_w": 1.0 + nrm((DEPTH, 3, D_MODEL), 0.02),
        "final_norm_w": 1.0 + nrm((D_MODEL,), 0.02),
        "w_ada": nrm((DEPTH, D_MODEL, N_MOD * D_MODEL), 0.5 * D_MODEL ** -0.5),
        "b_ada": nrm((DEPTH, N_MOD * D_MODEL), 0.02),
        "w_ffn_gu": nrm((DEPTH, 2, D_MODEL, 2 * D_FF), D_MODEL ** -0.5),
        "w_ffn_d": nrm((DEPTH, 2, D_FF, D_MODEL), D_FF ** -0.5),
        "w_in_ab": nrm((N_AB, D_MODEL, D_IN_AB), D_MODEL ** -0.5),
        "s5_lam_re": -0.5 * (1.0 + nrm((N_AB, G_A, N_A), 0.01)),
        "s5_lam_im": math.pi * n_idx + nrm((N_AB, G_A, N_A), 0.01),
        "s5_b_re": nrm((N_AB, G_A, N_A, GS_A), (2 * GS_A) ** -0.5),
        "s5_b_im": nrm((N_AB, G_A, N_A, GS_A), (2 * GS_A) ** -0.5),
        "s5_c_re": nrm((N_AB, G_A, GS_A, N_A), N_A ** -0.5),
        "s5_c_im": nrm((N_AB, G_A, GS_A, N_A), N_A ** -0.5),
        "s5_d": nrm((N_AB, D_A), 0.5),
        "s5_log_step": jax.random.uniform(next(ks), (N_AB, G_A), f32, math.log(1e-3), math.log(1e-1)),
        "s5_w_glu": nrm((N_AB, D_A, D_A), D_A ** -0.5),
        "s5_b_glu": nrm((N_AB, D_A), 0.02),
        "hg_lb_logits": nrm((N_AB, D_B), 0.1),
        "hg_norm_w": 1.0 + nrm((N_AB, V_B), 0.02),
        "w_out_ab": nrm((N_AB, D_A + D_B, D_MODEL), (D_A + D_B) ** -0.5),
        "w_in_c": nrm((N_C, D_MODEL, 2 * D_RNN), D_MODEL ** -0.5),
        "conv_w": nrm((N_C, CONV_W, D_RNN), CONV_W ** -0.5),
        "conv_b": nrm((N_C, D_RNN), 0.02),
        "w_gate_a": nrm((N_C, NB_C, BS_C, BS_C), BS_C ** -0.5),
        "b_gate_a": nrm((N_C, D_RNN), 0.02),
        "w_gate_x": nrm((N_C, NB_C, BS_C, BS_C), BS_C ** -0.5),
        "b_gate_x": nrm((N_C, D_RNN), 0.02),
        "lru_lambda": jnp.log(p) - jnp.log1p(-p),
        "w_out_c": nrm((N_C, D_RNN, D_MODEL), D_RNN ** -0.5),
    }


def reference(x_prompt, x_sample, state_s5_re, state_s5_im, state_hgrn, state_lru, state_conv,
              c_prompt, c_sample, norm_w, final_norm_w, w_ada, b_ada, w_ffn_gu, w_ffn_d,
              w_in_ab, s5_lam_re, s5_lam_im, s5_b_re, s5_b_im, s5_c_re, s5_c_im, s5_d, s5_log_step,
              s5_w_glu, s5_b_glu, hg_lb_logits, hg_norm_w, w_out_ab, w_in_c, conv_w, conv_b,
              w_gate_a, b_gate_a, w_gate_x, b_gate_x, lru_lambda, w_out_c):
    P = dict(norm_w=norm_w, final_norm_w=final_norm_w, w_ada=w_ada, b_ada=b_ada,
             w_ffn_gu=w_ffn_gu, w_ffn_d=w_ffn_d, w_in_ab=w_in_ab,
             s5_lam_re=s5_lam_re, s5_lam_im=s5_lam_im, s5_b_re=s5_b_re, s5_b_im=s5_b_im,
             s5_c_re=s5_c_re, s5_c_im=s5_c_im, s5_d=s5_d, s5_log_step=s5_log_step,
             s5_w_glu=s5_w_glu, s5_b_glu=s5_b_glu, hg_lb_logits=hg_lb_logits, hg_norm_w=hg_norm_w,
             w_out_ab=w_out_ab, w_in_c=w_in_c, conv_w=conv_w, conv_b=conv_b,
             w_gate_a=w_gate_a, b_gate_a=b_gate_a, w_gate_x=w_gate_x, b_gate_x=b_gate_x,
             lru_lambda=lru_lambda, w_out_c=w_out_c)
    dt = x_prompt.dtype
    z_s5 = jnp.zeros((N_AB, BATCH, G_A, N_A), dt)
    z_hg = jnp.zeros((N_AB, BATCH, H_B, K_B, V_B), dt)
    z_lru = jnp.zeros((N_C, BATCH, D_RNN), dt)
    z_conv = jnp.zeros((N_C, BATCH, CONV_W - 1, D_RNN), dt)
    y_prompt, p_s5r, p_s5i, p_hg, p_lru, p_conv = run_group(x_prompt, c_prompt, z_s5, z_s5, z_hg, z_lru, z_conv, P)
    y_sample, s_s5r, s_s5i, s_hg, s_lru, s_conv = run_group(x_sample, c_sample, state_s5_re, state_s5_im,
                                                           state_hgrn, state_lru, state_conv, P)
    return (y_prompt, y_sample, p_s5r, p_s5i, p_hg, p_lru, p_conv, s_s5r, s_s5i, s_hg, s_lru, s_conv)
```

```python
import numpy as np
import concourse.bass as bass
import concourse.mybir as mybir
from concourse.bass_utils import run_bass_kernel_spmd
from contextlib import ExitStack

F32 = mybir.dt.float32
BF16 = mybir.dt.bfloat16
I32 = mybir.dt.int32
AF = mybir.ActivationFunctionType
ALU = mybir.AluOpType
AX = mybir.AxisListType

EPOCH = 30000
COMPUTE = ("pe", "act", "dve", "pool", "sp")

D = 2048
NPR = 1024
NS = 16
T = NPR + NS
DEPTH = 4
DFF = 5376
NCH = 16
TT = [(0, 512), (512, 512), (1024, 16)]
EPS = 1e-6
NCORES = 8


class Buf:
    __slots__ = ("w", "r")

    def __init__(self):
        self.w = {}
        self.r = {}


class Rec:
    __slots__ = ("fn", "waits", "marked", "dma")

    def __init__(self, fn):
        self.fn = fn
        self.waits = []
        self.marked = False
        self.dma = None


class Prog:
    def __init__(self, n_dma_sems=32, n_epochs=6):
        self.nc = bass.Bass("TRN2", target_bir_lowering=False)
        self.es = ExitStack()
        self.recs = {k: [] for k in COMPUTE}
        self.waited = {k: {} for k in COMPUTE}
        self.n_dma_sems = n_dma_sems
        self.n_epochs = n_epochs
        self.dma_tot = [0] * n_dma_sems
        self.dma_next = 0
        self._uid = 0
        self.colls = []

    def uid(self, p):
        self._uid += 1
        return f"{p}_{self._uid}"

    def sbuf(self, shape, dt, name=None):
        return self.es.enter_context(self.nc.sbuf_tensor(name or self.uid("sb"), list(shape), dt))

    def psum(self, shape, dt, name=None):
        return self.es.enter_context(self.nc.psum_tensor(name or self.uid("ps"), list(shape), dt))

    def dram_in(self, name, shape, dt=F32):
        return self.nc.dram_tensor(name, list(shape), dt, kind="ExternalInput").ap()

    def dram_out(self, name, shape, dt=F32):
        return self.nc.dram_tensor(name, list(shape), dt, kind="ExternalOutput").ap()

    def dram_tmp(self, name, shape, dt=F32):
        return self.nc.dram_tensor(name, list(shape), dt).ap()

    def _collect(self, reads, writes):
        deps = {}
        for b in reads:
            for k, v in b.w.items():
                if deps.get(k, -1) < v:
                    deps[k] = v
        for b in writes:
            for k, v in b.w.items():
                if deps.get(k, -1) < v:
                    deps[k] = v
            for k, v in b.r.items():
                if deps.get(k, -1) < v:
                    deps[k] = v
        return deps

    def _add_waits(self, eng, rec, deps):
        wd = self.waited[eng]
        for k, v in deps.items():
            if k == "pe" and eng == "pe":
                continue
            if wd.get(k, -1) >= v:
                continue
            wd[k] = v
            rec.waits.append((k, v))
            if isinstance(k, str):
                self.recs[k][v].marked = True

    def op(self, eng, fn, reads=(), writes=()):
        rec = Rec(fn)
        self._add_waits(eng, rec, self._collect(reads, writes))
        idx = len(self.recs[eng])
        self.recs[eng].append(rec)
        for b in reads:
            b.r[eng] = idx
        for b in writes:
            b.w = {eng: idx}
            b.r = {}
        return idx

    def dma(self, eng, out, in_, reads=(), writes=(), **kw):
        nsw = 8
        if eng == "pool":
            self.dma_next_sw = (getattr(self, "dma_next_sw", -1) + 1) % nsw
            s = self.n_dma_sems - nsw + self.dma_next_sw
        else:
            s = self.dma_next
            self.dma_next = (self.dma_next + 1) % (self.n_dma_sems - nsw)
        prev = self.dma_tot[s]
        tot = prev + 16
        self.dma_tot[s] = tot
        key = ("dma", s)

        def fn(e, out=out, in_=in_, kw=kw):
            return e.dma_start(out=out, in_=in_, **kw)

        rec = Rec(fn)
        rec.dma = (s, tot)
        deps = self._collect(reads, writes)
        if prev > 0:
            deps[key] = max(deps.get(key, -1), prev)
        self._add_waits(eng, rec, deps)
        self.recs[eng].append(rec)
        for b in reads:
            b.r[key] = tot
        for b in writes:
            b.w = {key: tot}
            b.r = {}

    def coll(self, kind, alu, ins, outs, groups, reads=(), writes=()):
        cid = len(self.colls)
        self.colls.append(None)
        key = ("cc", cid)

        def fn(e):
            return e.collective_compute(kind, alu, replica_groups=groups, ins=ins, outs=outs)

        rec = Rec(fn)
        rec.dma = ("cc", cid)
        self._add_waits("pool", rec, self._collect(reads, writes))
        self.recs["pool"].append(rec)
        for b in reads:
            b.r[key] = 1
        for b in writes:
            b.w = {key: 1}
            b.r = {}

    def build(self):
        nc = self.nc
        es = self.es
        sems = {k: [es.enter_context(nc.semaphore(f"s_{k}_{i}")) for i in range(self.n_epochs)] for k in COMPUTE}
        dsem = [es.enter_context(nc.semaphore(f"s_dma_{i}")) for i in range(self.n_dma_sems)]
        csem = [es.enter_context(nc.semaphore(f"s_cc_{i}")) for i in range(len(self.colls))]
        val = {}
        for k in COMPUTE:
            c = 0
            arr = []
            for r in self.recs[k]:
                if r.marked:
                    c += 1
                arr.append(c)
            val[k] = arr
            assert c < EPOCH * self.n_epochs, (k, c)
        fin = Rec(None)
        for s in range(self.n_dma_sems):
            if self.dma_tot[s] > 0:
                fin.waits.append((("dma", s), self.dma_tot[s]))
        self.recs["sp"].append(fin)
        val["sp"].append(val["sp"][-1] if val["sp"] else 0)

        def emit(k, e):
            for i, r in enumerate(self.recs[k]):
                for (wk, wv) in r.waits:
                    if isinstance(wk, str):
                        v = val[wk][wv]
                        e.wait_ge(sems[wk][(v - 1) // EPOCH], (v - 1) % EPOCH + 1)
                    elif wk[0] == "cc":
                        e.wait_ge(csem[wk[1]], 1)
                    else:
                        e.wait_ge(dsem[wk[1]], wv)
                if r.fn is None:
                    continue
                inst = r.fn(e)
                if r.dma is not None and r.dma[0] == "cc":
                    inst.then_inc(csem[r.dma[1]])
                elif r.dma is not None:
                    inst.then_inc(dsem[r.dma[0]], 16)
                elif r.marked:
                    v = val[k][i]
                    inst.then_inc(sems[k][(v - 1) // EPOCH], 1)

        with nc.Block() as block:
            @block.tensor
            def _(e):
                emit("pe", e)

            @block.scalar
            def _(e):
                emit("act", e)

            @block.vector
            def _(e):
                emit("dve", e)

            @block.gpsimd
            def _(e):
                emit("pool", e)

            @block.sync
            def _(e):
                emit("sp", e)
        es.close()
        return nc


def fm(v):
    v = np.asarray(v, np.float32)
    n = v.shape[-1] // 128
    lead = v.shape[:-1]
    r = v.reshape(lead + (n, 128))
    r = np.moveaxis(r, -1, 0)
    return np.ascontiguousarray(r.reshape(128, -1))


class Pack:
    def __init__(self):
        self.fields = {}
        self.cols = 0
        self.data = []

    def add(self, name, arr):
        arr = np.ascontiguousarray(arr, np.float32).reshape(128, -1)
        self.fields[name] = (self.cols, arr.shape[1])
        self.cols += arr.shape[1]
        self.data.append(arr)

    def array(self):
        return np.ascontiguousarray(np.concatenate(self.data, axis=1))


def pack_layout():
    pk = Pack()
    make_pack(pk, None)
    return pk


def make_pack(pk, inp):
    z = lambda *s: np.zeros(s, np.float32)
    g = (lambda k: inp[k]) if inp is not None else None
    pk.add("ident", np.eye(128, dtype=np.float32))
    pk.add("nw", fm(g("norm_w")) if inp is not None else z(128, 4 * 3 * 16))
    pk.add("fnw", fm(g("final_norm_w")) if inp is not None else z(128, 16))
    pk.add("bada", fm(g("b_ada")) if inp is not None else z(128, 4 * 144))
    pk.add("convw", fm(g("conv_w")) if inp is not None else z(128, 2 * 4 * 20))
    pk.add("convb", fm(g("conv_b")) if inp is not None else z(128, 2 * 20))
    pk.add("bga", fm(g("b_gate_a")) if inp is not None else z(128, 2 * 20))
    pk.add("bgx", fm(g("b_gate_x")) if inp is not None else z(128, 2 * 20))
    pk.add("lam", fm(g("lru_lambda")) if inp is not None else z(128, 2 * 20))
    pairl = lambda a: np.stack([a[j].reshape(2, 32, 64).transpose(0, 2, 1).reshape(128, 32) for j in range(2)], axis=1)
    pk.add("s5lamr", pairl(g("s5_lam_re")) if inp is not None else z(128, 64))
    pk.add("s5lami", pairl(g("s5_lam_im")) if inp is not None else z(128, 64))
    pk.add("s5lstep", np.stack([np.broadcast_to(g("s5_log_step")[j].reshape(2, 1, 32), (2, 64, 32)).reshape(128, 32) for j in range(2)], axis=1) if inp is not None else z(128, 64))
    pk.add("s5d2", np.stack([np.broadcast_to(g("s5_d")[j].reshape(1, 64, 16).transpose(0, 2, 1), (8, 16, 64)).reshape(128, 64) for j in range(2)], axis=1) if inp is not None else z(128, 128))
    pk.add("bglu", fm(g("s5_b_glu")) if inp is not None else z(128, 16))
    pk.add("hglog", fm(g("hg_lb_logits")) if inp is not None else z(128, 16))
    pk.add("hgnw", np.ascontiguousarray(g("hg_norm_w").T) if inp is not None else z(128, 2))
    r_ = np.arange(128)
    pk.add("tmask2", -((r_[None, :] // 16) < (r_[:, None] // 16)).astype(np.float32))
    cm = np.zeros((128, 64), np.float32); cm[0:64] = (np.arange(64)[None, :] >= np.arange(64)[:, None])
    pk.add("cmask64", cm)


class K:
    def __init__(self, cfg):
        self.cfg = cfg
        P = self.P = Prog()
        self.lay = pack_layout()
        self.xin = P.dram_in("xin", [T, D])
        self.cin = P.dram_in("cin", [NS + 1, D])
        self.pkd = P.dram_in("pk", [128, self.lay.cols])
        nl = max(1, cfg.get("layers", DEPTH))
        nf = nl if cfg.get("ffn", True) else 1
        self.w_ada = P.dram_in("w_ada", [nl, D, 9 * D])
        self.w_gu = P.dram_in("w_ffn_gu", [nf, 2, D, 2 * DFF])
        self.w_d = P.dram_in("w_ffn_d", [nf, 2, DFF, D])
        self.y = P.dram_out("y", [T, D])
        self.xres = P.sbuf([128, NCH, T], F32, "xres")
        self.bx = [[Buf() for _ in TT] for _ in range(NCH)]
        self.hT = P.sbuf([128, NCH, T], BF16, "hT")
        self.bh = [[Buf() for _ in TT] for _ in range(NCH)]
        self.arena = P.sbuf([128, 13408], F32, "arena")
        self.hid = self.arena[:, 0:7280].bitcast(BF16).rearrange("p (a t) -> p a t", t=T)
        self.bhid = [[Buf() for _ in TT] for _ in range(14)]
        self.pk = P.sbuf([128, self.lay.cols], F32, "pk_sb")
        self.bpk = Buf()
        self.mT = P.sbuf([128, 144, NS + 1], F32, "mT")
        self.bm = [Buf() for _ in range(9)]
        self.scT = P.sbuf([128, NCH, NS + 1], BF16, "scT")
        self.bsc = Buf()
        self.ones_bf = P.sbuf([128, 128], BF16, "ones_bf")
        self.bones = Buf()
        self.scr = self.arena[:, 7280:7280 + 5136]
        self.sm = self.arena[:, 7280 + 5136:13408]
        self.bscr = [Buf() for _ in range(8)]
        self.NSLOT = 3
        self.wsl = [P.sbuf([128, 4096], BF16, f"wsl{i}") for i in range(self.NSLOT)]
        self.bws = [Buf() for _ in range(self.NSLOT)]
        self.wnext = 0
        self.Amod = P.sbuf([128, NCH, NS + 1], F32, "Amod")
        self.bA = Buf()
        self.Gmod = [P.sbuf([128, NCH, NS + 1], F32, f"Gmod{i}") for i in range(3)]
        self.bG = [Buf() for _ in range(3)]
        self.rstd = P.sbuf([128, 512], F32, "rstd")
        self.brstd = Buf()
        self.tmpA = [P.sbuf([128, 512], F32, f"tmpA{i}") for i in range(2)]
        self.btmpA = [Buf(), Buf()]
        self.tmpB = [P.sbuf([128, 512], BF16, f"tmpB{i}") for i in range(2)]
        self.btmpB = [Buf(), Buf()]
        self.tmpS = P.sbuf([128, NCH, NS], F32, "tmpS")
        self.btmpS = Buf()
        self.rr = 0
        self.pst = [P.psum([128, 512], F32, f"ps{i}") for i in range(7)]
        self.bps = [Buf() for _ in range(7)]
        self.psn = 0
        self.ps_ada = P.psum([128, 512], F32, "ps_ada")
        self.bps_ada = Buf()
        self.ada_pending = []

    def ps(self):
        i = self.psn
        self.psn = (self.psn + 1) % 7
        return self.pst[i], self.bps[i]

    def fld(self, name, a=0, b=None):
        o, w = self.lay.fields[name]
        if b is None:
            b = w
        return self.pk[:, o + a:o + b]

    def wload(self, src, kc, ncols):
        i = self.wnext
        self.wnext = (self.wnext + 1) % self.NSLOT
        view = self.wsl[i][:, 0:kc * ncols].rearrange("p (c n) -> p c n", n=ncols)
        self.P.dma("pool", view, src.rearrange("(c p) n -> p c n", p=128), writes=[self.bws[i]])
        return view, self.bws[i]

    def ewise_eng(self):
        self.rr += 1
        return "act" if self.rr % 2 else "dve"

    def setup(self):
        P = self.P
        P.dma("sp", self.pk[:], self.pkd, writes=[self.bpk])
        P.op("dve", lambda e: e.memset(self.ones_bf[:], 1.0), writes=[self.bones])
        ident = self.fld("ident")
        cst = self.scr[0:NS + 1, 0:D]
        bc = self.bscr[0]
        P.dma("sp", cst, self.cin, writes=[bc])
        P.op("act", lambda e: e.activation(cst, cst, AF.Silu), reads=[bc], writes=[bc])
        pp, bp = self.ps()
        for c in range(NCH):
            P.op("pe", lambda e, c=c: e.transpose(pp[:, c * 17:(c + 1) * 17], cst[:, c * 128:(c + 1) * 128], ident[0:NS + 1, 0:NS + 1]),
                 reads=[bc, self.bpk], writes=[bp])
        P.op("dve", lambda e: e.tensor_copy(self.scT[:], pp[:, 0:NCH * 17].rearrange("p (c t) -> p c t", t=17)),
             reads=[bp], writes=[self.bsc])

    def load_x(self):
        P = self.P
        ident = self.fld("ident")
        stv = self.scr[:, 0:4096].rearrange("p (a f) -> p a f", f=D)
        bst = self.bscr[0]
        for g2 in range(NPR // 256):
            ti = (g2 * 256) // 512
            P.dma("sp", stv, self.xin[g2 * 256:(g2 + 1) * 256, :].rearrange("(a p) f -> p a f", p=128), writes=[bst])
            for c in range(NCH):
                pp, bp = self.ps()
                for a in range(2):
                    P.op("pe", lambda e, pp=pp, a=a, c=c: e.transpose(pp[:, a * 128:(a + 1) * 128], stv[:, a, c * 128:(c + 1) * 128], ident),
                         reads=[bst, self.bpk], writes=[bp])
                eng = self.ewise_eng()
                dst = self.xres[:, c, g2 * 256:(g2 + 1) * 256]
                if eng == "act":
                    P.op("act", lambda e, pp=pp, dst=dst: e.copy(dst, pp[:, 0:256]), reads=[bp], writes=[self.bx[c][ti]])
                else:
                    P.op("dve", lambda e, pp=pp, dst=dst: e.tensor_copy(dst, pp[:, 0:256]), reads=[bp], writes=[self.bx[c][ti]])
        sst = self.scr[0:NS, 0:D]
        P.dma("sp", sst, self.xin[NPR:T, :], writes=[self.bscr[0]])
        pp, bp = self.ps()
        for c in range(NCH):
            P.op("pe", lambda e, c=c: e.transpose(pp[:, c * NS:(c + 1) * NS], sst[:, c * 128:(c + 1) * 128], ident[0:NS, 0:NS]),
                 reads=[self.bscr[0], self.bpk], writes=[bp])
        P.op("dve", lambda e: e.tensor_copy(self.xres[:, :, NPR:T], pp[:, 0:NCH * NS].rearrange("p (c t) -> p c t", t=NS)),
             reads=[bp], writes=[self.bx[c][2] for c in range(NCH)])

    def ada_items(self, l):
        P = self.P
        items = []
        for blk in range(72):
            def item(blk=blk):
                oc0 = blk * 2
                wv, bw = self.wload(self.w_ada[l, :, oc0 * 128:(oc0 + 2) * 128], NCH, 256)
                grp = oc0 // 16
                for j in range(2):
                    oc = oc0 + j
                    loc = oc % 16
                    dst = self.ps_ada[:, loc * 17:(loc + 1) * 17]
                    for kc in range(NCH):
                        P.op("pe", lambda e, dst=dst, wv=wv, j=j, kc=kc: e.matmul(dst, wv[:, kc, j * 128:(j + 1) * 128], self.scT[:, kc, :], start=(kc == 0), stop=(kc == NCH - 1)),
                             reads=[bw, self.bsc], writes=[self.bps_ada])
                if oc0 % 16 == 14:
                    o, _ = self.lay.fields["bada"]
                    bias = self.pk[:, o + l * 144 + grp * 16:o + l * 144 + grp * 16 + 16].unsqueeze(2).broadcast_to([128, 16, NS + 1])
                    P.op("dve", lambda e, grp=grp, bias=bias: e.tensor_tensor(self.mT[:, grp * 16:(grp + 1) * 16, :], self.ps_ada[:, 0:16 * 17].rearrange("p (c t) -> p c t", t=17), bias, ALU.add),
                         reads=[self.bps_ada, self.bpk], writes=[self.bm[grp]])
            items.append(item)
        return items

    def pump_ada(self, n):
        for _ in range(n):
            if self.ada_pending:
                self.ada_pending.pop(0)()

    def derive_mod(self, l, s, gscale):
        P = self.P
        o, _ = self.lay.fields["nw"]
        nwb = self.pk[:, o + (l * 3 + s) * 16:o + (l * 3 + s) * 16 + 16].unsqueeze(2).broadcast_to([128, 16, NS + 1])
        sc = self.mT[:, (3 * s + 1) * 16:(3 * s + 2) * 16, :]
        P.op("dve", lambda e: e.scalar_tensor_tensor(self.Amod[:], sc, 1.0, nwb, ALU.add, ALU.mult),
             reads=[self.bm[3 * s + 1], self.bpk], writes=[self.bA])
        gt = self.mT[:, (3 * s + 2) * 16:(3 * s + 3) * 16, :]
        P.op("dve", lambda e: e.tensor_scalar(self.Gmod[s][:], gt, float(gscale), None, ALU.mult),
             reads=[self.bm[3 * s + 2]], writes=[self.bG[s]])

    def rms_stats(self, ti):
        P = self.P
        t0, n = TT[ti]
        pp, bp = self.ps()
        for c in range(NCH):
            k = c % 2
            P.op("act", lambda e, c=c, k=k: e.activation(self.tmpB[k][:, 0:n], self.xres[:, c, t0:t0 + n], AF.Square),
                 reads=[self.bx[c][ti]], writes=[self.btmpB[k]])
            P.op("pe", lambda e, c=c, k=k: e.matmul(pp[:, 0:n], self.ones_bf[:], self.tmpB[k][:, 0:n], start=(c == 0), stop=(c == NCH - 1)),
                 reads=[self.btmpB[k], self.bones], writes=[bp])
        return pp, bp

    def norm_mod(self, l, s):
        P = self.P
        shift = lambda c, a, b: self.mT[:, 3 * s * 16 + c, a:b]
        for ti in range(3):
            t0, n = TT[ti]
            pp, bp = self.rms_stats(ti)
            P.op("act", lambda e, pp=pp, n=n: e.activation(self.rstd[:, 0:n], pp[:, 0:n], AF.Sqrt, bias=self.eps_t[:, 0:1], scale=1.0 / D),
                 reads=[bp, self.beps], writes=[self.brstd])
            P.op("dve", lambda e, n=n: e.reciprocal(self.rstd[:, 0:n], self.rstd[:, 0:n]), reads=[self.brstd], writes=[self.brstd])
            if ti < 2:
                for c in range(NCH):
                    k = c % 2
                    P.op("dve", lambda e, c=c, k=k, t0=t0, n=n: e.scalar_tensor_tensor(self.tmpA[k][:], self.xres[:, c, t0:t0 + n], self.Amod[:, c, 0:1], self.rstd[:], ALU.mult, ALU.mult),
                         reads=[self.bx[c][ti], self.bA, self.brstd], writes=[self.btmpA[k]])
                    P.op("act", lambda e, c=c, k=k, t0=t0, n=n: e.activation(self.hT[:, c, t0:t0 + n], self.tmpA[k][:], AF.Identity, bias=shift(c, 0, 1)),
                         reads=[self.btmpA[k], self.bm[3 * s]], writes=[self.bh[c][ti]])
            else:
                xs = self.xres[:, :, NPR:T]
                rb = self.rstd[:, 0:NS].unsqueeze(1).broadcast_to([128, NCH, NS])
                allx = [self.bx[c][2] for c in range(NCH)]
                P.op("dve", lambda e: e.tensor_tensor(self.tmpS[:], xs, rb, ALU.mult), reads=allx + [self.brstd], writes=[self.btmpS])
                P.op("dve", lambda e: e.tensor_tensor(self.tmpS[:], self.tmpS[:], self.Amod[:, :, 1:NS + 1], ALU.mult), reads=[self.btmpS, self.bA], writes=[self.btmpS])
                P.op("dve", lambda e: e.tensor_tensor(self.hT[:, :, NPR:T], self.tmpS[:], self.mT[:, 3 * s * 16:3 * s * 16 + 16, 1:NS + 1], ALU.add),
                     reads=[self.btmpS, self.bm[3 * s]], writes=[self.bh[c][2] for c in range(NCH)])

    def resid_add(self, pp, bp, oc, ti, G, bG):
        P = self.P
        t0, n = TT[ti]
        if ti < 2:
            P.op("dve", lambda e: e.scalar_tensor_tensor(self.xres[:, oc, t0:t0 + n], pp[:, 0:n], G[:, oc, 0:1], self.xres[:, oc, t0:t0 + n], ALU.mult, ALU.add),
                 reads=[bp, bG, self.bx[oc][ti]], writes=[self.bx[oc][ti]])
        else:
            tmp = self.tmpS[:, 0, :]
            P.op("dve", lambda e: e.tensor_tensor(tmp, pp[:, 0:n], G[:, oc, 1:NS + 1], ALU.mult), reads=[bp, bG], writes=[self.btmpS])
            P.op("dve", lambda e: e.tensor_tensor(self.xres[:, oc, t0:t0 + n], self.xres[:, oc, t0:t0 + n], tmp, ALU.add),
                 reads=[self.btmpS, self.bx[oc][ti]], writes=[self.bx[oc][ti]])

    def ffn(self, l, w, s, pump=0):
        P = self.P
        G, bG = self.Gmod[s], self.bG[s]
        for third in range(3):
            for hp in range(7):
                hc0 = third * 14 + hp * 2
                wg, bwg = self.wload(self.w_gu[l, w, :, hc0 * 128:(hc0 + 2) * 128], NCH, 256)
                wu, bwu = self.wload(self.w_gu[l, w, :, DFF + hc0 * 128:DFF + (hc0 + 2) * 128], NCH, 256)
                for j in range(2):
                    hl = hp * 2 + j
                    for ti in range(3):
                        t0, n = TT[ti]
                        pg, bpg = self.ps()
                        pu, bpu = self.ps()
                        for kc in range(NCH):
                            P.op("pe", lambda e, pg=pg, kc=kc, j=j, wg=wg, t0=t0, n=n: e.matmul(pg[:, 0:n], wg[:, kc, j * 128:(j + 1) * 128], self.hT[:, kc, t0:t0 + n], start=(kc == 0), stop=(kc == NCH - 1)),
                                 reads=[bwg, self.bh[kc][ti]], writes=[bpg])
                        for kc in range(NCH):
                            P.op("pe", lambda e, pu=pu, kc=kc, j=j, wu=wu, t0=t0, n=n: e.matmul(pu[:, 0:n], wu[:, kc, j * 128:(j + 1) * 128], self.hT[:, kc, t0:t0 + n], start=(kc == 0), stop=(kc == NCH - 1)),
                                 reads=[bwu, self.bh[kc][ti]], writes=[bpu])
                        k = (hl * 3 + ti) % 2
                        P.op("act", lambda e, pg=pg, k=k, n=n: e.activation(self.tmpA[k][:, 0:n], pg[:, 0:n], AF.Silu), reads=[bpg], writes=[self.btmpA[k]])
                        P.op("dve", lambda e, pu=pu, k=k, n=n, hl=hl, t0=t0: e.tensor_tensor(self.hid[:, hl, t0:t0 + n], self.tmpA[k][:, 0:n], pu[:, 0:n], ALU.mult),
                             reads=[self.btmpA[k], bpu], writes=[self.bhid[hl][ti]])
                self.pump_ada(pump)
            for ocp in range(8):
                wd, bwd = self.wload(self.w_d[l, w, third * 1792:(third + 1) * 1792, ocp * 256:(ocp + 1) * 256], 14, 256)
                for j in range(2):
                    oc = ocp * 2 + j
                    for ti in range(3):
                        t0, n = TT[ti]
                        pp, bp = self.ps()
                        for kc in range(14):
                            P.op("pe", lambda e, pp=pp, kc=kc, j=j, wd=wd, t0=t0, n=n: e.matmul(pp[:, 0:n], wd[:, kc, j * 128:(j + 1) * 128], self.hid[:, kc, t0:t0 + n], start=(kc == 0), stop=(kc == 13)),
                                 reads=[bwd, self.bhid[kc][ti]], writes=[bp])
                        self.resid_add(pp, bp, oc, ti, G, bG)
                self.pump_ada(pump)

    def final(self):
        P = self.P
        ident = self.fld("ident")
        fo, _ = self.lay.fields["fnw"]
        rst = self.scr[:, 0:1040]
        brst = self.bscr[0]
        for ti in range(3):
            t0, n = TT[ti]
            pp, bp = self.rms_stats(ti)
            P.op("act", lambda e, pp=pp, n=n, t0=t0: e.activation(rst[:, t0:t0 + n], pp[:, 0:n], AF.Sqrt, bias=self.eps_t[:, 0:1], scale=1.0 / D),
                 reads=[bp, self.beps], writes=[brst])
        P.op("dve", lambda e: e.reciprocal(rst, rst), reads=[brst], writes=[brst])
        for c in range(NCH):
            for ti in range(3):
                t0, n = TT[ti]
                P.op("dve", lambda e, c=c, t0=t0, n=n: e.scalar_tensor_tensor(self.xres[:, c, t0:t0 + n], self.xres[:, c, t0:t0 + n], self.pk[:, fo + c:fo + c + 1], rst[:, t0:t0 + n], ALU.mult, ALU.mult),
                     reads=[self.bx[c][ti], self.bpk, brst], writes=[self.bx[c][ti]])
        ost = [self.scr[:, 1040:3088], self.scr[:, 3088:5136]]
        bost = [self.bscr[1], self.bscr[2]]
        ntile = NPR // 128
        for a in range(ntile + 1):
            rows = 128 if a < ntile else NS
            k = a % 2
            ti = (a * 128) // 512 if a < ntile else 2
            for q in range(4):
                pp, bp = self.ps()
                for cc in range(4):
                    c = q * 4 + cc
                    P.op("pe", lambda e, pp=pp, cc=cc, c=c, a=a, rows=rows: e.transpose(pp[0:rows, cc * 128:(cc + 1) * 128], self.xres[:, c, a * 128:a * 128 + rows], ident),
                         reads=[self.bx[c][ti], self.bpk], writes=[bp])
                eng = self.ewise_eng()
                dst = ost[k][0:rows, q * 512:(q + 1) * 512]
                if eng == "act":
                    P.op("act", lambda e, pp=pp, dst=dst, rows=rows: e.copy(dst, pp[0:rows, :]), reads=[bp], writes=[bost[k]])
                else:
                    P.op("dve", lambda e, pp=pp, dst=dst, rows=rows: e.tensor_copy(dst, pp[0:rows, :]), reads=[bp], writes=[bost[k]])
            P.dma("sp", self.y[a * 128:a * 128 + rows, :], ost[k][0:rows, :], reads=[bost[k]])

    def alias(self, new_bufs, old_bufs):
        w = {}
        for ob in old_bufs:
            for d in (ob.w, ob.r):
                for k, v in d.items():
                    if w.get(k, -1) < v:
                        w[k] = v
        for nb in new_bufs:
            nb.w = {}
            nb.r = dict(w)

    def arena_bufs(self):
        return [b for row in self.bhid for b in row] + list(self.bscr)

    def odd_init(self):
        P = self.P
        cfg = self.cfg
        self.w_in_c = P.dram_in("w_in_c", [2, D, 5120])
        self.w_out_c = P.dram_in("w_out_c", [2, 2560, D])
        self.wgate = P.dram_in("wgate", [2, 20, 128, 768])
        self.st_lru = P.dram_in("st_lru", [2, 128, 20, NS])
        self.st_conv = P.dram_in("st_conv", [2, 128, 20, 3, NS])
        self.pmd = P.dram_in("pm", [128, 1])
        self.o_lru_p = P.dram_out("o_lru_p", [2, 128, 20])
        self.o_lru_s = P.dram_out("o_lru_s", [2, 128, 20, NS])
        self.o_conv_p = P.dram_out("o_conv_p", [2, 128, 20, 3])
        self.o_conv_s = P.dram_out("o_conv_s", [2, 128, 20, 3, NS])
        self.a_d = P.dram_tmp("a_d", [20, 128, NPR])
        self.b_d = P.dram_tmp("b_d", [20, 128, NPR])
        self.ba_d = [[Buf(), Buf()] for _ in range(20)]
        self.bb_d = [[Buf(), Buf()] for _ in range(20)]
        self.agin1 = P.dram_tmp("agin1", [128, 60]); self.agout1 = P.dram_tmp("agout1", [256, 60])
        self.agin2 = P.dram_tmp("agin2", [128, 20]); self.agout2 = P.dram_tmp("agout2", [256, 20])
        self.bag = [Buf() for _ in range(4)]
        self.pm = P.sbuf([128, 1], F32, "pm_sb"); self.bpm = Buf()
        P.dma("sp", self.pm[:], self.pmd, writes=[self.bpm])
        self.sm = self.arena[:, 10807:13408]
        self.bsm = {}
        off = [0]

        def carve(name, n):
            v = self.sm[:, off[0]:off[0] + n]
            off[0] += n
            self.bsm[name] = Buf()
            return v
        self.clam = carve("clam", 20)
        self.tails = carve("tails", 60)
        self.halo = carve("halo", 60)
        self.hend = carve("hend", 20)
        self.hinit = carve("hinit", 20)
        self.hcur = carve("hcur", 20)
        self.xs_all = carve("xs_all", 320)
        self.xc_s = carve("xc_s", 320)
        self.hs_all = carve("hs_all", 320)
        self.h0s = carve("h0s", 320)
        self.tsm = carve("tsm", 80)
        self.tsm2 = carve("tsm2", 80)
        self.tsm_big = carve("cbuf", 960)
        self.bcbuf = self.bsm["cbuf"]
        assert off[0] <= 2601, off[0]

    def band_tiles(self, m):
        lo = 160 * ((128 * m) // 160)
        kc0 = lo // 128
        res = []
        for s in range(3):
            kc = kc0 + s
            if kc > 19:
                continue
            blocks_out = set(range((128 * m) // 160, (128 * m + 127) // 160 + 1))
            blocks_in = set(range((128 * kc) // 160, (128 * kc + 127) // 160 + 1))
            if blocks_out & blocks_in:
                res.append((s, kc))
        return kc0, res

    def odd_mixer(self, l):
        P = self.P
        j = l // 2
        o_cw, _ = self.lay.fields["convw"]
        o_cb, _ = self.lay.fields["convb"]
        o_ga, _ = self.lay.fields["bga"]
        o_gx, _ = self.lay.fields["bgx"]
        o_lm, _ = self.lay.fields["lam"]
        cw = lambda tap, m: self.pk[:, o_cw + j * 80 + tap * 20 + m:o_cw + j * 80 + tap * 20 + m + 1]
        cwv = lambda tap, m0, k: self.pk[:, o_cw + j * 80 + tap * 20 + m0:o_cw + j * 80 + tap * 20 + m0 + k]
        cb = lambda m: self.pk[:, o_cb + j * 20 + m:o_cb + j * 20 + m + 1]
        cbv = lambda m0, k: self.pk[:, o_cb + j * 20 + m0:o_cb + j * 20 + m0 + k]
        bga = lambda m: self.pk[:, o_ga + j * 20 + m:o_ga + j * 20 + m + 1]
        bgx = lambda m: self.pk[:, o_gx + j * 20 + m:o_gx + j * 20 + m + 1]
        bs = self.bsm
        G, bG = self.Gmod[1], self.bG[1]
        xcb = self.arena[:, 0:2600].bitcast(BF16).rearrange("p (a t) -> p a t", t=T)
        bxcb = [[Buf() for _ in range(3)] for _ in range(5)]
        tmp = [self.arena[:, 2600 + i * 512:2600 + (i + 1) * 512] for i in range(6)]
        btmp = [Buf() for _ in range(6)]
        xp = self.arena[:, 5672:5672 + 5 * (NPR + 3)].rearrange("p (a t) -> p a t", t=NPR + 3)
        bxp = [[Buf() for _ in range(2)] for _ in range(5)]
        bxph = Buf()
        mine = [b for r in bxcb for b in r] + btmp + [b for r in bxp for b in r] + [bxph] + list(self.bsm.values())
        self.alias(mine, self.arena_bufs())

        lam = self.pk[:, o_lm + j * 20:o_lm + j * 20 + 20]
        P.op("act", lambda e: e.activation(self.clam, lam, AF.Sigmoid), reads=[self.bpk], writes=[bs["clam"]])
        P.op("act", lambda e: e.activation(self.clam, self.clam, AF.Ln), reads=[bs["clam"]], writes=[bs["clam"]])
        P.op("dve", lambda e: e.tensor_scalar(self.clam, self.clam, 8.0, None, ALU.mult), reads=[bs["clam"]], writes=[bs["clam"]])
        cbuf = self.tsm_big
        P.dma("sp", self.h0s.rearrange("p (a t) -> p a t", t=NS), self.st_lru[j], writes=[bs["h0s"]])
        P.dma("sp", cbuf[:].rearrange("p (a k t) -> p a k t", k=3, t=NS), self.st_conv[j], writes=[self.bcbuf])

        pp, bp = self.ps()
        for pr in range(10):
            wv, bw = self.wload(self.w_in_c[j, :, 2560 + pr * 256:2560 + (pr + 1) * 256], NCH, 256)
            for jj in range(2):
                m = pr * 2 + jj
                for kc in range(NCH):
                    P.op("pe", lambda e, m=m, jj=jj, kc=kc, wv=wv: e.matmul(pp[:, m * 3:(m + 1) * 3], wv[:, kc, jj * 128:(jj + 1) * 128], self.hT[:, kc, NPR - 3:NPR], start=(kc == 0), stop=(kc == NCH - 1)),
                         reads=[bw, self.bh[kc][1]], writes=[bp])
        P.op("dve", lambda e: e.tensor_copy(self.tails, pp[:, 0:60]), reads=[bp], writes=[bs["tails"]])
        P.dma("sp", self.o_conv_p[j].rearrange("p a k -> p (a k)"), self.tails, reads=[bs["tails"]])
        P.dma("sp", self.agin1, self.tails, reads=[bs["tails"]], writes=[self.bag[0]])
        P.coll("AllGather", ALU.bypass, [self.agin1], [self.agout1], [[0, 1], [2, 3], [4, 5], [6, 7]], reads=[self.bag[0]], writes=[self.bag[1]])
        P.dma("sp", self.halo, self.agout1[0:128, :], reads=[self.bag[1]], writes=[bs["halo"]])
        P.op("dve", lambda e: e.tensor_scalar(self.halo, self.halo, self.pm[:, 0:1], None, ALU.mult), reads=[bs["halo"], self.bpm], writes=[bs["halo"]])

        xs3 = self.xs_all.rearrange("p (a t) -> p a t", t=NS)
        xcs3 = self.xc_s.rearrange("p (a t) -> p a t", t=NS)
        hs3 = self.hs_all.rearrange("p (a t) -> p a t", t=NS)
        h0s3 = self.h0s.rearrange("p (a t) -> p a t", t=NS)
        cb4 = cbuf[:].rearrange("p (a k t) -> p a k t", k=3, t=NS)
        cnt = [0]
        for g in range(4):
            m0 = g * 5
            P.op("dve", lambda e, m0=m0: e.tensor_copy(xp[:, :, 0:3], self.halo[:, m0 * 3:(m0 + 5) * 3].rearrange("p (a k) -> p a k", k=3)),
                 reads=[bs["halo"]], writes=[bxph])
            for (ma, nb) in ((0, 2), (2, 2), (4, 1)):
                wv, bw = self.wload(self.w_in_c[j, :, 2560 + (m0 + ma) * 128:2560 + (m0 + ma + nb) * 128], NCH, nb * 128)
                for jj in range(nb):
                    ml = ma + jj
                    m = m0 + ml
                    for ti in range(3):
                        t0, n = TT[ti]
                        ps_, bps_ = self.ps()
                        for kc in range(NCH):
                            P.op("pe", lambda e, ps_=ps_, jj=jj, kc=kc, wv=wv, t0=t0, n=n: e.matmul(ps_[:, 0:n], wv[:, kc, jj * 128:(jj + 1) * 128], self.hT[:, kc, t0:t0 + n], start=(kc == 0), stop=(kc == NCH - 1)),
                                 reads=[bw, self.bh[kc][ti]], writes=[bps_])
                        if ti < 2:
                            P.op("act", lambda e, ps_=ps_, ml=ml, t0=t0, n=n: e.copy(xp[:, ml, 3 + t0:3 + t0 + n], ps_[:, 0:n]), reads=[bps_], writes=[bxp[ml][ti]])
                        else:
                            P.op("act", lambda e, ps_=ps_, m=m: e.copy(xs3[:, m, :], ps_[:, 0:NS]), reads=[bps_], writes=[bs["xs_all"]])
            for ml in range(5):
                m = m0 + ml
                for ti in range(2):
                    t0, n = TT[ti]
                    tx = tmp[4]; btx = btmp[4]
                    rd = [bxp[ml][0], bxp[ml][1], bxph, self.bpk]
                    P.op("dve", lambda e, ml=ml, m=m, t0=t0, n=n, tx=tx: e.tensor_scalar(tx, xp[:, ml, t0:t0 + n], cw(0, m), cb(m), ALU.mult, ALU.add), reads=rd, writes=[btx])
                    P.op("dve", lambda e, ml=ml, m=m, t0=t0, n=n, tx=tx: e.scalar_tensor_tensor(tx, xp[:, ml, t0 + 1:t0 + 1 + n], cw(1, m), tx, ALU.mult, ALU.add), reads=rd + [btx], writes=[btx])
                    P.op("dve", lambda e, ml=ml, m=m, t0=t0, n=n, tx=tx: e.scalar_tensor_tensor(tx, xp[:, ml, t0 + 2:t0 + 2 + n], cw(2, m), tx, ALU.mult, ALU.add), reads=rd + [btx], writes=[btx])
                    P.op("dve", lambda e, ml=ml, m=m, t0=t0, n=n, tx=tx: e.scalar_tensor_tensor(xcb[:, ml, t0:t0 + n], xp[:, ml, t0 + 3:t0 + 3 + n], cw(3, m), tx, ALU.mult, ALU.add), reads=rd + [btx], writes=[bxcb[ml][ti]])
            bc_ = lambda v: v.unsqueeze(2).broadcast_to([128, 5, NS])
            t5 = self.tsm.rearrange("p (a t) -> p a t", t=NS)
            t5b = self.tsm2.rearrange("p (a t) -> p a t", t=NS)
            P.op("dve", lambda e, m0=m0: e.tensor_tensor(t5, cb4[:, m0:m0 + 5, 0, :], bc_(cwv(0, m0, 5)), ALU.mult), reads=[self.bcbuf, self.bpk], writes=[bs["tsm"]])
            for tap in (1, 2):
                P.op("dve", lambda e, m0=m0, tap=tap: e.tensor_tensor(t5b, cb4[:, m0:m0 + 5, tap, :], bc_(cwv(tap, m0, 5)), ALU.mult), reads=[self.bcbuf, self.bpk], writes=[bs["tsm2"]])
                P.op("dve", lambda e: e.tensor_tensor(t5, t5, t5b, ALU.add), reads=[bs["tsm"], bs["tsm2"]], writes=[bs["tsm"]])
            P.op("dve", lambda e, m0=m0: e.tensor_tensor(t5b, xs3[:, m0:m0 + 5, :], bc_(cwv(3, m0, 5)), ALU.mult), reads=[bs["xs_all"], self.bpk], writes=[bs["tsm2"]])
            P.op("dve", lambda e: e.tensor_tensor(t5, t5, t5b, ALU.add), reads=[bs["tsm"], bs["tsm2"]], writes=[bs["tsm"]])
            P.op("dve", lambda e, m0=m0: e.tensor_tensor(xcs3[:, m0:m0 + 5, :], t5, bc_(cbv(m0, 5)), ALU.add), reads=[bs["tsm"], self.bpk], writes=[bs["xc_s"]])
            P.op("dve", lambda e, m0=m0: e.tensor_copy(xcb[:, :, NPR:T], xcs3[:, m0:m0 + 5, :]), reads=[bs["xc_s"]], writes=[bxcb[a][2] for a in range(5)])
            for ml in range(5):
                m = m0 + ml
                kc0, nz = self.band_tiles(m)
                i = self.wnext
                self.wnext = (self.wnext + 1) % self.NSLOT
                wv = self.wsl[i][:, 0:768].rearrange("p (s n) -> p s n", n=128)
                bw = self.bws[i]
                P.dma("pool", self.wsl[i][:, 0:768], self.wgate[j, m], writes=[bw])
                for ti in range(3):
                    t0, n = TT[ti]
                    pa, bpa = self.ps()
                    px, bpx = self.ps()
                    for gi, (pt, bpt) in enumerate(((pa, bpa), (px, bpx))):
                        for q, (s, kc) in enumerate(nz):
                            P.op("pe", lambda e, pt=pt, gi=gi, s=s, kc=kc, wv=wv, t0=t0, n=n, q=q, m0=m0: e.matmul(pt[:, 0:n], wv[:, gi * 3 + s, :], xcb[:, kc - m0, t0:t0 + n], start=(q == 0), stop=(q == len(nz) - 1)),
                                 reads=[bw, bxcb[kc - m0][ti]], writes=[bpt])
                    if ti < 2:
                        k2 = cnt[0] % 2
                        cnt[0] += 1
                        ta, bta = tmp[0 + k2], btmp[0 + k2]
                        tb, btb = tmp[2 + k2], btmp[2 + k2]
                        tq, btq = tmp[4], btmp[4]
                        txc, btxc = tmp[5], btmp[5]
                        th, bth = tmp[5], btmp[5]
                        P.op("act", lambda e, pa=pa, ta=ta, m=m: e.activation(ta, pa[:, 0:512], AF.Sigmoid, bias=bga(m)), reads=[bpa, self.bpk], writes=[bta])
                        P.op("act", lambda e, px=px, tb=tb, m=m: e.activation(tb, px[:, 0:512], AF.Sigmoid, bias=bgx(m)), reads=[bpx, self.bpk], writes=[btb])
                        P.op("act", lambda e, ta=ta, m=m: e.activation(ta, ta, AF.Exp, scale=self.clam[:, m:m + 1]), reads=[bta, bs["clam"]], writes=[bta])
                        P.op("dve", lambda e, ta=ta, tq=tq: e.tensor_tensor(tq, ta, ta, ALU.mult), reads=[bta], writes=[btq])
                        P.op("dve", lambda e, tq=tq: e.tensor_scalar(tq, tq, -1.0, 1.0, ALU.mult, ALU.add), reads=[btq], writes=[btq])
                        P.op("act", lambda e, tq=tq: e.activation(tq, tq, AF.Sqrt), reads=[btq], writes=[btq])
                        rd = [bxp[ml][0], bxp[ml][1], bxph, self.bpk]
                        P.op("dve", lambda e, ml=ml, m=m, t0=t0, n=n, txc=txc: e.tensor_scalar(txc, xp[:, ml, t0:t0 + n], cw(0, m), cb(m), ALU.mult, ALU.add), reads=rd, writes=[btxc])
                        for tap in (1, 2, 3):
                            P.op("dve", lambda e, ml=ml, m=m, t0=t0, n=n, txc=txc, tap=tap: e.scalar_tensor_tensor(txc, xp[:, ml, t0 + tap:t0 + tap + n], cw(tap, m), txc, ALU.mult, ALU.add), reads=rd + [btxc], writes=[btxc])
                        P.op("dve", lambda e, tb=tb, tq=tq: e.tensor_tensor(tb, tb, tq, ALU.mult), reads=[btb, btq], writes=[btb])
                        P.op("dve", lambda e, tb=tb, txc=txc: e.tensor_tensor(tb, tb, txc, ALU.mult), reads=[btb, btxc], writes=[btb])
                        P.dma("sp", self.a_d[m, :, t0:t0 + n], ta, reads=[bta], writes=[self.ba_d[m][ti]])
                        P.dma("sp", self.b_d[m, :, t0:t0 + n], tb, reads=[btb], writes=[self.bb_d[m][ti]])
                        init = 0.0 if ti == 0 else self.hend[:, m:m + 1]
                        P.op("dve", lambda e, ta=ta, tb=tb, th=th, init=init: e.tensor_tensor_scan(th, ta, tb, init, ALU.mult, ALU.add),
                             reads=[bta, btb, bs["hend"]], writes=[bth])
                        P.op("act", lambda e, th=th, m=m: e.copy(self.hend[:, m:m + 1], th[:, 511:512]), reads=[bth], writes=[bs["hend"]])
                    else:
                        sa, sb_ = self.tsm[:, 0:NS], self.tsm[:, NS:2 * NS]
                        sq_ = self.tsm[:, 2 * NS:3 * NS]
                        bt = bs["tsm"]
                        P.op("act", lambda e, pa=pa, m=m: e.activation(sa, pa[:, 0:NS], AF.Sigmoid, bias=bga(m)), reads=[bpa, self.bpk], writes=[bt])
                        P.op("act", lambda e, px=px, m=m: e.activation(sb_, px[:, 0:NS], AF.Sigmoid, bias=bgx(m)), reads=[bpx, self.bpk], writes=[bt])
                        P.op("act", lambda e, m=m: e.activation(sa, sa, AF.Exp, scale=self.clam[:, m:m + 1]), reads=[bt, bs["clam"]], writes=[bt])
                        P.op("dve", lambda e: e.tensor_tensor(sq_, sa, sa, ALU.mult), reads=[bt], writes=[bt])
                        P.op("dve", lambda e: e.tensor_scalar(sq_, sq_, -1.0, 1.0, ALU.mult, ALU.add), reads=[bt], writes=[bt])
                        P.op("act", lambda e: e.activation(sq_, sq_, AF.Sqrt), reads=[bt], writes=[bt])
                        P.op("dve", lambda e: e.tensor_tensor(sb_, sb_, sq_, ALU.mult), reads=[bt], writes=[bt])
                        P.op("dve", lambda e, m=m: e.tensor_tensor(sb_, sb_, xcs3[:, m, :], ALU.mult), reads=[bt, bs["xc_s"]], writes=[bt])
                        P.op("dve", lambda e, m=m: e.tensor_tensor(sa, sa, h0s3[:, m, :], ALU.mult), reads=[bt, bs["h0s"]], writes=[bt])
                        P.op("dve", lambda e, m=m: e.tensor_tensor(hs3[:, m, :], sa, sb_, ALU.add), reads=[bt], writes=[bs["hs_all"]])
        P.dma("sp", self.agin2, self.hend, reads=[bs["hend"]], writes=[self.bag[2]])
        P.coll("AllGather", ALU.bypass, [self.agin2], [self.agout2], [[0, 1], [2, 3], [4, 5], [6, 7]], reads=[self.bag[2]], writes=[self.bag[3]])
        P.dma("sp", self.hinit, self.agout2[0:128, :], reads=[self.bag[3]], writes=[bs["hinit"]])
        P.op("dve", lambda e: e.tensor_scalar(self.hcur, self.hinit, self.pm[:, 0:1], None, ALU.mult), reads=[bs["hinit"], self.bpm], writes=[bs["hcur"]])
        P.dma("sp", self.o_lru_s[j], hs3, reads=[bs["hs_all"]])
        P.dma("sp", self.o_conv_s[j, :, :, 0:2, :], cb4[:, :, 1:3, :], reads=[self.bcbuf])
        P.dma("sp", self.o_conv_s[j, :, :, 2, :], xs3, reads=[bs["xs_all"]])

        ybuf = xcb
        by = bxcb
        for g in range(4):
            m0 = g * 5
            for (ma, nb) in ((0, 2), (2, 2), (4, 1)):
                wv, bw = self.wload(self.w_in_c[j, :, (m0 + ma) * 128:(m0 + ma + nb) * 128], NCH, nb * 128)
                for jj in range(nb):
                    ml = ma + jj
                    m = m0 + ml
                    for ti in range(3):
                        t0, n = TT[ti]
                        ps_, bps_ = self.ps()
                        for kc in range(NCH):
                            P.op("pe", lambda e, ps_=ps_, jj=jj, kc=kc, wv=wv, t0=t0, n=n: e.matmul(ps_[:, 0:n], wv[:, kc, jj * 128:(jj + 1) * 128], self.hT[:, kc, t0:t0 + n], start=(kc == 0), stop=(kc == NCH - 1)),
                                 reads=[bw, self.bh[kc][ti]], writes=[bps_])
                        if ti < 2:
                            k2 = cnt[0] % 2
                            cnt[0] += 1
                            ta, bta = tmp[0 + k2], btmp[0 + k2]
                            tb, btb = tmp[2 + k2], btmp[2 + k2]
                            tg, btg = tmp[4], btmp[4]
                            th, bth = tmp[5], btmp[5]
                            P.dma("sp", ta, self.a_d[m, :, t0:t0 + n], reads=[self.ba_d[m][ti]], writes=[bta])
                            P.dma("sp", tb, self.b_d[m, :, t0:t0 + n], reads=[self.bb_d[m][ti]], writes=[btb])
                            P.op("act", lambda e, ps_=ps_, tg=tg: e.activation(tg, ps_[:, 0:512], AF.Gelu_apprx_tanh), reads=[bps_], writes=[btg])
                            P.op("dve", lambda e, ta=ta, tb=tb, th=th, m=m: e.tensor_tensor_scan(th, ta, tb, self.hcur[:, m:m + 1], ALU.mult, ALU.add),
                                 reads=[bta, btb, bs["hcur"]], writes=[bth])
                            P.op("act", lambda e, th=th, m=m: e.copy(self.hcur[:, m:m + 1], th[:, 511:512]), reads=[bth], writes=[bs["hcur"]])
                            P.op("dve", lambda e, tg=tg, th=th, ml=ml, t0=t0, n=n: e.tensor_tensor(ybuf[:, ml, t0:t0 + n], tg, th, ALU.mult), reads=[btg, bth], writes=[by[ml][ti]])
                        else:
                            sg = self.tsm[:, 0:NS]
                            P.op("act", lambda e, ps_=ps_: e.activation(sg, ps_[:, 0:NS], AF.Gelu_apprx_tanh), reads=[bps_], writes=[bs["tsm"]])
                            P.op("dve", lambda e, ml=ml, m=m: e.tensor_tensor(ybuf[:, ml, NPR:T], sg, hs3[:, m, :], ALU.mult), reads=[bs["tsm"], bs["hs_all"]], writes=[by[ml][2]])
            for ocp in range(8):
                wv, bw = self.wload(self.w_out_c[j, m0 * 128:(m0 + 5) * 128, ocp * 256:(ocp + 1) * 256], 5, 256)
                for jj in range(2):
                    oc = ocp * 2 + jj
                    for ti in range(3):
                        t0, n = TT[ti]
                        ps_, bps_ = self.ps()
                        for kc in range(5):
                            P.op("pe", lambda e, ps_=ps_, jj=jj, kc=kc, wv=wv, t0=t0, n=n: e.matmul(ps_[:, 0:n], wv[:, kc, jj * 128:(jj + 1) * 128], ybuf[:, kc, t0:t0 + n], start=(kc == 0), stop=(kc == 4)),
                                 reads=[bw, by[kc][ti]], writes=[bps_])
                        self.resid_add(ps_, bps_, oc, ti, G, bG)
        P.dma("sp", self.o_lru_p[j], self.hcur, reads=[bs["hcur"]])
        self.alias(self.arena_bufs(), mine)

    def even_init(self):
        P = self.P
        nj = 2 if self.cfg.get("layers", DEPTH) > 2 else 1
        self.nj_even = nj
        self.w_in_ab = P.dram_in("w_in_ab", [nj, D, 5120])
        self.w_glu = P.dram_in("s5_w_glu", [nj, 1024, 1024])
        self.w_out_ab = P.dram_in("w_out_ab", [nj, D, D])
        self.s5bc = P.dram_in("s5bc", [nj, 128, 4, 32, 16])
        self.st_s5 = P.dram_in("st_s5", [nj, 128, 2, 32, NS])
        self.st_hg = P.dram_in("st_hg", [nj, 8, 128, NS, 128])
        if not hasattr(self, "pm"):
            self.pmd = P.dram_in("pm", [128, 1])
            self.pm = P.sbuf([128, 1], F32, "pm_sb"); self.bpm = Buf()
            P.dma("sp", self.pm[:], self.pmd, writes=[self.bpm])
        self.o_s5p = P.dram_out("o_s5p", [nj, 128, 2, 32])
        self.o_s5s = P.dram_out("o_s5s", [nj, 128, 2, 32, NS])
        self.o_hgp = P.dram_out("o_hgp", [nj, 8, 128, 128])
        self.o_hgs = P.dram_out("o_hgs", [nj, 8, 128, NS, 128])
        self.s5T = P.dram_tmp("s5T", [nj, 64, 128, 128])
        self.s5R = P.dram_tmp("s5R", [nj, 64, 2, 128, 128])
        self.s5P = P.dram_tmp("s5P", [nj, 32, 128, 2, 128])
        self.bs5m = [Buf() for _ in range(nj)]
        self.Ud = P.dram_tmp("Ud", [1024, 1024]); self.Uds = P.dram_tmp("Uds", [1024, NS])
        self.Yd = P.dram_tmp("Yd", [1024, 1024], BF16); self.Yds = P.dram_tmp("Yds", [1024, NS], BF16)
        self.bUd = [Buf() for _ in range(8)]; self.bYd = [Buf() for _ in range(64)]
        self.agin3 = P.dram_tmp("agin3", [128, 1088]); self.agout3 = P.dram_tmp("agout3", [256, 1088])
        self.bag3 = [Buf(), Buf()]
        self.s5co = [P.sbuf([128, 8, 32], F32, f"s5co{j}") for j in range(nj)]
        self.bs5co = [Buf() for _ in range(nj)]
        self.lbt = P.sbuf([128, 2, 8], F32, "lbt"); self.blbt = Buf()
        self.omlb = P.sbuf([128, 2, 8], F32, "omlb")
        self.ident_bf = P.sbuf([128, 128], BF16, "ident_bf"); self.bidb = Buf()
        self.ones_f = P.sbuf([128, 128], F32, "ones_f"); self.bonesf = Buf()
        P.op("dve", lambda e: e.tensor_copy(self.ident_bf[:], self.fld("ident")), reads=[self.bpk], writes=[self.bidb])
        P.op("dve", lambda e: e.memset(self.ones_f[:], 1.0), writes=[self.bonesf])
        for j in range(nj):
            if self.cfg.get("eprebuild", True):
                self.s5_prebuild(j)
        o, _ = self.lay.fields["hglog"]
        P.op("dve", lambda e: e.memset(self.lbt[:, 0, :], 0.0), writes=[self.blbt])
        P.op("dve", lambda e: e.tensor_tensor(self.lbt[:, 1, :], self.pk[:, o + 8:o + 16], self.pk[:, o:o + 8], ALU.subtract), reads=[self.bpk, self.blbt], writes=[self.blbt])
        P.op("act", lambda e: e.activation(self.lbt[:, 1, :], self.lbt[:, 1, :], AF.Sigmoid), reads=[self.blbt], writes=[self.blbt])
        P.op("dve", lambda e: e.tensor_scalar(self.omlb[:], self.lbt[:], -1.0, 1.0, ALU.mult, ALU.add), reads=[self.blbt], writes=[self.blbt])

    def s5_prebuild(self, j):
        P = self.P
        ar = self.arena
        B = {}
        off = [0]

        def tl(name, n):
            v = ar[:, off[0]:off[0] + n]
            off[0] += n
            B[name] = Buf()
            return v
        step = tl("step", 32); lrd = tl("lrd", 32); th = tl("th", 32)
        Er = tl("Er", 256); Ei = tl("Ei", 256); Fr = tl("Fr", 256); Fi = tl("Fi", 256)
        t1 = tl("t1", 32); t2 = tl("t2", 32); t3 = tl("t3", 32); t4 = tl("t4", 32)
        ti_ = ar[:, off[0]:off[0] + 32].bitcast(I32); off[0] += 32; B["ti"] = Buf()
        zr = tl("zr", 32); zi = tl("zi", 32)
        bc = tl("bc", 2048)
        Bzr = tl("Bzr", 512); Bzi = tl("Bzi", 512)
        Pr = tl("Pr", 1024); NPi = tl("NPi", 1024); Rr = tl("Rr", 1024); Ri = tl("Ri", 1024)
        ta = tl("ta", 1024); tb = tl("tb", 1024)
        stg = [tl(f"stg{i}", 128) for i in range(4)]
        assert off[0] <= 13408, off[0]
        self.alias(list(B.values()), self.arena_bufs())
        o_lr, _ = self.lay.fields["s5lamr"]; o_li, _ = self.lay.fields["s5lami"]; o_ls, _ = self.lay.fields["s5lstep"]
        lamr = self.pk[:, o_lr + j * 32:o_lr + j * 32 + 32]
        lami = self.pk[:, o_li + j * 32:o_li + j * 32 + 32]
        lstep = self.pk[:, o_ls + j * 32:o_ls + j * 32 + 32]
        E3 = lambda v: v.rearrange("p (g m) -> p g m", m=8)
        P.dma("sp", bc.rearrange("p (a g k) -> p a g k", a=4, k=16), self.s5bc[j], writes=[B["bc"]])
        P.op("act", lambda e: e.activation(step, lstep, AF.Exp), reads=[self.bpk], writes=[B["step"]])
        P.op("dve", lambda e: e.tensor_tensor(lrd, lamr, step, ALU.mult), reads=[self.bpk, B["step"]], writes=[B["lrd"]])
        P.op("dve", lambda e: e.tensor_tensor(th, lami, step, ALU.mult), reads=[self.bpk, B["step"]], writes=[B["th"]])
        TWO_PI = 6.283185
        for m in range(1, 9):
            P.op("act", lambda e, m=m: e.activation(t1, lrd, AF.Exp, scale=float(m)), reads=[B["lrd"]], writes=[B["t1"]])
            P.op("act", lambda e, m=m: e.activation(t2, lrd, AF.Exp, scale=-float(m)), reads=[B["lrd"]], writes=[B["t2"]])
            for which in (0, 1):
                sh = 0.0 if which == 0 else 0.25
                P.op("dve", lambda e, m=m, sh=sh: e.tensor_scalar(t3, th, m / (2 * np.pi), sh, ALU.mult, ALU.add), reads=[B["th"]], writes=[B["t3"]])
                P.op("dve", lambda e: e.tensor_copy(ti_, t3), reads=[B["t3"]], writes=[B["ti"]])
                P.op("dve", lambda e: e.tensor_copy(t4, ti_), reads=[B["ti"]], writes=[B["t4"]])
                P.op("dve", lambda e: e.tensor_tensor(t3, t3, t4, ALU.subtract), reads=[B["t3"], B["t4"]], writes=[B["t3"]])
                P.op("dve", lambda e: e.tensor_scalar(t4, t3, 0.5, None, ALU.is_gt), reads=[B["t3"]], writes=[B["t4"]])
                P.op("dve", lambda e: e.tensor_tensor(t3, t3, t4, ALU.subtract), reads=[B["t3"], B["t4"]], writes=[B["t3"]])
                P.op("dve", lambda e: e.tensor_scalar(t4, t3, -0.5, None, ALU.is_lt), reads=[B["t3"]], writes=[B["t4"]])
                P.op("dve", lambda e: e.tensor_tensor(t3, t3, t4, ALU.add), reads=[B["t3"], B["t4"]], writes=[B["t3"]])
                P.op("act", lambda e: e.activation(t3, t3, AF.Sin, scale=TWO_PI), reads=[B["t3"]], writes=[B["t3"]])
                if which == 0:
                    P.op("dve", lambda e, m=m: e.tensor_tensor(E3(Ei)[:, :, m - 1], t1, t3, ALU.mult), reads=[B["t1"], B["t3"]], writes=[B["Ei"]])
                    P.op("dve", lambda e, m=m: e.scalar_tensor_tensor(E3(Fi)[:, :, m - 1], t2, -1.0, t3, ALU.mult, ALU.mult), reads=[B["t2"], B["t3"]], writes=[B["Fi"]])
                else:
                    P.op("dve", lambda e, m=m: e.tensor_tensor(E3(Er)[:, :, m - 1], t1, t3, ALU.mult), reads=[B["t1"], B["t3"]], writes=[B["Er"]])
                    P.op("dve", lambda e, m=m: e.tensor_tensor(E3(Fr)[:, :, m - 1], t2, t3, ALU.mult), reads=[B["t2"], B["t3"]], writes=[B["Fr"]])
        co = self.s5co[j]
        bco = self.bs5co[j]
        for (dst, src, m, neg) in ((0, Er, 8, False), (1, Er, 8, False), (2, Ei, 8, False), (3, Ei, 8, True),
                                   (4, Er, 1, False), (5, Er, 1, False), (6, Ei, 1, False), (7, Ei, 1, True)):
            P.op("dve", lambda e, dst=dst, src=src, m=m, neg=neg: e.tensor_scalar(co[:, dst, :], E3(src)[:, :, m - 1], -1.0 if neg else 1.0, None, ALU.mult),
                 reads=[B["Er"], B["Ei"]], writes=[bco])
        a1r, a1i = E3(Er)[:, :, 0], E3(Ei)[:, :, 0]
        rdE = [B["Er"], B["Ei"], self.bpk]
        P.op("dve", lambda e: e.tensor_tensor(t1, lamr, lamr, ALU.mult), reads=[self.bpk], writes=[B["t1"]])
        P.op("dve", lambda e: e.tensor_tensor(t2, lami, lami, ALU.mult), reads=[self.bpk], writes=[B["t2"]])
        P.op("dve", lambda e: e.tensor_tensor(t1, t1, t2, ALU.add), reads=[B["t1"], B["t2"]], writes=[B["t1"]])
        P.op("dve", lambda e: e.reciprocal(t1, t1), reads=[B["t1"]], writes=[B["t1"]])
        P.op("dve", lambda e: e.tensor_scalar(t2, a1r, -1.0, None, ALU.add), reads=rdE, writes=[B["t2"]])
        P.op("dve", lambda e: e.tensor_tensor(t3, t2, lamr, ALU.mult), reads=[B["t2"], self.bpk], writes=[B["t3"]])
        P.op("dve", lambda e: e.tensor_tensor(t4, a1i, lami, ALU.mult), reads=rdE, writes=[B["t4"]])
        P.op("dve", lambda e: e.tensor_tensor(t3, t3, t4, ALU.add), reads=[B["t3"], B["t4"]], writes=[B["t3"]])
        P.op("dve", lambda e: e.tensor_tensor(zr, t3, t1, ALU.mult), reads=[B["t3"], B["t1"]], writes=[B["zr"]])
        P.op("dve", lambda e: e.tensor_tensor(t3, a1i, lamr, ALU.mult), reads=rdE, writes=[B["t3"]])
        P.op("dve", lambda e: e.tensor_tensor(t4, t2, lami, ALU.mult), reads=[B["t2"], self.bpk], writes=[B["t4"]])
        P.op("dve", lambda e: e.tensor_tensor(t3, t3, t4, ALU.subtract), reads=[B["t3"], B["t4"]], writes=[B["t3"]])
        P.op("dve", lambda e: e.tensor_tensor(zi, t3, t1, ALU.mult), reads=[B["t3"], B["t1"]], writes=[B["zi"]])
        bc4 = bc.rearrange("p (a g k) -> p a g k", a=4, k=16)
        Cr_, Ci_, Br_, Bi_ = bc4[:, 0], bc4[:, 1], bc4[:, 2], bc4[:, 3]
        zb = lambda z: z.unsqueeze(2).broadcast_to([128, 32, 16])
        G3 = lambda v: v.rearrange("p (g k) -> p g k", k=16)
        ta3 = ta[:, 0:512].rearrange("p (g k) -> p g k", k=16)
        P.op("dve", lambda e: e.tensor_tensor(G3(Bzr), Br_, zb(zr), ALU.mult), reads=[B["bc"], B["zr"]], writes=[B["Bzr"]])
        P.op("dve", lambda e: e.tensor_tensor(ta3, Bi_, zb(zi), ALU.mult), reads=[B["bc"], B["zi"]], writes=[B["ta"]])
        P.op("dve", lambda e: e.tensor_tensor(G3(Bzr), G3(Bzr), ta3, ALU.subtract), reads=[B["Bzr"], B["ta"]], writes=[B["Bzr"]])
        P.op("dve", lambda e: e.tensor_tensor(G3(Bzi), Bi_, zb(zr), ALU.mult), reads=[B["bc"], B["zr"]], writes=[B["Bzi"]])
        P.op("dve", lambda e: e.tensor_tensor(ta3, Br_, zb(zi), ALU.mult), reads=[B["bc"], B["zi"]], writes=[B["ta"]])
        P.op("dve", lambda e: e.tensor_tensor(G3(Bzi), G3(Bzi), ta3, ALU.add), reads=[B["Bzi"], B["ta"]], writes=[B["Bzi"]])
        o_m2, _ = self.lay.fields["tmask2"]
        tmask2 = self.pk[:, o_m2:o_m2 + 128]
        ident = self.fld("ident")
        M4 = lambda v: v.rearrange("p (g c k) -> p g c k", c=8, k=16)
        M3 = lambda v: v.rearrange("p (g n) -> p g n", n=128)
        bm = self.bs5m[j]
        sc_ = 0
        for r in range(4):
            g0 = r * 8
            bk = lambda X, g0=g0: X[:, g0:g0 + 8, :].unsqueeze(2).broadcast_to([128, 8, 8, 16])
            bm_ = lambda Tb, g0=g0: E3(Tb)[:, g0:g0 + 8, :].unsqueeze(3).broadcast_to([128, 8, 8, 16])
            rdT = [B["Er"], B["Ei"], B["Fr"], B["Fi"], B["bc"], B["Bzr"], B["Bzi"]]
            ta4, tb4 = M4(ta), M4(tb)
            P.op("dve", lambda e, bk=bk, bm_=bm_: e.tensor_tensor(ta4, bk(Cr_), bm_(Er), ALU.mult), reads=rdT, writes=[B["ta"]])
            P.op("dve", lambda e, bk=bk, bm_=bm_: e.tensor_tensor(tb4, bk(Ci_), bm_(Ei), ALU.mult), reads=rdT, writes=[B["tb"]])
            P.op("dve", lambda e: e.tensor_tensor(M4(Pr), ta4, tb4, ALU.subtract), reads=[B["ta"], B["tb"]], writes=[B["Pr"]])
            P.op("dve", lambda e, bk=bk, bm_=bm_: e.tensor_tensor(ta4, bk(Cr_), bm_(Ei), ALU.mult), reads=rdT, writes=[B["ta"]])
            P.op("dve", lambda e, bk=bk, bm_=bm_: e.tensor_tensor(tb4, bk(Ci_), bm_(Er), ALU.mult), reads=rdT, writes=[B["tb"]])
            P.op("dve", lambda e: e.scalar_tensor_tensor(M4(NPi), ta4, -1.0, tb4, ALU.mult, ALU.subtract), reads=[B["ta"], B["tb"]], writes=[B["NPi"]])
            P.op("dve", lambda e, bk=bk, bm_=bm_: e.tensor_tensor(ta4, bk(G3(Bzr)), bm_(Fr), ALU.mult), reads=rdT, writes=[B["ta"]])
            P.op("dve", lambda e, bk=bk, bm_=bm_: e.tensor_tensor(tb4, bk(G3(Bzi)), bm_(Fi), ALU.mult), reads=rdT, writes=[B["tb"]])
            P.op("dve", lambda e: e.tensor_tensor(M4(Rr), ta4, tb4, ALU.subtract), reads=[B["ta"], B["tb"]], writes=[B["Rr"]])
            P.op("dve", lambda e, bk=bk, bm_=bm_: e.tensor_tensor(ta4, bk(G3(Bzi)), bm_(Fr), ALU.mult), reads=rdT, writes=[B["ta"]])
            P.op("dve", lambda e, bk=bk, bm_=bm_: e.tensor_tensor(tb4, bk(G3(Bzr)), bm_(Fi), ALU.mult), reads=rdT, writes=[B["tb"]])
            P.op("dve", lambda e: e.tensor_tensor(M4(Ri), ta4, tb4, ALU.add), reads=[B["ta"], B["tb"]], writes=[B["Ri"]])
            P.dma("sp", self.s5P[j, g0:g0 + 8, :, 0, :].rearrange("g p n -> p g n"), M3(Pr), reads=[B["Pr"]], writes=[bm])
            P.dma("sp", self.s5P[j, g0:g0 + 8, :, 1, :].rearrange("g p n -> p g n"), M3(NPi), reads=[B["NPi"]], writes=[bm])
            for gl in range(8):
                gp = g0 + gl
                for gh in range(2):
                    g = gh * 32 + gp
                    hs = slice(gh * 64, gh * 64 + 64)
                    pp, bp = self.ps()
                    P.op("pe", lambda e, pp=pp, gl=gl, hs=hs: e.matmul(pp[:, 0:128], M3(Rr)[hs, gl, :], M3(Pr)[hs, gl, :], start=True, stop=False), reads=[B["Rr"], B["Pr"]], writes=[bp])
                    P.op("pe", lambda e, pp=pp, gl=gl, hs=hs: e.matmul(pp[:, 0:128], M3(Ri)[hs, gl, :], M3(NPi)[hs, gl, :], start=False, stop=True), reads=[B["Ri"], B["NPi"]], writes=[bp])
                    st = stg[sc_ % 4]; bst = B[f"stg{sc_ % 4}"]; sc_ += 1
                    P.op("dve", lambda e, pp=pp, st=st: e.tensor_tensor(st, pp[:, 0:128], tmask2, ALU.mult), reads=[bp, self.bpk], writes=[bst])
                    P.dma("sp", self.s5T[j, g], st, reads=[bst], writes=[bm])
                    for ri, Rm, bR in ((0, Rr, B["Rr"]), (1, Ri, B["Ri"])):
                        pp, bp = self.ps()
                        P.op("pe", lambda e, pp=pp, gl=gl, Rm=Rm: e.transpose(pp[:, 0:128], M3(Rm)[:, gl, :], ident), reads=[bR, self.bpk], writes=[bp])
                        st = stg[sc_ % 4]; bst = B[f"stg{sc_ % 4}"]; sc_ += 1
                        P.op("dve", lambda e, pp=pp, st=st, gh=gh: e.tensor_copy(st[:, gh * 64:gh * 64 + 64], pp[:, gh * 64:gh * 64 + 64]), reads=[bp], writes=[bst])
                        P.op("dve", lambda e, st=st, gh=gh: e.memset(st[:, (1 - gh) * 64:(1 - gh) * 64 + 64], 0.0), reads=[bst], writes=[bst])
                        P.dma("sp", self.s5R[j, g, ri], st, reads=[bst], writes=[bm])
        self.alias(self.arena_bufs(), list(B.values()))

    def even_mixer(self, l):
        P = self.P
        j = l // 2
        self._emix_j = j
        ar = self.arena
        G, bG = self.Gmod[1], self.bG[1]
        co = self.s5co[j]; bco = self.bs5co[j]
        bm = self.bs5m[j]
        PAIRS = [[0, 1], [2, 3], [4, 5], [6, 7]]
        B = {}
        off = [0]

        def tl(name, n):
            v = ar[:, off[0]:off[0] + n]
            off[0] += n
            B[name] = Buf()
            return v
        V2 = tl("V2", 8192)
        V4 = V2.rearrange("p (j r g) -> p j r g", r=2, g=32)
        vs = tl("vs", 1024)
        vs4 = vs.rearrange("p (r g b) -> p r g b", r=2, b=NS)
        Rb = [tl(f"Rb{i}", 512) for i in range(2)]
        ub = [tl(f"ub{i}", 288) for i in range(2)]
        sc = tl("sc", 64); tt_ = tl("tt", 64); p1 = tl("p1", 64); p2 = tl("p2", 64)
        ust_off = off[0]
        ust = [tl(f"ust{i}", 512) for i in range(2)]
        assert off[0] <= 12096, off[0]
        stg = ar[:, 12096:12096 + 1088]
        bstg = Buf()
        hgS = stg[:, 64:1088]
        B["hgS"] = bstg
        self.alias(list(B.values()), self.arena_bufs())
        ident = self.fld("ident")

        if self.cfg.get("estop", 99) <= 0:
            self.alias(self.arena_bufs(), list(B.values()))
            return
        self.hgrn_pass1(j, hgS, B["hgS"], B)

        if self.cfg.get("estop", 99) <= 1:
            self.alias(self.arena_bufs(), list(B.values()))
            return
        hperm = lambda kc: self.hT[:, kc, 0:NPR].rearrange("p (j c) -> p c j", c=8)
        P.op("dve", lambda e: e.memset(ub[0], 0.0), writes=[B["ub0"]])
        P.op("dve", lambda e: e.memset(ub[1], 0.0), writes=[B["ub1"]])
        P.dma("sp", vs4, self.st_s5[j], writes=[B["vs"]])
        k_ = 0
        for pr in range(4):
            wv, bw = self.wload(self.w_in_ab[j, :, pr * 256:(pr + 1) * 256], NCH, 256)
            for jj in range(2):
                uc = pr * 2 + jj
                for ti in range(3):
                    pp, bp = self.ps()
                    if ti < 2:
                        for c4 in range(4):
                            for kc in range(NCH):
                                P.op("pe", lambda e, pp=pp, jj=jj, kc=kc, wv=wv, ti=ti, c4=c4: e.matmul(pp[:, c4 * 128:(c4 + 1) * 128], wv[:, kc, jj * 128:(jj + 1) * 128], hperm(kc)[:, 4 * ti + c4, :], start=(kc == 0), stop=(kc == NCH - 1)),
                                     reads=[bw, self.bh[kc][0], self.bh[kc][1]], writes=[bp])
                        st = ust[k_ % 2]; bst = B[f"ust{k_ % 2}"]; k_ += 1
                        P.op("act", lambda e, pp=pp, st=st: e.copy(st, pp[:, 0:512]), reads=[bp], writes=[bst])
                        P.dma("sp", self.Ud[uc * 128:(uc + 1) * 128, ti * 512:(ti + 1) * 512], st, reads=[bst], writes=[self.bUd[uc]])
                    else:
                        for kc in range(NCH):
                            P.op("pe", lambda e, pp=pp, jj=jj, kc=kc, wv=wv: e.matmul(pp[:, 0:NS], wv[:, kc, jj * 128:(jj + 1) * 128], self.hT[:, kc, NPR:T], start=(kc == 0), stop=(kc == NCH - 1)),
                                 reads=[bw, self.bh[kc][2]], writes=[bp])
                        st = ust[k_ % 2]; bst = B[f"ust{k_ % 2}"]; k_ += 1
                        P.op("act", lambda e, pp=pp, st=st: e.copy(st[:, 0:NS], pp[:, 0:NS]), reads=[bp], writes=[bst])
                        P.dma("sp", self.Uds[uc * 128:(uc + 1) * 128, :], st[:, 0:NS], reads=[bst], writes=[self.bUd[uc]])

        if self.cfg.get("estop", 99) <= 2:
            self.alias(self.arena_bufs(), list(B.values()))
            return
        def load_u(gp, k):
            u3 = ub[k].rearrange("p (h n) -> p h n", h=2)
            for gh in range(2):
                g = gh * 32 + gp
                for c8 in range(8):
                    P.dma("sp", u3[c8 * 16:(c8 + 1) * 16, gh, 0:128], self.Ud[g * 16:(g + 1) * 16, c8 * 128:(c8 + 1) * 128], reads=[self.bUd[g // 8]], writes=[B[f"ub{k}"]])
                P.dma("sp", u3[0:16, gh, 128:144], self.Uds[g * 16:(g + 1) * 16, :], reads=[self.bUd[g // 8]], writes=[B[f"ub{k}"]])
            return u3

        for gp in range(32):
            k = gp % 2
            R4 = Rb[k].rearrange("p (h r n) -> p h r n", h=2, r=2)
            for gh in range(2):
                g = gh * 32 + gp
                P.dma("sp", R4[:, gh], self.s5R[j, g].rearrange("r p n -> p r n"), reads=[bm], writes=[B[f"Rb{k}"]])
            u3 = load_u(gp, k)
            for ri in range(2):
                pp, bp = self.ps()
                for gh in range(2):
                    P.op("pe", lambda e, pp=pp, R4=R4, u3=u3, gh=gh, ri=ri: e.matmul(pp[:, 0:144], R4[:, gh, ri, :], u3[:, gh, :], start=(gh == 0), stop=(gh == 1)),
                         reads=[B[f"Rb{k}"], B[f"ub{k}"]], writes=[bp])
                P.op("dve", lambda e, pp=pp, ri=ri, gp=gp: e.tensor_copy(V4[:, :, ri, gp], pp[:, 0:128]), reads=[bp], writes=[B["V2"]])
                P.op("dve", lambda e, pp=pp, ri=ri, gp=gp: e.tensor_tensor(vs4[:, ri, gp, :], pp[:, 128:144], vs4[:, ri, gp, :], ALU.add), reads=[bp, B["vs"]], writes=[B["vs"]])

        if self.cfg.get("estop", 99) <= 3:
            self.alias(self.arena_bufs(), list(B.values()))
            return
        A8 = co[:, 0:2, :]
        a8i, na8i = co[:, 2, :], co[:, 3, :]
        sc3 = sc.rearrange("p (r g) -> p r g", r=2)
        t3 = tt_.rearrange("p (r g) -> p r g", r=2)
        p13 = p1.rearrange("p (r g) -> p r g", r=2)
        p23 = p2.rearrange("p (r g) -> p r g", r=2)

        def cstep(t_ap, A, ai, nai, out3, eng="dve"):
            P.op(eng, lambda e: e.tensor_tensor(p13, A, t_ap, ALU.mult), reads=[B["tt"], B["V2"], B["vs"], bco], writes=[B["p1"]])
            P.op(eng, lambda e: e.tensor_tensor(p23[:, 0, :], t_ap[:, 1, :], nai, ALU.mult), reads=[B["tt"], B["V2"], B["vs"], bco], writes=[B["p2"]])
            P.op(eng, lambda e: e.tensor_tensor(p23[:, 1, :], t_ap[:, 0, :], ai, ALU.mult), reads=[B["tt"], B["V2"], B["vs"], bco], writes=[B["p2"]])
            P.op(eng, lambda e: e.tensor_tensor(out3, p13, p23, ALU.add), reads=[B["p1"], B["p2"]], writes=[B["sc"]])

        P.op("dve", lambda e: e.memset(sc, 0.0), writes=[B["sc"]])
        for jj in range(128):
            P.op("dve", lambda e, jj=jj: e.tensor_tensor(t3, sc3, V4[:, jj], ALU.add), reads=[B["sc"], B["V2"]], writes=[B["tt"]])
            cstep(t3, A8, a8i, na8i, sc3)
        P.op("dve", lambda e: e.tensor_copy(stg[:, 0:64], sc), reads=[B["sc"]], writes=[bstg])
        P.dma("sp", self.agin3, stg, reads=[bstg], writes=[self.bag3[0]])
        P.coll("AllGather", ALU.bypass, [self.agin3], [self.agout3], PAIRS, reads=[self.bag3[0]], writes=[self.bag3[1]])
        P.dma("sp", stg, self.agout3[0:128, :], reads=[self.bag3[1]], writes=[bstg])
        P.op("dve", lambda e: e.tensor_scalar(stg, stg, self.pm[:, 0:1], None, ALU.mult), reads=[bstg, self.bpm], writes=[bstg])
        hn4 = ar[:, ust_off:ust_off + 1024].rearrange("p (r g b) -> p r g b", r=2, b=NS)
        tq = Rb[0].rearrange("p (g b) -> p g b", b=NS)
        bcs = lambda v: v.unsqueeze(2).broadcast_to([128, 32, NS])
        A1b = co[:, 4:6, :].unsqueeze(3).broadcast_to([128, 2, 32, NS])
        bhn = [B["ust0"], B["ust1"]]
        P.op("dve", lambda e: e.tensor_tensor(hn4, vs4, A1b, ALU.mult), reads=[B["vs"], bco], writes=bhn)
        P.op("dve", lambda e: e.tensor_tensor(tq, vs4[:, 1], bcs(co[:, 7, :]), ALU.mult), reads=[B["vs"], bco], writes=[B["Rb0"]])
        P.op("dve", lambda e: e.tensor_tensor(hn4[:, 0], hn4[:, 0], tq, ALU.add), reads=bhn + [B["Rb0"]], writes=bhn)
        P.op("dve", lambda e: e.tensor_tensor(tq, vs4[:, 0], bcs(co[:, 6, :]), ALU.mult), reads=[B["vs"], bco], writes=[B["Rb0"]])
        P.op("dve", lambda e: e.tensor_tensor(hn4[:, 1], hn4[:, 1], tq, ALU.add), reads=bhn + [B["Rb0"]], writes=bhn)
        P.dma("sp", self.o_s5s[j], hn4, reads=bhn)
        P.op("dve", lambda e: e.tensor_copy(sc, stg[:, 0:64]), reads=[bstg], writes=[B["sc"]])
        for jj in range(128):
            P.op("dve", lambda e, jj=jj: e.tensor_tensor(V4[:, jj], sc3, V4[:, jj], ALU.add), reads=[B["sc"], B["V2"]], writes=[B["V2"]])
            cstep(V4[:, jj], A8, a8i, na8i, sc3)
        P.dma("sp", self.o_s5p[j], sc3, reads=[B["sc"]])
        self.hg_init = stg[:, 64:1088]
        self.bhg_init = bstg

        if self.cfg.get("estop", 99) <= 4:
            self.alias(self.arena_bufs(), list(B.values()))
            return
        o_d2, _ = self.lay.fields["s5d2"]
        Tb = Rb
        for gp in range(32):
            k = gp % 2
            T4 = Tb[k].rearrange("p (h r n) -> p h r n", h=2, r=2)
            for gh in range(2):
                g = gh * 32 + gp
                P.dma("sp", T4[:, gh, 0, :], self.s5T[j, g], reads=[bm], writes=[B[f"Rb{k}"]])
            P.dma("sp", T4[:, :, 1, :], self.s5P[j, gp].rearrange("p r n -> p r n"), reads=[bm], writes=[B[f"Rb{k}"]])
            u3 = load_u(gp, k)
            for gh in range(2):
                g = gh * 32 + gp
                hs = slice(gh * 64, gh * 64 + 64)
                pp, bp = self.ps()
                rd = [B[f"Rb{k}"], B[f"ub{k}"], B["V2"], B["vs"]]
                P.op("pe", lambda e, pp=pp, T4=T4, u3=u3, gh=gh: e.matmul(pp[:, 0:128], T4[:, gh, 0, :], u3[:, gh, 0:128], start=True, stop=False), reads=rd, writes=[bp])
                P.op("pe", lambda e, pp=pp, T4=T4, hs=hs, gp=gp: e.matmul(pp[:, 0:128], T4[hs, 0, 1, :], V4[hs, :, 0, gp], start=False, stop=False), reads=rd, writes=[bp])
                P.op("pe", lambda e, pp=pp, T4=T4, hs=hs, gp=gp: e.matmul(pp[:, 0:128], T4[hs, 1, 1, :], V4[hs, :, 1, gp], start=False, stop=True), reads=rd, writes=[bp])
                P.op("pe", lambda e, pp=pp, T4=T4, u3=u3, gh=gh: e.matmul(pp[:, 128:144], T4[:, gh, 0, :], u3[:, gh, 128:144], start=True, stop=False), reads=rd, writes=[bp])
                P.op("pe", lambda e, pp=pp, T4=T4, hs=hs, gp=gp: e.matmul(pp[:, 128:144], T4[hs, 0, 1, :], vs4[hs, 0, gp, :], start=False, stop=False), reads=rd, writes=[bp])
                P.op("pe", lambda e, pp=pp, T4=T4, hs=hs, gp=gp: e.matmul(pp[:, 128:144], T4[hs, 1, 1, :], vs4[hs, 1, gp, :], start=False, stop=True), reads=rd, writes=[bp])
                st = ust[g % 2]; bst = B[f"ust{g % 2}"]
                P.op("dve", lambda e, pp=pp, st=st, u3=u3, gh=gh, g=g: e.scalar_tensor_tensor(st[:, 0:144], u3[:, gh, :], self.pk[:, o_d2 + j * 64 + g:o_d2 + j * 64 + g + 1], pp[:, 0:144], ALU.mult, ALU.add),
                     reads=[bp, B[f"ub{k}"], self.bpk], writes=[bst])
                yb16 = st[:, 256:256 + 72].bitcast(BF16)
                P.op("act", lambda e, st=st, yb16=yb16: e.activation(yb16, st[:, 0:144], AF.Gelu_apprx_tanh), reads=[bst], writes=[bst])
                for c8 in range(8):
                    P.dma("sp", self.Yd[g * 16:(g + 1) * 16, c8 * 128:(c8 + 1) * 128], yb16[c8 * 16:(c8 + 1) * 16, 0:128], reads=[bst], writes=[self.bYd[g]])
                P.dma("sp", self.Yds[g * 16:(g + 1) * 16, :], yb16[0:16, 128:144], reads=[bst], writes=[self.bYd[g]])

        if self.cfg.get("estop", 99) <= 5:
            self.alias(self.arena_bufs(), list(B.values()))
            return
        yp = ar[:, 0:4160].bitcast(BF16).rearrange("p (a t) -> p a t", t=T)
        byp = [Buf() for _ in range(8)]
        yg = ar[:, 4160:4160 + 2080].bitcast(BF16).rearrange("p (a t) -> p a t", t=T)
        byg = [[Buf() for _ in range(3)] for _ in range(4)]
        sgm = [ar[:, 6240 + i * 512:6240 + (i + 1) * 512] for i in range(2)]
        bsgm = [Buf(), Buf()]
        self.alias(byp + [b for r in byg for b in r] + bsgm, list(B.values()))
        for uc in range(8):
            P.dma("sp", yp[:, uc, 0:NPR], self.Yd[uc * 128:(uc + 1) * 128, :], reads=self.bYd[uc * 8:(uc + 1) * 8], writes=[byp[uc]])
            P.dma("sp", yp[:, uc, NPR:T], self.Yds[uc * 128:(uc + 1) * 128, :], reads=self.bYd[uc * 8:(uc + 1) * 8], writes=[byp[uc]])
        o_bg, _ = self.lay.fields["bglu"]
        ypp = lambda c, ti: yp[:, c, ti * 512:(ti + 1) * 512]
        for grp in range(2):
            for pr in range(2):
                wv, bw = self.wload(self.w_glu[j, :, (grp * 2 + pr) * 256:(grp * 2 + pr + 1) * 256], 8, 256)
                for jj in range(2):
                    oc = grp * 4 + pr * 2 + jj
                    ol = pr * 2 + jj
                    for ti in range(3):
                        t0, n = TT[ti]
                        pp, bp = self.ps()
                        for kc in range(8):
                            P.op("pe", lambda e, pp=pp, jj=jj, kc=kc, wv=wv, t0=t0, n=n: e.matmul(pp[:, 0:n], wv[:, kc, jj * 128:(jj + 1) * 128], yp[:, kc, t0:t0 + n], start=(kc == 0), stop=(kc == 7)),
                                 reads=[bw, byp[kc]], writes=[bp])
                        sg = sgm[ti % 2]; bsg = bsgm[ti % 2]
                        P.op("act", lambda e, pp=pp, sg=sg, n=n, oc=oc: e.activation(sg[:, 0:n], pp[:, 0:n], AF.Sigmoid, bias=self.pk[:, o_bg + j * 8 + oc:o_bg + j * 8 + oc + 1]), reads=[bp, self.bpk], writes=[bsg])
                        if ti < 2:
                            dst = yg[:, ol, 0:NPR].rearrange("p (j c) -> p c j", c=8)[:, 4 * ti:4 * ti + 4, :]
                            P.op("dve", lambda e, dst=dst, sg=sg, oc=oc, ti=ti: e.tensor_tensor(dst, ypp(oc, ti).rearrange("p (c j) -> p c j", c=4), sg.rearrange("p (c j) -> p c j", c=4), ALU.mult),
                                 reads=[bsg, byp[oc]], writes=[byg[ol][0], byg[ol][1]])
                        else:
                            P.op("dve", lambda e, sg=sg, oc=oc, ol=ol: e.tensor_tensor(yg[:, ol, NPR:T], yp[:, oc, NPR:T], sg[:, 0:NS], ALU.mult), reads=[bsg, byp[oc]], writes=[byg[ol][2]])
            self.outproj_group(self.w_out_ab[j], grp * 4, 4, yg, byg, G, bG)
        self.alias(list(B.values()), byp + [b for r in byg for b in r] + bsgm)
        if self.cfg.get("hgrn", True):
            self.hgrn_pass2(j, B)
        self.alias(self.arena_bufs(), list(B.values()))

    def outproj_group(self, w, k0, nk, yb, byb, G, bG):
        P = self.P
        for ocp in range(8):
            wv, bw = self.wload(w[k0 * 128:(k0 + nk) * 128, ocp * 256:(ocp + 1) * 256], nk, 256)
            for jj in range(2):
                oc = ocp * 2 + jj
                for ti in range(3):
                    t0, n = TT[ti]
                    ps_, bps_ = self.ps()
                    for kc in range(nk):
                        P.op("pe", lambda e, ps_=ps_, jj=jj, kc=kc, wv=wv, t0=t0, n=n: e.matmul(ps_[:, 0:n], wv[:, kc, jj * 128:(jj + 1) * 128], yb[:, kc, t0:t0 + n], start=(kc == 0), stop=(kc == nk - 1)),
                             reads=[bw, byb[kc][ti]], writes=[bps_])
                    self.resid_add(ps_, bps_, oc, ti, G, bG)

    def hg_fpath(self, j, hd, pf, bpf, n, fs, bfs, kf_dst, bkf, lg_dst, blg):
        P = self.P
        lb = self.lbt[:, j, hd:hd + 1]
        oml = self.omlb[:, j, hd:hd + 1]
        P.op("act", lambda e: e.activation(fs[:, 0:n], pf[:, 0:n], AF.Sigmoid), reads=[bpf], writes=[bfs])
        P.op("dve", lambda e: e.tensor_scalar(fs[:, 0:n], fs[:, 0:n], oml, lb, ALU.mult, ALU.add), reads=[bfs, self.blbt], writes=[bfs])
        P.op("dve", lambda e: e.tensor_scalar(kf_dst, fs[:, 0:n], -1.0, 1.0, ALU.mult, ALU.add), reads=[bfs], writes=[bkf])
        P.op("act", lambda e: e.activation(lg_dst, fs[:, 0:n], AF.Ln), reads=[bfs], writes=[blg])

    def hg_vtok(self, j, hd, wv, bw, col0, vt, bvt):
        P = self.P
        for q4 in range(4):
            pp, bp = self.ps()
            for cc in range(4):
                jc = q4 * 4 + cc
                for kc in range(NCH):
                    P.op("pe", lambda e, pp=pp, cc=cc, jc=jc, kc=kc: e.matmul(pp[0:64, cc * 128:(cc + 1) * 128], self.hT[:, kc, jc * 64:(jc + 1) * 64], wv[:, kc, col0:col0 + 128], start=(kc == 0), stop=(kc == NCH - 1)),
                         reads=[bw, self.bh[kc][jc // 8]], writes=[bp])
            P.op("act", lambda e, pp=pp, q4=q4: e.copy(vt.rearrange("p c v -> p (c v)")[0:64, q4 * 512:(q4 + 1) * 512], pp[0:64, 0:512]), reads=[bp], writes=[bvt])

    def hgrn_pass1(self, j, hgS, bhgS, B):
        P = self.P
        ar = self.arena
        X = {}
        off = [0]

        def tl(name, n):
            v = ar[:, off[0]:off[0] + n]
            off[0] += n
            X[name] = Buf()
            return v
        lg = tl("lg", 1024); kf = tl("kf", 1024); gg = tl("gg", 1024); fs = tl("fs", 512); onesf = tl("ones", 512)
        kE = tl("kE", 512).bitcast(BF16)
        vt = tl("vt", 1024).bitcast(BF16).rearrange("p (c v) -> p c v", v=128)
        kt = tl("kt", 1024).bitcast(BF16).rearrange("p (c v) -> p c v", v=128)
        gend = tl("gend", 8)
        assert off[0] <= 8192
        self.alias(list(X.values()), [B["V2"]])
        P.op("dve", lambda e: e.memset(onesf, 1.0), writes=[X["ones"]])
        for hd in range(8):
            if hd % 2 == 0:
                wf, bwf = self.wload(self.w_in_ab[j, :, 2048 + hd * 128:2048 + (hd + 2) * 128], NCH, 256)
                wi, bwi = self.wload(self.w_in_ab[j, :, 3072 + hd * 128:3072 + (hd + 2) * 128], NCH, 256)
            c0 = (hd % 2) * 128
            for ti in range(2):
                t0, n = TT[ti]
                pf, bpf = self.ps()
                for kc in range(NCH):
                    P.op("pe", lambda e, pf=pf, kc=kc, wf=wf, c0=c0, t0=t0, n=n: e.matmul(pf[:, 0:n], wf[:, kc, c0:c0 + 128], self.hT[:, kc, t0:t0 + n], start=(kc == 0), stop=(kc == NCH - 1)),
                         reads=[bwf, self.bh[kc][ti]], writes=[bpf])
                self.hg_fpath(j, hd, pf, bpf, n, fs, X["fs"], kf[:, t0:t0 + n], X["kf"], lg[:, t0:t0 + n], X["lg"])
            for ti in range(2):
                t0, n = TT[ti]
                init = 0.0 if ti == 0 else lg[:, t0 - 1:t0]
                init = 0.0 if ti == 0 else gg[:, t0 - 1:t0]
                P.op("dve", lambda e, t0=t0, n=n, init=init: e.tensor_tensor_scan(gg[:, t0:t0 + n], onesf[:, 0:n], lg[:, t0:t0 + n], init, ALU.mult, ALU.add),
                     reads=[X["lg"], X["ones"], X["gg"]], writes=[X["gg"]])
            P.op("dve", lambda e: e.tensor_scalar(lg, gg, -1.0, gg[:, NPR - 1:NPR], ALU.mult, ALU.add), reads=[X["gg"]], writes=[X["lg"]])
            P.op("act", lambda e: e.activation(lg, lg, AF.Exp), reads=[X["lg"]], writes=[X["lg"]])
            P.op("dve", lambda e: e.tensor_tensor(kE, kf, lg, ALU.mult), reads=[X["lg"], X["kf"]], writes=[X["kE"]])
            self.hg_vtok(j, hd, wi, bwi, c0, vt, X["vt"])
            for q4 in range(4):
                pp, bp = self.ps()
                ppb = pp[:].bitcast(BF16)
                for cc in range(4):
                    jc = q4 * 4 + cc
                    P.op("pe", lambda e, ppb=ppb, cc=cc, jc=jc: e.transpose(ppb[0:64, cc * 128:(cc + 1) * 128], kE[:, jc * 64:(jc + 1) * 64], self.ident_bf[:]),
                         reads=[X["kE"], self.bidb], writes=[bp])
                P.op("act", lambda e, ppb=ppb, q4=q4: e.copy(kt.rearrange("p c v -> p (c v)")[0:64, q4 * 512:(q4 + 1) * 512], ppb[0:64, 0:512]), reads=[bp], writes=[X["kt"]])
            pp, bp = self.ps()
            for jc in range(16):
                P.op("pe", lambda e, pp=pp, jc=jc: e.matmul(pp[:, 0:128], kt[0:64, jc, :], vt[0:64, jc, :], start=(jc == 0), stop=(jc == 15)), reads=[X["kt"], X["vt"]], writes=[bp])
            P.op("act", lambda e, pp=pp, hd=hd: e.copy(hgS[:, hd * 128:(hd + 1) * 128], pp[:, 0:128]), reads=[bp], writes=[bhgS])
        self.alias([B["V2"]], list(X.values()))

    def hgrn_pass2(self, j, B):
        P = self.P
        ar = self.arena
        G, bG = self.Gmod[1], self.bG[1]
        X = {}
        off = [0]

        def tl(name, n):
            v = ar[:, off[0]:off[0] + n]
            off[0] += n
            X[name] = Buf()
            return v
        yg = tl("yg", 2080).bitcast(BF16).rearrange("p (a t) -> p a t", t=T)
        byg = [[Buf() for _ in range(3)] for _ in range(4)]
        lk = tl("lk", 2048); lg = lk[:, 0:1024]; kf = lk[:, 1024:2048]; X["lg"] = X["lk"]; X["kf"] = X["lk"]
        qo = tl("qo", 1040); qf = qo[:, 0:1024]; o = qo; X["qf"] = X["qo"]; X["o"] = X["qo"]
        fs = tl("fs", 512); onesf = fs; X["ones"] = X["fs"]
        dif = tl("dif", 1024); tmp = dif[:, 0:512]; X["tmp"] = X["dif"]
        ex = lg; X["ex"] = X["lk"]
        qg = tl("qg", 512).bitcast(BF16); kg = tl("kg", 512).bitcast(BF16); qq = tl("qq", 512).bitcast(BF16)
        kk4 = [tl(f"kk4_{I}", 128 * (I + 1)).bitcast(BF16).rearrange("p (c s) -> p c s", c=16) for I in range(4)]
        kkz = tl("kkz", 128).bitcast(BF16).rearrange("p (i s) -> p i s", i=4)
        sg = tl("sg", 520).bitcast(BF16)
        vt = tl("vt", 1024).bitcast(BF16).rearrange("p (c v) -> p c v", v=128)
        S = tl("S", 128); Sb = tl("Sb", 64).bitcast(BF16)
        scb = tl("scb", 32).bitcast(BF16); kkt = tl("kkt", 64).bitcast(BF16)
        dd = tl("dd", 64)
        sm = tl("smp", 256)
        sm2 = tl("sm2", 256)
        S0 = lk; X["S0"] = X["lk"]
        assert off[0] <= 12096, off[0]
        mine = list(X.values()) + [b for r in byg for b in r]
        self.alias(mine, list(B.values()))
        o_nw, _ = self.lay.fields["hgnw"]
        o_cm, _ = self.lay.fields["cmask64"]
        cmask = self.pk[0:64, o_cm:o_cm + 64]
        d1, d2, d3 = dd[:, 0:16], dd[:, 16:32], dd[:, 32:48]
        S03 = S0.rearrange("p (b v) -> p b v", v=128)
        qs, fsm, ks, qfs, qk, vsT, osm, rs = [sm[:, i * NS:(i + 1) * NS] for i in range(8)]
        kst = sm[0:NS, 128:256]
        vst = sm2[0:NS, 0:128]
        ksel = sm2[0:NS, 128:256]
        for hd in range(8):
            hl = hd % 4
            wq, bwq = self.wload(self.w_in_ab[j, :, 1024 + hd * 128:1024 + (hd + 1) * 128], NCH, 128)
            wf, bwf = self.wload(self.w_in_ab[j, :, 2048 + hd * 128:2048 + (hd + 1) * 128], NCH, 128)
            c0 = 0
            P.op("dve", lambda e, hd=hd: e.tensor_copy(S, self.hg_init[:, hd * 128:(hd + 1) * 128]), reads=[self.bhg_init], writes=[X["S"]])
            for ti in range(3):
                t0, n = TT[ti]
                pq, bpq = self.ps()
                pf, bpf = self.ps()
                for (pt, bpt, w_, bw_) in ((pq, bpq, wq, bwq), (pf, bpf, wf, bwf)):
                    for kc in range(NCH):
                        P.op("pe", lambda e, pt=pt, kc=kc, w_=w_, c0=c0, t0=t0, n=n: e.matmul(pt[:, 0:n], w_[:, kc, c0:c0 + 128], self.hT[:, kc, t0:t0 + n], start=(kc == 0), stop=(kc == NCH - 1)),
                             reads=[bw_, self.bh[kc][ti]], writes=[bpt])
                if ti < 2:
                    P.op("act", lambda e, pq=pq, t0=t0, n=n: e.copy(qf[:, t0:t0 + n], pq[:, 0:n]), reads=[bpq], writes=[X["qf"]])
                    self.hg_fpath(j, hd, pf, bpf, n, fs, X["fs"], kf[:, t0:t0 + n], X["kf"], lg[:, t0:t0 + n], X["lg"])
                else:
                    P.op("act", lambda e, pq=pq: e.copy(qs, pq[:, 0:NS]), reads=[bpq], writes=[X["smp"]])
                    lb = self.lbt[:, j, hd:hd + 1]; oml = self.omlb[:, j, hd:hd + 1]
                    P.op("act", lambda e, pf=pf: e.activation(fsm, pf[:, 0:NS], AF.Sigmoid), reads=[bpf], writes=[X["smp"]])
                    P.op("dve", lambda e, lb=lb, oml=oml: e.tensor_scalar(fsm, fsm, oml, lb, ALU.mult, ALU.add), reads=[X["smp"], self.blbt], writes=[X["smp"]])
                    P.op("dve", lambda e: e.tensor_scalar(ks, fsm, -1.0, 1.0, ALU.mult, ALU.add), reads=[X["smp"]], writes=[X["smp"]])
                    P.op("dve", lambda e: e.tensor_tensor(qfs, qs, fsm, ALU.mult), reads=[X["smp"]], writes=[X["smp"]])
                    P.op("dve", lambda e: e.tensor_tensor(qk, qs, ks, ALU.mult), reads=[X["smp"]], writes=[X["smp"]])
            wg_, bwg_ = self.wload(self.w_in_ab[j, :, 4096 + hd * 128:4096 + (hd + 1) * 128], NCH, 128)
            for ti in range(3):
                t0, n = TT[ti]
                pg, bpg = self.ps()
                for kc in range(NCH):
                    P.op("pe", lambda e, pg=pg, kc=kc, c0=c0, t0=t0, n=n, wg_=wg_: e.matmul(pg[:, 0:n], wg_[:, kc, c0:c0 + 128], self.hT[:, kc, t0:t0 + n], start=(kc == 0), stop=(kc == NCH - 1)),
                         reads=[bwg_, self.bh[kc][ti]], writes=[bpg])
                P.op("act", lambda e, pg=pg, t0=t0, n=n: e.activation(sg[:, t0:t0 + n], pg[:, 0:n], AF.Silu), reads=[bpg], writes=[X["sg"]])
            wi, bwi = self.wload(self.w_in_ab[j, :, 3072 + hd * 128:3072 + (hd + 1) * 128], NCH, 128)
            self.hg_vtok(j, hd, wi, bwi, c0, vt, X["vt"])
            pv, bpv = self.ps()
            for kc in range(NCH):
                P.op("pe", lambda e, pv=pv, kc=kc, c0=c0, wi=wi: e.matmul(pv[0:NS, 0:128], self.hT[:, kc, NPR:T], wi[:, kc, c0:c0 + 128], start=(kc == 0), stop=(kc == NCH - 1)),
                     reads=[bwi, self.bh[kc][2]], writes=[bpv])
            for kc in range(NCH):
                P.op("pe", lambda e, pv=pv, kc=kc, c0=c0, wi=wi: e.matmul(pv[:, 128:128 + NS], wi[:, kc, c0:c0 + 128], self.hT[:, kc, NPR:T], start=(kc == 0), stop=(kc == NCH - 1)),
                     reads=[bwi, self.bh[kc][2]], writes=[bpv])
            P.op("act", lambda e, pv=pv: e.copy(vst, pv[0:NS, 0:128]), reads=[bpv], writes=[X["sm2"]])
            P.op("act", lambda e, pv=pv: e.copy(vsT, pv[:, 128:128 + NS]), reads=[bpv], writes=[X["smp"]])
            P.op("dve", lambda e: e.memset(onesf, 1.0), reads=[X["fs"]], writes=[X["fs"]])
            for ti in range(2):
                t0, n = TT[ti]
                init = 0.0 if ti == 0 else dif[:, t0 - 1:t0]
                P.op("dve", lambda e, t0=t0, n=n, init=init: e.tensor_tensor_scan(dif[:, t0:t0 + n], onesf[:, 0:n], lg[:, t0:t0 + n], init, ALU.mult, ALU.add),
                     reads=[X["lg"], X["ones"], X["dif"]], writes=[X["dif"]])
            G3 = dif.rearrange("p (c t) -> p c t", t=64)
            ex3 = ex.rearrange("p (c t) -> p c t", t=64)
            q3 = qf.rearrange("p (c t) -> p c t", t=64)
            k3 = kf.rearrange("p (c t) -> p c t", t=64)
            glast = G3[:, :, 63]
            P.op("dve", lambda e: e.tensor_copy(d3[:, 0:1], glast[:, 0:1]), reads=[X["dif"]], writes=[X["dd"]])
            P.op("dve", lambda e: e.tensor_tensor(d3[:, 1:16], glast[:, 1:16], glast[:, 0:15], ALU.subtract), reads=[X["dif"]], writes=[X["dd"]])
            P.op("act", lambda e: e.activation(d3, d3, AF.Exp), reads=[X["dd"]], writes=[X["dd"]])
            P.op("dve", lambda e: e.memset(d1[:, 0:1], 0.0), reads=[X["dd"]], writes=[X["dd"]])
            P.op("dve", lambda e: e.tensor_copy(d1[:, 1:16], glast[:, 0:15]), reads=[X["dif"]], writes=[X["dd"]])
            bc64 = lambda v: v.unsqueeze(2).broadcast_to([128, 16, 64])
            P.op("dve", lambda e: e.tensor_tensor(ex3, G3, bc64(d1), ALU.subtract), reads=[X["dif"], X["dd"]], writes=[X["ex"]])
            P.op("act", lambda e: e.activation(ex, ex, AF.Exp), reads=[X["ex"]], writes=[X["ex"]])
            P.op("dve", lambda e: e.tensor_tensor(qg, qf, ex, ALU.mult), reads=[X["qf"], X["ex"]], writes=[X["qg"]])
            P.op("dve", lambda e: e.tensor_tensor(ex3, G3, bc64(glast), ALU.subtract), reads=[X["dif"]], writes=[X["ex"]])
            P.op("act", lambda e: e.activation(ex, ex, AF.Exp, scale=-1.0), reads=[X["ex"]], writes=[X["ex"]])
            P.op("dve", lambda e: e.tensor_tensor(kg, kf, ex, ALU.mult), reads=[X["kf"], X["ex"]], writes=[X["kg"]])
            qq3 = qq.rearrange("p (c t) -> p c t", t=64)
            for I in range(4):
                w_ = 16 * (I + 1)
                gm = G3[:, :, 16 * I + 7]
                bcw = gm.unsqueeze(2).broadcast_to([128, 16, w_])
                exI = ex[:, 0:16 * w_].rearrange("p (c s) -> p c s", c=16)
                P.op("dve", lambda e, exI=exI, bcw=bcw, w_=w_: e.tensor_tensor(exI, G3[:, :, 0:w_], bcw, ALU.subtract), reads=[X["dif"]], writes=[X["ex"]])
                P.op("dve", lambda e, exI=exI: e.tensor_scalar(exI, exI, -80.0, 80.0, ALU.max, ALU.min), reads=[X["ex"]], writes=[X["ex"]])
                fsI = fs[:, 0:256].rearrange("p (c s) -> p c s", c=16)
                P.op("act", lambda e, exI=exI, fsI=fsI, I=I: e.activation(fsI, exI[:, :, 16 * I:16 * I + 16], AF.Exp), reads=[X["ex"], X["fs"]], writes=[X["fs"]])
                P.op("dve", lambda e, fsI=fsI, I=I: e.tensor_tensor(qq3[:, :, 16 * I:16 * I + 16], q3[:, :, 16 * I:16 * I + 16], fsI, ALU.mult), reads=[X["qf"], X["fs"]], writes=[X["qq"]])
                P.op("act", lambda e, exI=exI: e.activation(exI, exI, AF.Exp, scale=-1.0), reads=[X["ex"]], writes=[X["ex"]])
                P.op("dve", lambda e, exI=exI, I=I, w_=w_: e.tensor_tensor(kk4[I], k3[:, :, 0:w_], exI, ALU.mult), reads=[X["kf"], X["ex"]], writes=[X[f"kk4_{I}"]])
            P.op("dve", lambda e: e.memset(kkz, 0.0), reads=[X["kkz"]], writes=[X["kkz"]])
            for jc in range(16):
                cs = slice(jc * 64, (jc + 1) * 64)
                for I in range(4):
                    P.op("dve", lambda e, I=I, jc=jc: e.tensor_copy(kkz[:, I, 0:16 * (I + 1)], kk4[I][:, jc, :]), reads=[X[f"kk4_{I}"], X["kkz"]], writes=[X["kkz"]])
                ps1, bps1 = self.ps()
                for I in range(4):
                    P.op("pe", lambda e, ps1=ps1, I=I, jc=jc: e.matmul(ps1[0:64, 16 * I:16 * I + 16], kkz[:, I, :], qq[:, jc * 64 + 16 * I:jc * 64 + 16 * I + 16], start=True, stop=True), reads=[X["kkz"], X["qq"]], writes=[bps1])
                P.op("dve", lambda e, ps1=ps1: e.tensor_tensor(scb[0:64, 0:64], ps1[0:64, 0:64], cmask, ALU.mult), reads=[bps1, self.bpk], writes=[X["scb"]])
                ps2, bps2 = self.ps()
                ps2b = ps2[:].bitcast(BF16)
                P.op("pe", lambda e, ps2b=ps2b, cs=cs: e.transpose(ps2b[0:64, 0:128], kg[:, cs], self.ident_bf[:]), reads=[X["kg"], self.bidb], writes=[bps2])
                P.op("act", lambda e, ps2b=ps2b: e.copy(kkt[0:64, 0:128], ps2b[0:64, 0:128]), reads=[bps2], writes=[X["kkt"]])
                P.op("act", lambda e: e.copy(Sb[:, 0:128], S), reads=[X["S"]], writes=[X["Sb"]])
                ps3, bps3 = self.ps()
                P.op("pe", lambda e, ps3=ps3, jc=jc: e.matmul(ps3[:, 0:64], vt[0:64, jc, :], scb[0:64, 0:64], start=True, stop=False), reads=[X["vt"], X["scb"]], writes=[bps3])
                P.op("pe", lambda e, ps3=ps3, cs=cs: e.matmul(ps3[:, 0:64], Sb[:, 0:128], qg[:, cs], start=False, stop=True), reads=[X["Sb"], X["qg"]], writes=[bps3])
                P.op("act", lambda e, ps3=ps3, cs=cs: e.copy(o[:, cs], ps3[:, 0:64]), reads=[bps3, X["qq"], X["qg"]], writes=[X["o"]])
                ps4, bps4 = self.ps()
                P.op("pe", lambda e, ps4=ps4, jc=jc: e.matmul(ps4[:, 0:128], kkt[0:64, 0:128], vt[0:64, jc, :], start=True, stop=True), reads=[X["kkt"], X["vt"]], writes=[bps4])
                P.op("dve", lambda e, ps4=ps4, jc=jc: e.scalar_tensor_tensor(S, S, d3[:, jc:jc + 1], ps4[:, 0:128], ALU.mult, ALU.add), reads=[bps4, X["S"], X["dd"]], writes=[X["S"]])
            P.dma("sp", self.o_hgp[j, hd], S, reads=[X["S"]])
            P.dma("sp", S03, self.st_hg[j, hd], writes=[X["S0"]])
            pqk, bpqk = self.ps()
            P.op("pe", lambda e, pqk=pqk: e.matmul(pqk[:, 0:NS], self.ones_f[:], qk, start=True, stop=True), reads=[X["smp"], self.bonesf], writes=[bpqk])
            P.op("dve", lambda e, pqk=pqk: e.tensor_tensor(osm, vsT, pqk[:, 0:NS], ALU.mult), reads=[bpqk, X["smp"]], writes=[X["smp"]])
            po, bpo = self.ps()
            for b in range(NS):
                P.op("pe", lambda e, po=po, b=b: e.matmul(po[:, b:b + 1], S03[:, b, :], qfs[:, b:b + 1], start=True, stop=True), reads=[X["S0"], X["smp"]], writes=[bpo])
            P.op("dve", lambda e, po=po: e.tensor_tensor(o[:, NPR:T], osm, po[:, 0:NS], ALU.add), reads=[bpo, X["smp"]], writes=[X["o"]])
            pk_, bpk_ = self.ps()
            P.op("pe", lambda e, pk_=pk_: e.transpose(pk_[0:NS, 0:128], ks, self.fld("ident")), reads=[X["smp"], self.bpk], writes=[bpk_])
            P.op("act", lambda e, pk_=pk_: e.copy(kst, pk_[0:NS, 0:128]), reads=[bpk_], writes=[X["smp"]])
            io, _ = self.lay.fields["ident"]
            for b in range(NS):
                P.op("dve", lambda e, b=b: e.tensor_scalar(ksel, kst, self.pk[0:NS, io + b:io + b + 1], None, ALU.mult), reads=[X["smp"], self.bpk], writes=[X["sm2"]])
                pd, bpd = self.ps()
                P.op("pe", lambda e, pd=pd: e.matmul(pd[:, 0:128], ksel, vst, start=True, stop=True), reads=[X["sm2"]], writes=[bpd])
                P.op("dve", lambda e, pd=pd, b=b: e.scalar_tensor_tensor(S03[:, b, :], S03[:, b, :], fsm[:, b:b + 1], pd[:, 0:128], ALU.mult, ALU.add), reads=[bpd, X["S0"], X["smp"]], writes=[X["S0"]])
            P.dma("sp", self.o_hgs[j, hd], S03, reads=[X["S0"]])
            for ti in range(3):
                t0, n = TT[ti]
                P.op("act", lambda e, t0=t0, n=n: e.activation(tmp[:, 0:n], o[:, t0:t0 + n], AF.Square), reads=[X["o"]], writes=[X["tmp"]])
                pn, bpn = self.ps()
                P.op("pe", lambda e, pn=pn, n=n: e.matmul(pn[:, 0:n], self.ones_f[:], tmp[:, 0:n], start=True, stop=True), reads=[X["tmp"], self.bonesf], writes=[bpn])
                P.op("act", lambda e, pn=pn, n=n: e.activation(tmp[:, 0:n], pn[:, 0:n], AF.Sqrt, bias=self.eps_t[:, 0:1], scale=1.0 / 128), reads=[bpn, self.beps], writes=[X["tmp"]])
                P.op("dve", lambda e, n=n: e.reciprocal(tmp[:, 0:n], tmp[:, 0:n]), reads=[X["tmp"]], writes=[X["tmp"]])
                P.op("dve", lambda e, t0=t0, n=n: e.scalar_tensor_tensor(tmp[:, 0:n], o[:, t0:t0 + n], self.pk[:, o_nw + j:o_nw + j + 1], tmp[:, 0:n], ALU.mult, ALU.mult), reads=[X["o"], X["tmp"], self.bpk], writes=[X["tmp"]])
                P.op("dve", lambda e, t0=t0, n=n, hl=hl: e.tensor_tensor(yg[:, hl, t0:t0 + n], tmp[:, 0:n], sg[:, t0:t0 + n], ALU.mult), reads=[X["tmp"], X["sg"]], writes=[byg[hl][ti]])
            if hl == 3:
                self.outproj_group(self.w_out_ab[j], 8 + (hd // 4) * 4, 4, yg, byg, G, bG)
        self.alias(list(B.values()), mine)

    def build(self):
        P = self.P
        cfg = self.cfg
        self.eps_t = P.sbuf([128, 1], F32, "eps_t")
        self.beps = Buf()
        P.op("dve", lambda e: e.memset(self.eps_t[:], EPS), writes=[self.beps])
        self.setup()
        if cfg.get("mixer", True) and cfg.get("odd", True) and cfg.get("layers", DEPTH) > 1:
            self.odd_init()
        if cfg.get("mixer", True) and cfg.get("even", True) and cfg.get("layers", DEPTH) > 0:
            self.even_init()
        self.load_x()
        nl = cfg.get("layers", DEPTH)
        self.ada_pending = self.ada_items(0) if nl > 0 else []
        self.pump_ada(72)
        for l in range(nl):
            self.derive_mod(l, 0, 0.5)
            self.norm_mod(l, 0)
            if cfg.get("ffn", True):
                self.ffn(l, 0, 0)
            self.derive_mod(l, 1, 1.0)
            self.norm_mod(l, 1)
            if cfg.get("mixer", True):
                self.mixer(l)
            self.derive_mod(l, 2, 0.5)
            self.norm_mod(l, 2)
            if l + 1 < nl:
                self.ada_pending = self.ada_items(l + 1)
            if cfg.get("ffn", True):
                self.ffn(l, 1, 2, pump=2)
            self.pump_ada(72)
        self.final()
        return P.build()

    def mixer(self, l):
        if l % 2 == 1:
            if self.cfg.get("odd", True):
                self.odd_mixer(l)
        else:
            if self.cfg.get("even", True):
                self.even_mixer(l)


_CACHE = {}


def _get_nc(cfg):
    key = tuple(sorted(cfg.items()))
    if key not in _CACHE:
        _CACHE[key] = K(cfg).build()
    return _CACHE[key]


def _gate_band(w):
    dense = np.zeros((2560, 2560), np.float32)
    for n in range(16):
        dense[n * 160:(n + 1) * 160, n * 160:(n + 1) * 160] = w[n]
    out = np.zeros((20, 128, 3, 128), np.float32)
    for m in range(20):
        lo = 160 * ((128 * m) // 160)
        kc0 = lo // 128
        for s_ in range(3):
            kc = kc0 + s_
            if kc > 19:
                continue
            out[m, :, s_, :] = dense[kc * 128:(kc + 1) * 128, m * 128:(m + 1) * 128]
    return out


def kernel(cfg=None, **inp):
    cfg = dict(cfg or {})
    inp = {k: np.asarray(v) for k, v in inp.items()}
    pk = Pack()
    make_pack(pk, inp)
    pka = pk.array()
    xp, xs = inp["x_prompt"], inp["x_sample"]
    nl = max(1, cfg.get("layers", DEPTH))
    nf = nl if cfg.get("ffn", True) else 1
    use_odd = cfg.get("mixer", True) and cfg.get("odd", True) and nl > 1
    use_even = cfg.get("mixer", True) and cfg.get("even", True)
    shared = {"pk": pka, "w_ada": inp["w_ada"][:nl], "w_ffn_gu": inp["w_ffn_gu"][:nf], "w_ffn_d": inp["w_ffn_d"][:nf]}
    if use_odd:
        wg = np.stack([np.concatenate([_gate_band(inp["w_gate_a"][j]), _gate_band(inp["w_gate_x"][j])], axis=2).reshape(20, 128, 768)
                       for j in range(2)])
        shared.update({"w_in_c": inp["w_in_c"], "w_out_c": inp["w_out_c"], "wgate": np.ascontiguousarray(wg)})
    nje = 2 if nl > 2 else 1
    if use_even:
        def pl(a):
            return a
        bc = np.zeros((nje, 128, 4, 32, 16), np.float32)
        for j in range(nje):
            for q, key in enumerate(("s5_c_re", "s5_c_im")):
                bc[j, :, q] = inp[key][j].reshape(2, 32, 16, 64).transpose(0, 3, 1, 2).reshape(128, 32, 16)
            for q, key in enumerate(("s5_b_re", "s5_b_im")):
                bc[j, :, 2 + q] = inp[key][j].reshape(2, 32, 64, 16).transpose(0, 2, 1, 3).reshape(128, 32, 16)
        shared.update({"w_in_ab": inp["w_in_ab"][:nje], "s5_w_glu": inp["s5_w_glu"][:nje], "w_out_ab": inp["w_out_ab"][:nje], "s5bc": bc})
    in_maps = []
    for c in range(NCORES):
        seq, half = c // 2, c % 2
        xin = np.concatenate([xp[seq, half * NPR:(half + 1) * NPR], xs[c * NS:(c + 1) * NS, 0]], axis=0)
        cin = np.concatenate([inp["c_prompt"][seq:seq + 1], inp["c_sample"][c * NS:(c + 1) * NS]], axis=0)
        d = dict(shared)
        if use_even:
            st = np.zeros((nje, 128, 2, 32, NS), np.float32)
            for q, key in enumerate(("state_s5_re", "state_s5_im")):
                a = inp[key][:nje, c * NS:(c + 1) * NS]
                st[:, :, q] = a.reshape(nje, NS, 2, 32, 64).transpose(0, 2, 4, 3, 1).reshape(nje, 128, 32, NS)
            d["st_s5"] = st
            hg = inp["state_hgrn"][:nje, c * NS:(c + 1) * NS]
            d["st_hg"] = np.ascontiguousarray(hg.transpose(0, 2, 3, 1, 4))
            d["pm"] = np.full((128, 1), float(half), np.float32)
        d["xin"] = np.ascontiguousarray(xin, np.float32)
        d["cin"] = np.ascontiguousarray(cin, np.float32)
        if use_odd:
            sl = inp["state_lru"][:, c * NS:(c + 1) * NS]
            d["st_lru"] = np.ascontiguousarray(sl.reshape(2, NS, 20, 128).transpose(0, 3, 2, 1))
            scv = inp["state_conv"][:, c * NS:(c + 1) * NS]
            d["st_conv"] = np.ascontiguousarray(scv.reshape(2, NS, 3, 20, 128).transpose(0, 4, 3, 2, 1))
            d["pm"] = np.full((128, 1), float(half), np.float32)
        in_maps.append(d)
    nc = _get_nc(cfg)
    res = run_bass_kernel_spmd(nc, in_maps, core_ids=list(range(NCORES)))
    r = res.results
    f32 = np.float32
    y_prompt = np.zeros((4, 2048, D), f32)
    y_sample = np.zeros((128, 1, D), f32)
    s5r_p = np.zeros((2, 4, 64, 64), f32); s5i_p = np.zeros((2, 4, 64, 64), f32)
    hg_p = np.zeros((2, 4, 8, 128, 128), f32)
    lru_p = np.zeros((2, 4, 2560), f32); conv_p = np.zeros((2, 4, 3, 2560), f32)
    s5r_s = np.zeros((2, 128, 64, 64), f32); s5i_s = np.zeros((2, 128, 64, 64), f32)
    hg_s = np.zeros((2, 128, 8, 128, 128), f32)
    lru_s = np.zeros((2, 128, 2560), f32); conv_s = np.zeros((2, 128, 3, 2560), f32)
    for c in range(NCORES):
        seq, half = c // 2, c % 2
        rc = r[c]
        y_prompt[seq, half * NPR:(half + 1) * NPR] = rc["y"][0:NPR]
        y_sample[c * NS:(c + 1) * NS, 0] = rc["y"][NPR:T]
        bsl = slice(c * NS, (c + 1) * NS)
        if use_even:
            a = rc["o_s5s"].reshape(nje, 2, 64, 2, 32, NS)
            a = a.transpose(0, 3, 5, 1, 4, 2).reshape(nje, 2, NS, 64, 64)
            s5r_s[:nje, bsl] = a[:, 0]; s5i_s[:nje, bsl] = a[:, 1]
            hg_s[:nje, bsl] = rc["o_hgs"].transpose(0, 3, 1, 2, 4)
            if half == 1:
                a = rc["o_s5p"].reshape(nje, 2, 64, 2, 32).transpose(0, 3, 1, 4, 2).reshape(nje, 2, 64, 64)
                s5r_p[:nje, seq] = a[:, 0]; s5i_p[:nje, seq] = a[:, 1]
                hg_p[:nje, seq] = rc["o_hgp"]
        if use_odd:
            lru_s[:, bsl] = rc["o_lru_s"].transpose(0, 3, 2, 1).reshape(2, NS, 2560)
            conv_s[:, bsl] = rc["o_conv_s"].transpose(0, 4, 3, 2, 1).reshape(2, NS, 3, 2560)
            if half == 1:
                lru_p[:, seq] = rc["o_lru_p"].transpose(0, 2, 1).reshape(2, 2560)
                conv_p[:, seq] = rc["o_conv_p"].transpose(0, 3, 2, 1).reshape(2, 3, 2560)
    return (y_prompt, y_sample, s5r_p, s5i_p, hg_p, lru_p, conv_p, s5r_s, s5i_s, hg_s, lru_s, conv_s)
```

```python
import numpy as np
import concourse.bass as bass
import concourse.mybir as mybir
from concourse.bass_utils import run_bass_kernel_spmd
from contextlib import ExitStack

F32 = mybir.dt.float32
BF16 = mybir.dt.bfloat16
I32 = mybir.dt.int32
AF = mybir.ActivationFunctionType
ALU = mybir.AluOpType
AX = mybir.AxisListType

EPOCH = 30000
COMPUTE = ("pe", "act", "dve", "pool", "sp")

D = 2048
NPR = 1024
NS = 16
T = NPR + NS
DEPTH = 4
DFF = 5376
NCH = 16
TT = [(0, 512), (512, 512), (1024, 16)]
EPS = 1e-6
NCORES = 8


class Buf:
    __slots__ = ("w", "r")

    def __init__(self):
        self.w = {}
        self.r = {}


class Rec:
    __slots__ = ("fn", "waits", "marked", "dma")

    def __init__(self, fn):
        self.fn = fn
        self.waits = []
        self.marked = False
        self.dma = None


class Prog:
    def __init__(self, n_dma_sems=32, n_epochs=6):
        self.nc = bass.Bass("TRN2", target_bir_lowering=False)
        self.es = ExitStack()
        self.recs = {k: [] for k in COMPUTE}
        self.waited = {k: {} for k in COMPUTE}
        self.n_dma_sems = n_dma_sems
        self.n_epochs = n_epochs
        self.dma_tot = [0] * n_dma_sems
        self.dma_next = 0
        self._uid = 0
        self.colls = []

    def uid(self, p):
        self._uid += 1
        return f"{p}_{self._uid}"

    def sbuf(self, shape, dt, name=None):
        return self.es.enter_context(self.nc.sbuf_tensor(name or self.uid("sb"), list(shape), dt))

    def psum(self, shape, dt, name=None):
        return self.es.enter_context(self.nc.psum_tensor(name or self.uid("ps"), list(shape), dt))

    def dram_in(self, name, shape, dt=F32):
        return self.nc.dram_tensor(name, list(shape), dt, kind="ExternalInput").ap()

    def dram_out(self, name, shape, dt=F32):
        return self.nc.dram_tensor(name, list(shape), dt, kind="ExternalOutput").ap()

    def dram_tmp(self, name, shape, dt=F32):
        return self.nc.dram_tensor(name, list(shape), dt).ap()

    def _collect(self, reads, writes):
        deps = {}
        for b in reads:
            for k, v in b.w.items():
                if deps.get(k, -1) < v:
                    deps[k] = v
        for b in writes:
            for k, v in b.w.items():
                if deps.get(k, -1) < v:
                    deps[k] = v
            for k, v in b.r.items():
                if deps.get(k, -1) < v:
                    deps[k] = v
        return deps

    def _add_waits(self, eng, rec, deps):
        wd = self.waited[eng]
        for k, v in deps.items():
            if k == "pe" and eng == "pe":
                continue
            if wd.get(k, -1) >= v:
                continue
            wd[k] = v
            rec.waits.append((k, v))
            if isinstance(k, str):
                self.recs[k][v].marked = True

    def op(self, eng, fn, reads=(), writes=()):
        rec = Rec(fn)
        self._add_waits(eng, rec, self._collect(reads, writes))
        idx = len(self.recs[eng])
        self.recs[eng].append(rec)
        for b in reads:
            b.r[eng] = idx
        for b in writes:
            b.w = {eng: idx}
            b.r = {}
        return idx

    def dma(self, eng, out, in_, reads=(), writes=(), **kw):
        nsw = 8
        if eng == "pool":
            self.dma_next_sw = (getattr(self, "dma_next_sw", -1) + 1) % nsw
            s = self.n_dma_sems - nsw + self.dma_next_sw
        else:
            s = self.dma_next
            self.dma_next = (self.dma_next + 1) % (self.n_dma_sems - nsw)
        prev = self.dma_tot[s]
        tot = prev + 16
        self.dma_tot[s] = tot
        key = ("dma", s)

        def fn(e, out=out, in_=in_, kw=kw):
            return e.dma_start(out=out, in_=in_, **kw)

        rec = Rec(fn)
        rec.dma = (s, tot)
        deps = self._collect(reads, writes)
        if prev > 0:
            deps[key] = max(deps.get(key, -1), prev)
        self._add_waits(eng, rec, deps)
        self.recs[eng].append(rec)
        for b in reads:
            b.r[key] = tot
        for b in writes:
            b.w = {key: tot}
            b.r = {}

    def coll(self, kind, alu, ins, outs, groups, reads=(), writes=()):
        cid = len(self.colls)
        self.colls.append(None)
        key = ("cc", cid)

        def fn(e):
            return e.collective_compute(kind, alu, replica_groups=groups, ins=ins, outs=outs)

        rec = Rec(fn)
        rec.dma = ("cc", cid)
        self._add_waits("pool", rec, self._collect(reads, writes))
        self.recs["pool"].append(rec)
        for b in reads:
            b.r[key] = 1
        for b in writes:
            b.w = {key: 1}
            b.r = {}

    def build(self):
        nc = self.nc
        es = self.es
        sems = {k: [es.enter_context(nc.semaphore(f"s_{k}_{i}")) for i in range(self.n_epochs)] for k in COMPUTE}
        dsem = [es.enter_context(nc.semaphore(f"s_dma_{i}")) for i in range(self.n_dma_sems)]
        csem = [es.enter_context(nc.semaphore(f"s_cc_{i}")) for i in range(len(self.colls))]
        val = {}
        for k in COMPUTE:
            c = 0
            arr = []
            for r in self.recs[k]:
                if r.marked:
                    c += 1
                arr.append(c)
            val[k] = arr
            assert c < EPOCH * self.n_epochs, (k, c)
        fin = Rec(None)
        for s in range(self.n_dma_sems):
            if self.dma_tot[s] > 0:
                fin.waits.append((("dma", s), self.dma_tot[s]))
        self.recs["sp"].append(fin)
        val["sp"].append(val["sp"][-1] if val["sp"] else 0)

        def emit(k, e):
            for i, r in enumerate(self.recs[k]):
                for (wk, wv) in r.waits:
                    if isinstance(wk, str):
                        v = val[wk][wv]
                        e.wait_ge(sems[wk][(v - 1) // EPOCH], (v - 1) % EPOCH + 1)
                    elif wk[0] == "cc":
                        e.wait_ge(csem[wk[1]], 1)
                    else:
                        e.wait_ge(dsem[wk[1]], wv)
                if r.fn is None:
                    continue
                inst = r.fn(e)
                if r.dma is not None and r.dma[0] == "cc":
                    inst.then_inc(csem[r.dma[1]])
                elif r.dma is not None:
                    inst.then_inc(dsem[r.dma[0]], 16)
                elif r.marked:
                    v = val[k][i]
                    inst.then_inc(sems[k][(v - 1) // EPOCH], 1)

        with nc.Block() as block:
            @block.tensor
            def _(e):
                emit("pe", e)

            @block.scalar
            def _(e):
                emit("act", e)

            @block.vector
            def _(e):
                emit("dve", e)

            @block.gpsimd
            def _(e):
                emit("pool", e)

            @block.sync
            def _(e):
                emit("sp", e)
        es.close()
        return nc


def fm(v):
    v = np.asarray(v, np.float32)
    n = v.shape[-1] // 128
    lead = v.shape[:-1]
    r = v.reshape(lead + (n, 128))
    r = np.moveaxis(r, -1, 0)
    return np.ascontiguousarray(r.reshape(128, -1))


class Pack:
    def __init__(self):
        self.fields = {}
        self.cols = 0
        self.data = []

    def add(self, name, arr):
        arr = np.ascontiguousarray(arr, np.float32).reshape(128, -1)
        self.fields[name] = (self.cols, arr.shape[1])
        self.cols += arr.shape[1]
        self.data.append(arr)

    def array(self):
        return np.ascontiguousarray(np.concatenate(self.data, axis=1))


def pack_layout():
    pk = Pack()
    make_pack(pk, None)
    return pk


def make_pack(pk, inp):
    z = lambda *s: np.zeros(s, np.float32)
    g = (lambda k: inp[k]) if inp is not None else None
    pk.add("ident", np.eye(128, dtype=np.float32))
    pk.add("nw", fm(g("norm_w")) if inp is not None else z(128, 4 * 3 * 16))
    pk.add("fnw", fm(g("final_norm_w")) if inp is not None else z(128, 16))
    pk.add("bada", fm(g("b_ada")) if inp is not None else z(128, 4 * 144))
    pk.add("convw", fm(g("conv_w")) if inp is not None else z(128, 2 * 4 * 20))
    pk.add("convb", fm(g("conv_b")) if inp is not None else z(128, 2 * 20))
    pk.add("bga", fm(g("b_gate_a")) if inp is not None else z(128, 2 * 20))
    pk.add("bgx", fm(g("b_gate_x")) if inp is not None else z(128, 2 * 20))
    pk.add("lam", fm(g("lru_lambda")) if inp is not None else z(128, 2 * 20))
    pairl = lambda a: np.stack([a[j].reshape(2, 32, 64).transpose(0, 2, 1).reshape(128, 32) for j in range(2)], axis=1)
    pk.add("s5lamr", pairl(g("s5_lam_re")) if inp is not None else z(128, 64))
    pk.add("s5lami", pairl(g("s5_lam_im")) if inp is not None else z(128, 64))
    pk.add("s5lstep", np.stack([np.broadcast_to(g("s5_log_step")[j].reshape(2, 1, 32), (2, 64, 32)).reshape(128, 32) for j in range(2)], axis=1) if inp is not None else z(128, 64))
    pk.add("s5d2", np.stack([np.broadcast_to(g("s5_d")[j].reshape(1, 64, 16).transpose(0, 2, 1), (8, 16, 64)).reshape(128, 64) for j in range(2)], axis=1) if inp is not None else z(128, 128))
    pk.add("bglu", fm(g("s5_b_glu")) if inp is not None else z(128, 16))
    pk.add("hglog", fm(g("hg_lb_logits")) if inp is not None else z(128, 16))
    pk.add("hgnw", np.ascontiguousarray(g("hg_norm_w").T) if inp is not None else z(128, 2))
    r_ = np.arange(128)
    pk.add("tmask2", -((r_[None, :] // 16) < (r_[:, None] // 16)).astype(np.float32))
    cm = np.zeros((128, 64), np.float32); cm[0:64] = (np.arange(64)[None, :] >= np.arange(64)[:, None])
    pk.add("cmask64", cm)


class K:
    def __init__(self, cfg):
        self.cfg = cfg
        P = self.P = Prog()
        self.lay = pack_layout()
        self.xin = P.dram_in("xin", [T, D])
        self.cin = P.dram_in("cin", [NS + 1, D])
        self.pkd = P.dram_in("pk", [128, self.lay.cols])
        nl = max(1, cfg.get("layers", DEPTH))
        nf = nl if cfg.get("ffn", True) else 1
        self.w_ada = P.dram_in("w_ada", [nl, D, 9 * D])
        self.w_gu = P.dram_in("w_ffn_gu", [nf, 2, D, 2 * DFF])
        self.w_d = P.dram_in("w_ffn_d", [nf, 2, DFF, D])
        self.y = P.dram_out("y", [T, D])
        self.xres = P.sbuf([128, NCH, T], F32, "xres")
        self.bx = [[Buf() for _ in TT] for _ in range(NCH)]
        self.hT = P.sbuf([128, NCH, T], BF16, "hT")
        self.bh = [[Buf() for _ in TT] for _ in range(NCH)]
        self.arena = P.sbuf([128, 13408], F32, "arena")
        self.hid = self.arena[:, 0:7280].bitcast(BF16).rearrange("p (a t) -> p a t", t=T)
        self.bhid = [[Buf() for _ in TT] for _ in range(14)]
        self.pk = P.sbuf([128, self.lay.cols], F32, "pk_sb")
        self.bpk = Buf()
        self.mT = P.sbuf([128, 144, NS + 1], F32, "mT")
        self.bm = [Buf() for _ in range(9)]
        self.scT = P.sbuf([128, NCH, NS + 1], BF16, "scT")
        self.bsc = Buf()
        self.ones_bf = P.sbuf([128, 128], BF16, "ones_bf")
        self.bones = Buf()
        self.scr = self.arena[:, 7280:7280 + 5136]
        self.sm = self.arena[:, 7280 + 5136:13408]
        self.bscr = [Buf() for _ in range(8)]
        self.NSLOT = 3
        self.wsl = [P.sbuf([128, 4096], BF16, f"wsl{i}") for i in range(self.NSLOT)]
        self.bws = [Buf() for _ in range(self.NSLOT)]
        self.wnext = 0
        self.xslots = []
        self.Amod = P.sbuf([128, NCH, NS + 1], F32, "Amod")
        self.bA = Buf()
        self.Gmod = [P.sbuf([128, NCH, NS + 1], F32, f"Gmod{i}") for i in range(3)]
        self.bG = [Buf() for _ in range(3)]
        self.rstd = P.sbuf([128, 512], F32, "rstd")
        self.brstd = Buf()
        self.tmpA = [P.sbuf([128, 512], F32, f"tmpA{i}") for i in range(2)]
        self.btmpA = [Buf(), Buf()]
        self.tmpB = [P.sbuf([128, 512], BF16, f"tmpB{i}") for i in range(2)]
        self.btmpB = [Buf(), Buf()]
        self.tmpS = P.sbuf([128, NCH, NS], F32, "tmpS")
        self.btmpS = Buf()
        self.rr = 0
        self.pst = [P.psum([128, 512], F32, f"ps{i}") for i in range(7)]
        self.bps = [Buf() for _ in range(7)]
        self.psn = 0
        self.ps_ada = P.psum([128, 512], F32, "ps_ada")
        self.bps_ada = Buf()
        self.ada_pending = []

    def ps(self):
        i = self.psn
        self.psn = (self.psn + 1) % 7
        return self.pst[i], self.bps[i]

    def fld(self, name, a=0, b=None):
        o, w = self.lay.fields[name]
        if b is None:
            b = w
        return self.pk[:, o + a:o + b]

    def wload(self, src, kc, ncols):
        nsl = self.NSLOT + len(self.xslots)
        i = self.wnext % nsl
        self.wnext = (i + 1) % nsl
        if i < self.NSLOT:
            tile, buf = self.wsl[i], self.bws[i]
        else:
            tile, buf = self.xslots[i - self.NSLOT]
        view = tile[:, 0:kc * ncols].rearrange("p (c n) -> p c n", n=ncols)
        self.P.dma("pool", view, src.rearrange("(c p) n -> p c n", p=128), writes=[buf])
        return view, buf

    def ewise_eng(self):
        self.rr += 1
        return "act" if self.rr % 2 else "dve"

    def setup(self):
        P = self.P
        P.dma("sp", self.pk[:], self.pkd, writes=[self.bpk])
        P.op("dve", lambda e: e.memset(self.ones_bf[:], 1.0), writes=[self.bones])
        ident = self.fld("ident")
        cst = self.scr[0:NS + 1, 0:D]
        bc = self.bscr[0]
        P.dma("sp", cst, self.cin, writes=[bc])
        P.op("act", lambda e: e.activation(cst, cst, AF.Silu), reads=[bc], writes=[bc])
        pp, bp = self.ps()
        for c in range(NCH):
            P.op("pe", lambda e, c=c: e.transpose(pp[:, c * 17:(c + 1) * 17], cst[:, c * 128:(c + 1) * 128], ident[0:NS + 1, 0:NS + 1]),
                 reads=[bc, self.bpk], writes=[bp])
        P.op("dve", lambda e: e.tensor_copy(self.scT[:], pp[:, 0:NCH * 17].rearrange("p (c t) -> p c t", t=17)),
             reads=[bp], writes=[self.bsc])

    def load_x(self):
        P = self.P
        ident = self.fld("ident")
        stv = self.scr[:, 0:4096].rearrange("p (a f) -> p a f", f=D)
        bst = self.bscr[0]
        for g2 in range(NPR // 256):
            ti = (g2 * 256) // 512
            P.dma("sp", stv, self.xin[g2 * 256:(g2 + 1) * 256, :].rearrange("(a p) f -> p a f", p=128), writes=[bst])
            for c in range(NCH):
                pp, bp = self.ps()
                for a in range(2):
                    P.op("pe", lambda e, pp=pp, a=a, c=c: e.transpose(pp[:, a * 128:(a + 1) * 128], stv[:, a, c * 128:(c + 1) * 128], ident),
                         reads=[bst, self.bpk], writes=[bp])
                eng = self.ewise_eng()
                dst = self.xres[:, c, g2 * 256:(g2 + 1) * 256]
                if eng == "act":
                    P.op("act", lambda e, pp=pp, dst=dst: e.copy(dst, pp[:, 0:256]), reads=[bp], writes=[self.bx[c][ti]])
                else:
                    P.op("dve", lambda e, pp=pp, dst=dst: e.tensor_copy(dst, pp[:, 0:256]), reads=[bp], writes=[self.bx[c][ti]])
        sst = self.scr[0:NS, 0:D]
        P.dma("sp", sst, self.xin[NPR:T, :], writes=[self.bscr[0]])
        pp, bp = self.ps()
        for c in range(NCH):
            P.op("pe", lambda e, c=c: e.transpose(pp[:, c * NS:(c + 1) * NS], sst[:, c * 128:(c + 1) * 128], ident[0:NS, 0:NS]),
                 reads=[self.bscr[0], self.bpk], writes=[bp])
        P.op("dve", lambda e: e.tensor_copy(self.xres[:, :, NPR:T], pp[:, 0:NCH * NS].rearrange("p (c t) -> p c t", t=NS)),
             reads=[bp], writes=[self.bx[c][2] for c in range(NCH)])

    def ada_items(self, l):
        P = self.P
        items = []
        for blk in range(72):
            def item(blk=blk):
                oc0 = blk * 2
                wv, bw = self.wload(self.w_ada[l, :, oc0 * 128:(oc0 + 2) * 128], NCH, 256)
                grp = oc0 // 16
                for j in range(2):
                    oc = oc0 + j
                    loc = oc % 16
                    dst = self.ps_ada[:, loc * 17:(loc + 1) * 17]
                    for kc in range(NCH):
                        P.op("pe", lambda e, dst=dst, wv=wv, j=j, kc=kc: e.matmul(dst, wv[:, kc, j * 128:(j + 1) * 128], self.scT[:, kc, :], start=(kc == 0), stop=(kc == NCH - 1)),
                             reads=[bw, self.bsc], writes=[self.bps_ada])
                if oc0 % 16 == 14:
                    o, _ = self.lay.fields["bada"]
                    bias = self.pk[:, o + l * 144 + grp * 16:o + l * 144 + grp * 16 + 16].unsqueeze(2).broadcast_to([128, 16, NS + 1])
                    P.op("dve", lambda e, grp=grp, bias=bias: e.tensor_tensor(self.mT[:, grp * 16:(grp + 1) * 16, :], self.ps_ada[:, 0:16 * 17].rearrange("p (c t) -> p c t", t=17), bias, ALU.add),
                         reads=[self.bps_ada, self.bpk], writes=[self.bm[grp]])
            items.append(item)
        return items

    def pump_ada(self, n):
        for _ in range(n):
            if self.ada_pending:
                self.ada_pending.pop(0)()

    def derive_mod(self, l, s, gscale):
        P = self.P
        o, _ = self.lay.fields["nw"]
        nwb = self.pk[:, o + (l * 3 + s) * 16:o + (l * 3 + s) * 16 + 16].unsqueeze(2).broadcast_to([128, 16, NS + 1])
        sc = self.mT[:, (3 * s + 1) * 16:(3 * s + 2) * 16, :]
        P.op("dve", lambda e: e.scalar_tensor_tensor(self.Amod[:], sc, 1.0, nwb, ALU.add, ALU.mult),
             reads=[self.bm[3 * s + 1], self.bpk], writes=[self.bA])
        gt = self.mT[:, (3 * s + 2) * 16:(3 * s + 3) * 16, :]
        P.op("dve", lambda e: e.tensor_scalar(self.Gmod[s][:], gt, float(gscale), None, ALU.mult),
             reads=[self.bm[3 * s + 2]], writes=[self.bG[s]])

    def rms_stats(self, ti):
        P = self.P
        t0, n = TT[ti]
        pp, bp = self.ps()
        for c in range(NCH):
            k = c % 2
            P.op("act", lambda e, c=c, k=k: e.activation(self.tmpB[k][:, 0:n], self.xres[:, c, t0:t0 + n], AF.Square),
                 reads=[self.bx[c][ti]], writes=[self.btmpB[k]])
            P.op("pe", lambda e, c=c, k=k: e.matmul(pp[:, 0:n], self.ones_bf[:], self.tmpB[k][:, 0:n], start=(c == 0), stop=(c == NCH - 1)),
                 reads=[self.btmpB[k], self.bones], writes=[bp])
        return pp, bp

    def norm_mod(self, l, s):
        P = self.P
        shift = lambda c, a, b: self.mT[:, 3 * s * 16 + c, a:b]
        for ti in range(3):
            t0, n = TT[ti]
            pp, bp = self.rms_stats(ti)
            P.op("act", lambda e, pp=pp, n=n: e.activation(self.rstd[:, 0:n], pp[:, 0:n], AF.Sqrt, bias=self.eps_t[:, 0:1], scale=1.0 / D),
                 reads=[bp, self.beps], writes=[self.brstd])
            P.op("dve", lambda e, n=n: e.reciprocal(self.rstd[:, 0:n], self.rstd[:, 0:n]), reads=[self.brstd], writes=[self.brstd])
            if ti < 2:
                for c in range(NCH):
                    k = c % 2
                    P.op("dve", lambda e, c=c, k=k, t0=t0, n=n: e.scalar_tensor_tensor(self.tmpA[k][:], self.xres[:, c, t0:t0 + n], self.Amod[:, c, 0:1], self.rstd[:], ALU.mult, ALU.mult),
                         reads=[self.bx[c][ti], self.bA, self.brstd], writes=[self.btmpA[k]])
                    P.op("act", lambda e, c=c, k=k, t0=t0, n=n: e.activation(self.hT[:, c, t0:t0 + n], self.tmpA[k][:], AF.Identity, bias=shift(c, 0, 1)),
                         reads=[self.btmpA[k], self.bm[3 * s]], writes=[self.bh[c][ti]])
            else:
                xs = self.xres[:, :, NPR:T]
                rb = self.rstd[:, 0:NS].unsqueeze(1).broadcast_to([128, NCH, NS])
                allx = [self.bx[c][2] for c in range(NCH)]
                P.op("dve", lambda e: e.tensor_tensor(self.tmpS[:], xs, rb, ALU.mult), reads=allx + [self.brstd], writes=[self.btmpS])
                P.op("dve", lambda e: e.tensor_tensor(self.tmpS[:], self.tmpS[:], self.Amod[:, :, 1:NS + 1], ALU.mult), reads=[self.btmpS, self.bA], writes=[self.btmpS])
                P.op("dve", lambda e: e.tensor_tensor(self.hT[:, :, NPR:T], self.tmpS[:], self.mT[:, 3 * s * 16:3 * s * 16 + 16, 1:NS + 1], ALU.add),
                     reads=[self.btmpS, self.bm[3 * s]], writes=[self.bh[c][2] for c in range(NCH)])

    def resid_add(self, pp, bp, oc, ti, G, bG):
        P = self.P
        t0, n = TT[ti]
        if ti < 2:
            P.op("dve", lambda e: e.scalar_tensor_tensor(self.xres[:, oc, t0:t0 + n], pp[:, 0:n], G[:, oc, 0:1], self.xres[:, oc, t0:t0 + n], ALU.mult, ALU.add),
                 reads=[bp, bG, self.bx[oc][ti]], writes=[self.bx[oc][ti]])
        else:
            tmp = self.tmpS[:, 0, :]
            P.op("dve", lambda e: e.tensor_tensor(tmp, pp[:, 0:n], G[:, oc, 1:NS + 1], ALU.mult), reads=[bp, bG], writes=[self.btmpS])
            P.op("dve", lambda e: e.tensor_tensor(self.xres[:, oc, t0:t0 + n], self.xres[:, oc, t0:t0 + n], tmp, ALU.add),
                 reads=[self.btmpS, self.bx[oc][ti]], writes=[self.bx[oc][ti]])

    def ffn(self, l, w, s, pump=0):
        P = self.P
        G, bG = self.Gmod[s], self.bG[s]
        xb = [Buf(), Buf()]
        self.alias(xb, self.arena_bufs())
        self.xslots = [(self.arena[:, 7280 + i * 2048:7280 + (i + 1) * 2048].bitcast(BF16), xb[i]) for i in range(2)]
        self._ffn_body(l, w, s, pump, G, bG)
        self.xslots = []
        self.wnext = self.wnext % self.NSLOT
        for ab in self.arena_bufs():
            for b_ in xb:
                for d in (b_.w, b_.r):
                    for k_, v_ in d.items():
                        if ab.r.get(k_, -1) < v_:
                            ab.r[k_] = v_

    def _ffn_body(self, l, w, s, pump, G, bG):
        P = self.P
        for third in range(3):
            for hp in range(7):
                hc0 = third * 14 + hp * 2
                wg, bwg = self.wload(self.w_gu[l, w, :, hc0 * 128:(hc0 + 2) * 128], NCH, 256)
                wu, bwu = self.wload(self.w_gu[l, w, :, DFF + hc0 * 128:DFF + (hc0 + 2) * 128], NCH, 256)
                for j in range(2):
                    hl = hp * 2 + j
                    for ti in range(3):
                        t0, n = TT[ti]
                        pg, bpg = self.ps()
                        pu, bpu = self.ps()
                        for kc in range(NCH):
                            P.op("pe", lambda e, pg=pg, kc=kc, j=j, wg=wg, t0=t0, n=n: e.matmul(pg[:, 0:n], wg[:, kc, j * 128:(j + 1) * 128], self.hT[:, kc, t0:t0 + n], start=(kc == 0), stop=(kc == NCH - 1)),
                                 reads=[bwg, self.bh[kc][ti]], writes=[bpg])
                        for kc in range(NCH):
                            P.op("pe", lambda e, pu=pu, kc=kc, j=j, wu=wu, t0=t0, n=n: e.matmul(pu[:, 0:n], wu[:, kc, j * 128:(j + 1) * 128], self.hT[:, kc, t0:t0 + n], start=(kc == 0), stop=(kc == NCH - 1)),
                                 reads=[bwu, self.bh[kc][ti]], writes=[bpu])
                        k = (hl * 3 + ti) % 2
                        P.op("act", lambda e, pg=pg, k=k, n=n: e.activation(self.tmpA[k][:, 0:n], pg[:, 0:n], AF.Silu), reads=[bpg], writes=[self.btmpA[k]])
                        P.op("dve", lambda e, pu=pu, k=k, n=n, hl=hl, t0=t0: e.tensor_tensor(self.hid[:, hl, t0:t0 + n], self.tmpA[k][:, 0:n], pu[:, 0:n], ALU.mult),
                             reads=[self.btmpA[k], bpu], writes=[self.bhid[hl][ti]])
                self.pump_ada(pump)
            for ocp in range(8):
                wd, bwd = self.wload(self.w_d[l, w, third * 1792:(third + 1) * 1792, ocp * 256:(ocp + 1) * 256], 14, 256)
                for j in range(2):
                    oc = ocp * 2 + j
                    for ti in range(3):
                        t0, n = TT[ti]
                        pp, bp = self.ps()
                        for kc in range(14):
                            P.op("pe", lambda e, pp=pp, kc=kc, j=j, wd=wd, t0=t0, n=n: e.matmul(pp[:, 0:n], wd[:, kc, j * 128:(j + 1) * 128], self.hid[:, kc, t0:t0 + n], start=(kc == 0), stop=(kc == 13)),
                                 reads=[bwd, self.bhid[kc][ti]], writes=[bp])
                        self.resid_add(pp, bp, oc, ti, G, bG)
                self.pump_ada(pump)

    def final(self):
        P = self.P
        ident = self.fld("ident")
        fo, _ = self.lay.fields["fnw"]
        rst = self.scr[:, 0:1040]
        brst = self.bscr[0]
        for ti in range(3):
            t0, n = TT[ti]
            pp, bp = self.rms_stats(ti)
            P.op("act", lambda e, pp=pp, n=n, t0=t0: e.activation(rst[:, t0:t0 + n], pp[:, 0:n], AF.Sqrt, bias=self.eps_t[:, 0:1], scale=1.0 / D),
                 reads=[bp, self.beps], writes=[brst])
        P.op("dve", lambda e: e.reciprocal(rst, rst), reads=[brst], writes=[brst])
        for c in range(NCH):
            for ti in range(3):
                t0, n = TT[ti]
                P.op("dve", lambda e, c=c, t0=t0, n=n: e.scalar_tensor_tensor(self.xres[:, c, t0:t0 + n], self.xres[:, c, t0:t0 + n], self.pk[:, fo + c:fo + c + 1], rst[:, t0:t0 + n], ALU.mult, ALU.mult),
                     reads=[self.bx[c][ti], self.bpk, brst], writes=[self.bx[c][ti]])
        ost = [self.scr[:, 1040:3088], self.scr[:, 3088:5136]]
        bost = [self.bscr[1], self.bscr[2]]
        ntile = NPR // 128
        for a in range(ntile + 1):
            rows = 128 if a < ntile else NS
            k = a % 2
            ti = (a * 128) // 512 if a < ntile else 2
            for q in range(4):
                pp, bp = self.ps()
                for cc in range(4):
                    c = q * 4 + cc
                    P.op("pe", lambda e, pp=pp, cc=cc, c=c, a=a, rows=rows: e.transpose(pp[0:rows, cc * 128:(cc + 1) * 128], self.xres[:, c, a * 128:a * 128 + rows], ident),
                         reads=[self.bx[c][ti], self.bpk], writes=[bp])
                eng = self.ewise_eng()
                dst = ost[k][0:rows, q * 512:(q + 1) * 512]
                if eng == "act":
                    P.op("act", lambda e, pp=pp, dst=dst, rows=rows: e.copy(dst, pp[0:rows, :]), reads=[bp], writes=[bost[k]])
                else:
                    P.op("dve", lambda e, pp=pp, dst=dst, rows=rows: e.tensor_copy(dst, pp[0:rows, :]), reads=[bp], writes=[bost[k]])
            P.dma("sp", self.y[a * 128:a * 128 + rows, :], ost[k][0:rows, :], reads=[bost[k]])

    def alias(self, new_bufs, old_bufs):
        w = {}
        for ob in old_bufs:
            for d in (ob.w, ob.r):
                for k, v in d.items():
                    if w.get(k, -1) < v:
                        w[k] = v
        for nb in new_bufs:
            nb.w = {}
            nb.r = dict(w)

    def arena_bufs(self):
        return [b for row in self.bhid for b in row] + list(self.bscr)

    def odd_init(self):
        P = self.P
        cfg = self.cfg
        self.w_in_c = P.dram_in("w_in_c", [2, D, 5120])
        self.w_out_c = P.dram_in("w_out_c", [2, 2560, D])
        self.wgate = P.dram_in("wgate", [2, 20, 128, 768])
        self.st_lru = P.dram_in("st_lru", [2, 128, 20, NS])
        self.st_conv = P.dram_in("st_conv", [2, 128, 20, 3, NS])
        self.pmd = P.dram_in("pm", [128, 1])
        self.o_lru_p = P.dram_out("o_lru_p", [2, 128, 20])
        self.o_lru_s = P.dram_out("o_lru_s", [2, 128, 20, NS])
        self.o_conv_p = P.dram_out("o_conv_p", [2, 128, 20, 3])
        self.o_conv_s = P.dram_out("o_conv_s", [2, 128, 20, 3, NS])
        self.a_d = P.dram_tmp("a_d", [20, 128, NPR])
        self.b_d = P.dram_tmp("b_d", [20, 128, NPR])
        self.ba_d = [[Buf(), Buf()] for _ in range(20)]
        self.bb_d = [[Buf(), Buf()] for _ in range(20)]
        self.agin1 = P.dram_tmp("agin1", [128, 60]); self.agout1 = P.dram_tmp("agout1", [256, 60])
        self.agin2 = P.dram_tmp("agin2", [128, 20]); self.agout2 = P.dram_tmp("agout2", [256, 20])
        self.bag = [Buf() for _ in range(4)]
        self.pm = P.sbuf([128, 1], F32, "pm_sb"); self.bpm = Buf()
        P.dma("sp", self.pm[:], self.pmd, writes=[self.bpm])
        self.sm = self.arena[:, 10807:13408]
        self.bsm = {}
        off = [0]

        def carve(name, n):
            v = self.sm[:, off[0]:off[0] + n]
            off[0] += n
            self.bsm[name] = Buf()
            return v
        self.clam = carve("clam", 20)
        self.tails = carve("tails", 60)
        self.halo = carve("halo", 60)
        self.hend = carve("hend", 20)
        self.hinit = carve("hinit", 20)
        self.hcur = carve("hcur", 20)
        self.xs_all = carve("xs_all", 320)
        self.xc_s = carve("xc_s", 320)
        self.hs_all = carve("hs_all", 320)
        self.h0s = carve("h0s", 320)
        self.tsm = carve("tsm", 80)
        self.tsm2 = carve("tsm2", 80)
        self.tsm_big = carve("cbuf", 960)
        self.bcbuf = self.bsm["cbuf"]
        assert off[0] <= 2601, off[0]

    def band_tiles(self, m):
        lo = 160 * ((128 * m) // 160)
        kc0 = lo // 128
        res = []
        for s in range(3):
            kc = kc0 + s
            if kc > 19:
                continue
            blocks_out = set(range((128 * m) // 160, (128 * m + 127) // 160 + 1))
            blocks_in = set(range((128 * kc) // 160, (128 * kc + 127) // 160 + 1))
            if blocks_out & blocks_in:
                res.append((s, kc))
        return kc0, res

    def odd_mixer(self, l):
        P = self.P
        j = l // 2
        o_cw, _ = self.lay.fields["convw"]
        o_cb, _ = self.lay.fields["convb"]
        o_ga, _ = self.lay.fields["bga"]
        o_gx, _ = self.lay.fields["bgx"]
        o_lm, _ = self.lay.fields["lam"]
        cw = lambda tap, m: self.pk[:, o_cw + j * 80 + tap * 20 + m:o_cw + j * 80 + tap * 20 + m + 1]
        cwv = lambda tap, m0, k: self.pk[:, o_cw + j * 80 + tap * 20 + m0:o_cw + j * 80 + tap * 20 + m0 + k]
        cb = lambda m: self.pk[:, o_cb + j * 20 + m:o_cb + j * 20 + m + 1]
        cbv = lambda m0, k: self.pk[:, o_cb + j * 20 + m0:o_cb + j * 20 + m0 + k]
        bga = lambda m: self.pk[:, o_ga + j * 20 + m:o_ga + j * 20 + m + 1]
        bgx = lambda m: self.pk[:, o_gx + j * 20 + m:o_gx + j * 20 + m + 1]
        bs = self.bsm
        G, bG = self.Gmod[1], self.bG[1]
        xcb = self.arena[:, 0:2600].bitcast(BF16).rearrange("p (a t) -> p a t", t=T)
        bxcb = [[Buf() for _ in range(3)] for _ in range(5)]
        tmp = [self.arena[:, 2600 + i * 512:2600 + (i + 1) * 512] for i in range(6)]
        btmp = [Buf() for _ in range(6)]
        xp = self.arena[:, 5672:5672 + 5 * (NPR + 3)].rearrange("p (a t) -> p a t", t=NPR + 3)
        bxp = [[Buf() for _ in range(2)] for _ in range(5)]
        bxph = Buf()
        mine = [b for r in bxcb for b in r] + btmp + [b for r in bxp for b in r] + [bxph] + list(self.bsm.values())
        self.alias(mine, self.arena_bufs())

        lam = self.pk[:, o_lm + j * 20:o_lm + j * 20 + 20]
        P.op("act", lambda e: e.activation(self.clam, lam, AF.Sigmoid), reads=[self.bpk], writes=[bs["clam"]])
        P.op("act", lambda e: e.activation(self.clam, self.clam, AF.Ln), reads=[bs["clam"]], writes=[bs["clam"]])
        P.op("dve", lambda e: e.tensor_scalar(self.clam, self.clam, 8.0, None, ALU.mult), reads=[bs["clam"]], writes=[bs["clam"]])
        cbuf = self.tsm_big
        P.dma("sp", self.h0s.rearrange("p (a t) -> p a t", t=NS), self.st_lru[j], writes=[bs["h0s"]])
        P.dma("sp", cbuf[:].rearrange("p (a k t) -> p a k t", k=3, t=NS), self.st_conv[j], writes=[self.bcbuf])

        pp, bp = self.ps()
        for pr in range(10):
            wv, bw = self.wload(self.w_in_c[j, :, 2560 + pr * 256:2560 + (pr + 1) * 256], NCH, 256)
            for jj in range(2):
                m = pr * 2 + jj
                for kc in range(NCH):
                    P.op("pe", lambda e, m=m, jj=jj, kc=kc, wv=wv: e.matmul(pp[:, m * 3:(m + 1) * 3], wv[:, kc, jj * 128:(jj + 1) * 128], self.hT[:, kc, NPR - 3:NPR], start=(kc == 0), stop=(kc == NCH - 1)),
                         reads=[bw, self.bh[kc][1]], writes=[bp])
        P.op("dve", lambda e: e.tensor_copy(self.tails, pp[:, 0:60]), reads=[bp], writes=[bs["tails"]])
        P.dma("sp", self.o_conv_p[j].rearrange("p a k -> p (a k)"), self.tails, reads=[bs["tails"]])
        P.dma("sp", self.agin1, self.tails, reads=[bs["tails"]], writes=[self.bag[0]])
        P.coll("AllGather", ALU.bypass, [self.agin1], [self.agout1], [[0, 1], [2, 3], [4, 5], [6, 7]], reads=[self.bag[0]], writes=[self.bag[1]])
        P.dma("sp", self.halo, self.agout1[0:128, :], reads=[self.bag[1]], writes=[bs["halo"]])
        P.op("dve", lambda e: e.tensor_scalar(self.halo, self.halo, self.pm[:, 0:1], None, ALU.mult), reads=[bs["halo"], self.bpm], writes=[bs["halo"]])

        xs3 = self.xs_all.rearrange("p (a t) -> p a t", t=NS)
        xcs3 = self.xc_s.rearrange("p (a t) -> p a t", t=NS)
        hs3 = self.hs_all.rearrange("p (a t) -> p a t", t=NS)
        h0s3 = self.h0s.rearrange("p (a t) -> p a t", t=NS)
        cb4 = cbuf[:].rearrange("p (a k t) -> p a k t", k=3, t=NS)
        cnt = [0]
        for g in range(4):
            m0 = g * 5
            P.op("dve", lambda e, m0=m0: e.tensor_copy(xp[:, :, 0:3], self.halo[:, m0 * 3:(m0 + 5) * 3].rearrange("p (a k) -> p a k", k=3)),
                 reads=[bs["halo"]], writes=[bxph])
            for (ma, nb) in ((0, 2), (2, 2), (4, 1)):
                wv, bw = self.wload(self.w_in_c[j, :, 2560 + (m0 + ma) * 128:2560 + (m0 + ma + nb) * 128], NCH, nb * 128)
                for jj in range(nb):
                    ml = ma + jj
                    m = m0 + ml
                    for ti in range(3):
                        t0, n = TT[ti]
                        ps_, bps_ = self.ps()
                        for kc in range(NCH):
                            P.op("pe", lambda e, ps_=ps_, jj=jj, kc=kc, wv=wv, t0=t0, n=n: e.matmul(ps_[:, 0:n], wv[:, kc, jj * 128:(jj + 1) * 128], self.hT[:, kc, t0:t0 + n], start=(kc == 0), stop=(kc == NCH - 1)),
                                 reads=[bw, self.bh[kc][ti]], writes=[bps_])
                        if ti < 2:
                            P.op("act", lambda e, ps_=ps_, ml=ml, t0=t0, n=n: e.copy(xp[:, ml, 3 + t0:3 + t0 + n], ps_[:, 0:n]), reads=[bps_], writes=[bxp[ml][ti]])
                        else:
                            P.op("act", lambda e, ps_=ps_, m=m: e.copy(xs3[:, m, :], ps_[:, 0:NS]), reads=[bps_], writes=[bs["xs_all"]])
            for ml in range(5):
                m = m0 + ml
                for ti in range(2):
                    t0, n = TT[ti]
                    tx = tmp[4]; btx = btmp[4]
                    rd = [bxp[ml][0], bxp[ml][1], bxph, self.bpk]
                    P.op("dve", lambda e, ml=ml, m=m, t0=t0, n=n, tx=tx: e.tensor_scalar(tx, xp[:, ml, t0:t0 + n], cw(0, m), cb(m), ALU.mult, ALU.add), reads=rd, writes=[btx])
                    P.op("dve", lambda e, ml=ml, m=m, t0=t0, n=n, tx=tx: e.scalar_tensor_tensor(tx, xp[:, ml, t0 + 1:t0 + 1 + n], cw(1, m), tx, ALU.mult, ALU.add), reads=rd + [btx], writes=[btx])
                    P.op("dve", lambda e, ml=ml, m=m, t0=t0, n=n, tx=tx: e.scalar_tensor_tensor(tx, xp[:, ml, t0 + 2:t0 + 2 + n], cw(2, m), tx, ALU.mult, ALU.add), reads=rd + [btx], writes=[btx])
                    P.op("dve", lambda e, ml=ml, m=m, t0=t0, n=n, tx=tx: e.scalar_tensor_tensor(xcb[:, ml, t0:t0 + n], xp[:, ml, t0 + 3:t0 + 3 + n], cw(3, m), tx, ALU.mult, ALU.add), reads=rd + [btx], writes=[bxcb[ml][ti]])
            bc_ = lambda v: v.unsqueeze(2).broadcast_to([128, 5, NS])
            t5 = self.tsm.rearrange("p (a t) -> p a t", t=NS)
            t5b = self.tsm2.rearrange("p (a t) -> p a t", t=NS)
            P.op("dve", lambda e, m0=m0: e.tensor_tensor(t5, cb4[:, m0:m0 + 5, 0, :], bc_(cwv(0, m0, 5)), ALU.mult), reads=[self.bcbuf, self.bpk], writes=[bs["tsm"]])
            for tap in (1, 2):
                P.op("dve", lambda e, m0=m0, tap=tap: e.tensor_tensor(t5b, cb4[:, m0:m0 + 5, tap, :], bc_(cwv(tap, m0, 5)), ALU.mult), reads=[self.bcbuf, self.bpk], writes=[bs["tsm2"]])
                P.op("dve", lambda e: e.tensor_tensor(t5, t5, t5b, ALU.add), reads=[bs["tsm"], bs["tsm2"]], writes=[bs["tsm"]])
            P.op("dve", lambda e, m0=m0: e.tensor_tensor(t5b, xs3[:, m0:m0 + 5, :], bc_(cwv(3, m0, 5)), ALU.mult), reads=[bs["xs_all"], self.bpk], writes=[bs["tsm2"]])
            P.op("dve", lambda e: e.tensor_tensor(t5, t5, t5b, ALU.add), reads=[bs["tsm"], bs["tsm2"]], writes=[bs["tsm"]])
            P.op("dve", lambda e, m0=m0: e.tensor_tensor(xcs3[:, m0:m0 + 5, :], t5, bc_(cbv(m0, 5)), ALU.add), reads=[bs["tsm"], self.bpk], writes=[bs["xc_s"]])
            P.op("dve", lambda e, m0=m0: e.tensor_copy(xcb[:, :, NPR:T], xcs3[:, m0:m0 + 5, :]), reads=[bs["xc_s"]], writes=[bxcb[a][2] for a in range(5)])
            for ml in range(5):
                m = m0 + ml
                kc0, nz = self.band_tiles(m)
                i = self.wnext % self.NSLOT
                self.wnext = (i + 1) % self.NSLOT
                wv = self.wsl[i][:, 0:768].rearrange("p (s n) -> p s n", n=128)
                bw = self.bws[i]
                P.dma("pool", self.wsl[i][:, 0:768], self.wgate[j, m], writes=[bw])
                for ti in range(3):
                    t0, n = TT[ti]
                    pa, bpa = self.ps()
                    px, bpx = self.ps()
                    for gi, (pt, bpt) in enumerate(((pa, bpa), (px, bpx))):
                        for q, (s, kc) in enumerate(nz):
                            P.op("pe", lambda e, pt=pt, gi=gi, s=s, kc=kc, wv=wv, t0=t0, n=n, q=q, m0=m0: e.matmul(pt[:, 0:n], wv[:, gi * 3 + s, :], xcb[:, kc - m0, t0:t0 + n], start=(q == 0), stop=(q == len(nz) - 1)),
                                 reads=[bw, bxcb[kc - m0][ti]], writes=[bpt])
                    if ti < 2:
                        k2 = cnt[0] % 2
                        cnt[0] += 1
                        ta, bta = tmp[0 + k2], btmp[0 + k2]
                        tb, btb = tmp[2 + k2], btmp[2 + k2]
                        tq, btq = tmp[4], btmp[4]
                        txc, btxc = tmp[5], btmp[5]
                        th, bth = tmp[5], btmp[5]
                        P.op("act", lambda e, pa=pa, ta=ta, m=m: e.activation(ta, pa[:, 0:512], AF.Sigmoid, bias=bga(m)), reads=[bpa, self.bpk], writes=[bta])
                        P.op("act", lambda e, px=px, tb=tb, m=m: e.activation(tb, px[:, 0:512], AF.Sigmoid, bias=bgx(m)), reads=[bpx, self.bpk], writes=[btb])
                        P.op("act", lambda e, ta=ta, m=m: e.activation(ta, ta, AF.Exp, scale=self.clam[:, m:m + 1]), reads=[bta, bs["clam"]], writes=[bta])
                        P.op("dve", lambda e, ta=ta, tq=tq: e.tensor_tensor(tq, ta, ta, ALU.mult), reads=[bta], writes=[btq])
                        P.op("dve", lambda e, tq=tq: e.tensor_scalar(tq, tq, -1.0, 1.0, ALU.mult, ALU.add), reads=[btq], writes=[btq])
                        P.op("act", lambda e, tq=tq: e.activation(tq, tq, AF.Sqrt), reads=[btq], writes=[btq])
                        rd = [bxp[ml][0], bxp[ml][1], bxph, self.bpk]
                        P.op("dve", lambda e, ml=ml, m=m, t0=t0, n=n, txc=txc: e.tensor_scalar(txc, xp[:, ml, t0:t0 + n], cw(0, m), cb(m), ALU.mult, ALU.add), reads=rd, writes=[btxc])
                        for tap in (1, 2, 3):
                            P.op("dve", lambda e, ml=ml, m=m, t0=t0, n=n, txc=txc, tap=tap: e.scalar_tensor_tensor(txc, xp[:, ml, t0 + tap:t0 + tap + n], cw(tap, m), txc, ALU.mult, ALU.add), reads=rd + [btxc], writes=[btxc])
                        P.op("dve", lambda e, tb=tb, tq=tq: e.tensor_tensor(tb, tb, tq, ALU.mult), reads=[btb, btq], writes=[btb])
                        P.op("dve", lambda e, tb=tb, txc=txc: e.tensor_tensor(tb, tb, txc, ALU.mult), reads=[btb, btxc], writes=[btb])
                        P.dma("sp", self.a_d[m, :, t0:t0 + n], ta, reads=[bta], writes=[self.ba_d[m][ti]])
                        P.dma("sp", self.b_d[m, :, t0:t0 + n], tb, reads=[btb], writes=[self.bb_d[m][ti]])
                        init = 0.0 if ti == 0 else self.hend[:, m:m + 1]
                        P.op("dve", lambda e, ta=ta, tb=tb, th=th, init=init: e.tensor_tensor_scan(th, ta, tb, init, ALU.mult, ALU.add),
                             reads=[bta, btb, bs["hend"]], writes=[bth])
                        P.op("act", lambda e, th=th, m=m: e.copy(self.hend[:, m:m + 1], th[:, 511:512]), reads=[bth], writes=[bs["hend"]])
                    else:
                        sa, sb_ = self.tsm[:, 0:NS], self.tsm[:, NS:2 * NS]
                        sq_ = self.tsm[:, 2 * NS:3 * NS]
                        bt = bs["tsm"]
                        P.op("act", lambda e, pa=pa, m=m: e.activation(sa, pa[:, 0:NS], AF.Sigmoid, bias=bga(m)), reads=[bpa, self.bpk], writes=[bt])
                        P.op("act", lambda e, px=px, m=m: e.activation(sb_, px[:, 0:NS], AF.Sigmoid, bias=bgx(m)), reads=[bpx, self.bpk], writes=[bt])
                        P.op("act", lambda e, m=m: e.activation(sa, sa, AF.Exp, scale=self.clam[:, m:m + 1]), reads=[bt, bs["clam"]], writes=[bt])
                        P.op("dve", lambda e: e.tensor_tensor(sq_, sa, sa, ALU.mult), reads=[bt], writes=[bt])
                        P.op("dve", lambda e: e.tensor_scalar(sq_, sq_, -1.0, 1.0, ALU.mult, ALU.add), reads=[bt], writes=[bt])
                        P.op("act", lambda e: e.activation(sq_, sq_, AF.Sqrt), reads=[bt], writes=[bt])
                        P.op("dve", lambda e: e.tensor_tensor(sb_, sb_, sq_, ALU.mult), reads=[bt], writes=[bt])
                        P.op("dve", lambda e, m=m: e.tensor_tensor(sb_, sb_, xcs3[:, m, :], ALU.mult), reads=[bt, bs["xc_s"]], writes=[bt])
                        P.op("dve", lambda e, m=m: e.tensor_tensor(sa, sa, h0s3[:, m, :], ALU.mult), reads=[bt, bs["h0s"]], writes=[bt])
                        P.op("dve", lambda e, m=m: e.tensor_tensor(hs3[:, m, :], sa, sb_, ALU.add), reads=[bt], writes=[bs["hs_all"]])
        P.dma("sp", self.agin2, self.hend, reads=[bs["hend"]], writes=[self.bag[2]])
        P.coll("AllGather", ALU.bypass, [self.agin2], [self.agout2], [[0, 1], [2, 3], [4, 5], [6, 7]], reads=[self.bag[2]], writes=[self.bag[3]])
        P.dma("sp", self.hinit, self.agout2[0:128, :], reads=[self.bag[3]], writes=[bs["hinit"]])
        P.op("dve", lambda e: e.tensor_scalar(self.hcur, self.hinit, self.pm[:, 0:1], None, ALU.mult), reads=[bs["hinit"], self.bpm], writes=[bs["hcur"]])
        P.dma("sp", self.o_lru_s[j], hs3, reads=[bs["hs_all"]])
        P.dma("sp", self.o_conv_s[j, :, :, 0:2, :], cb4[:, :, 1:3, :], reads=[self.bcbuf])
        P.dma("sp", self.o_conv_s[j, :, :, 2, :], xs3, reads=[bs["xs_all"]])

        ybuf = xcb
        by = bxcb
        for g in range(4):
            m0 = g * 5
            for (ma, nb) in ((0, 2), (2, 2), (4, 1)):
                wv, bw = self.wload(self.w_in_c[j, :, (m0 + ma) * 128:(m0 + ma + nb) * 128], NCH, nb * 128)
                for jj in range(nb):
                    ml = ma + jj
                    m = m0 + ml
                    for ti in range(3):
                        t0, n = TT[ti]
                        ps_, bps_ = self.ps()
                        for kc in range(NCH):
                            P.op("pe", lambda e, ps_=ps_, jj=jj, kc=kc, wv=wv, t0=t0, n=n: e.matmul(ps_[:, 0:n], wv[:, kc, jj * 128:(jj + 1) * 128], self.hT[:, kc, t0:t0 + n], start=(kc == 0), stop=(kc == NCH - 1)),
                                 reads=[bw, self.bh[kc][ti]], writes=[bps_])
                        if ti < 2:
                            k2 = cnt[0] % 2
                            cnt[0] += 1
                            ta, bta = tmp[0 + k2], btmp[0 + k2]
                            tb, btb = tmp[2 + k2], btmp[2 + k2]
                            tg, btg = tmp[4], btmp[4]
                            th, bth = tmp[5], btmp[5]
                            P.dma("sp", ta, self.a_d[m, :, t0:t0 + n], reads=[self.ba_d[m][ti]], writes=[bta])
                            P.dma("sp", tb, self.b_d[m, :, t0:t0 + n], reads=[self.bb_d[m][ti]], writes=[btb])
                            P.op("act", lambda e, ps_=ps_, tg=tg: e.activation(tg, ps_[:, 0:512], AF.Gelu_apprx_tanh), reads=[bps_], writes=[btg])
                            P.op("dve", lambda e, ta=ta, tb=tb, th=th, m=m: e.tensor_tensor_scan(th, ta, tb, self.hcur[:, m:m + 1], ALU.mult, ALU.add),
                                 reads=[bta, btb, bs["hcur"]], writes=[bth])
                            P.op("act", lambda e, th=th, m=m: e.copy(self.hcur[:, m:m + 1], th[:, 511:512]), reads=[bth], writes=[bs["hcur"]])
                            P.op("dve", lambda e, tg=tg, th=th, ml=ml, t0=t0, n=n: e.tensor_tensor(ybuf[:, ml, t0:t0 + n], tg, th, ALU.mult), reads=[btg, bth], writes=[by[ml][ti]])
                        else:
                            sg = self.tsm[:, 0:NS]
                            P.op("act", lambda e, ps_=ps_: e.activation(sg, ps_[:, 0:NS], AF.Gelu_apprx_tanh), reads=[bps_], writes=[bs["tsm"]])
                            P.op("dve", lambda e, ml=ml, m=m: e.tensor_tensor(ybuf[:, ml, NPR:T], sg, hs3[:, m, :], ALU.mult), reads=[bs["tsm"], bs["hs_all"]], writes=[by[ml][2]])
            for ocp in range(8):
                wv, bw = self.wload(self.w_out_c[j, m0 * 128:(m0 + 5) * 128, ocp * 256:(ocp + 1) * 256], 5, 256)
                for jj in range(2):
                    oc = ocp * 2 + jj
                    for ti in range(3):
                        t0, n = TT[ti]
                        ps_, bps_ = self.ps()
                        for kc in range(5):
                            P.op("pe", lambda e, ps_=ps_, jj=jj, kc=kc, wv=wv, t0=t0, n=n: e.matmul(ps_[:, 0:n], wv[:, kc, jj * 128:(jj + 1) * 128], ybuf[:, kc, t0:t0 + n], start=(kc == 0), stop=(kc == 4)),
                                 reads=[bw, by[kc][ti]], writes=[bps_])
                        self.resid_add(ps_, bps_, oc, ti, G, bG)
        P.dma("sp", self.o_lru_p[j], self.hcur, reads=[bs["hcur"]])
        self.alias(self.arena_bufs(), mine)

    def even_init(self):
        P = self.P
        nj = 2 if self.cfg.get("layers", DEPTH) > 2 else 1
        self.nj_even = nj
        self.w_in_ab = P.dram_in("w_in_ab", [nj, D, 5120])
        self.w_glu = P.dram_in("s5_w_glu", [nj, 1024, 1024])
        self.w_out_ab = P.dram_in("w_out_ab", [nj, D, D])
        self.s5bc = P.dram_in("s5bc", [nj, 128, 4, 32, 16])
        self.st_s5 = P.dram_in("st_s5", [nj, 128, 2, 32, NS])
        self.st_hg = P.dram_in("st_hg", [nj, 8, 128, NS, 128])
        if not hasattr(self, "pm"):
            self.pmd = P.dram_in("pm", [128, 1])
            self.pm = P.sbuf([128, 1], F32, "pm_sb"); self.bpm = Buf()
            P.dma("sp", self.pm[:], self.pmd, writes=[self.bpm])
        self.o_s5p = P.dram_out("o_s5p", [nj, 128, 2, 32])
        self.o_s5s = P.dram_out("o_s5s", [nj, 128, 2, 32, NS])
        self.o_hgp = P.dram_out("o_hgp", [nj, 8, 128, 128])
        self.o_hgs = P.dram_out("o_hgs", [nj, 8, 128, NS, 128])
        self.s5T = P.dram_tmp("s5T", [nj, 64, 128, 128])
        self.s5R = P.dram_tmp("s5R", [nj, 64, 2, 128, 128])
        self.s5P = P.dram_tmp("s5P", [nj, 32, 128, 2, 128])
        self.bs5m = [Buf() for _ in range(nj)]
        self.Ud = P.dram_tmp("Ud", [1024, 1024]); self.Uds = P.dram_tmp("Uds", [1024, NS])
        self.Yd = P.dram_tmp("Yd", [1024, 1024], BF16); self.Yds = P.dram_tmp("Yds", [1024, NS], BF16)
        self.bUd = [Buf() for _ in range(8)]; self.bYd = [Buf() for _ in range(64)]
        self.agin3 = P.dram_tmp("agin3", [128, 1088]); self.agout3 = P.dram_tmp("agout3", [256, 1088])
        self.bag3 = [Buf(), Buf()]
        self.s5co = [P.sbuf([128, 8, 32], F32, f"s5co{j}") for j in range(nj)]
        self.bs5co = [Buf() for _ in range(nj)]
        self.lbt = P.sbuf([128, 2, 8], F32, "lbt"); self.blbt = Buf()
        self.omlb = P.sbuf([128, 2, 8], F32, "omlb")
        self.ident_bf = P.sbuf([128, 128], BF16, "ident_bf"); self.bidb = Buf()
        self.ones_f = P.sbuf([128, 128], F32, "ones_f"); self.bonesf = Buf()
        P.op("dve", lambda e: e.tensor_copy(self.ident_bf[:], self.fld("ident")), reads=[self.bpk], writes=[self.bidb])
        P.op("dve", lambda e: e.memset(self.ones_f[:], 1.0), writes=[self.bonesf])
        for j in range(nj):
            if self.cfg.get("eprebuild", True):
                self.s5_prebuild(j)
        o, _ = self.lay.fields["hglog"]
        P.op("dve", lambda e: e.memset(self.lbt[:, 0, :], 0.0), writes=[self.blbt])
        P.op("dve", lambda e: e.tensor_tensor(self.lbt[:, 1, :], self.pk[:, o + 8:o + 16], self.pk[:, o:o + 8], ALU.subtract), reads=[self.bpk, self.blbt], writes=[self.blbt])
        P.op("act", lambda e: e.activation(self.lbt[:, 1, :], self.lbt[:, 1, :], AF.Sigmoid), reads=[self.blbt], writes=[self.blbt])
        P.op("dve", lambda e: e.tensor_scalar(self.omlb[:], self.lbt[:], -1.0, 1.0, ALU.mult, ALU.add), reads=[self.blbt], writes=[self.blbt])

    def s5_prebuild(self, j):
        P = self.P
        ar = self.arena
        B = {}
        off = [0]

        def tl(name, n):
            v = ar[:, off[0]:off[0] + n]
            off[0] += n
            B[name] = Buf()
            return v
        step = tl("step", 32); lrd = tl("lrd", 32); th = tl("th", 32)
        Er = tl("Er", 256); Ei = tl("Ei", 256); Fr = tl("Fr", 256); Fi = tl("Fi", 256)
        t1 = tl("t1", 32); t2 = tl("t2", 32); t3 = tl("t3", 32); t4 = tl("t4", 32)
        ti_ = ar[:, off[0]:off[0] + 32].bitcast(I32); off[0] += 32; B["ti"] = Buf()
        zr = tl("zr", 32); zi = tl("zi", 32)
        bc = tl("bc", 2048)
        Bzr = tl("Bzr", 512); Bzi = tl("Bzi", 512)
        Pr = tl("Pr", 1024); NPi = tl("NPi", 1024); Rr = tl("Rr", 1024); Ri = tl("Ri", 1024)
        ta = tl("ta", 1024); tb = tl("tb", 1024)
        stg = [tl(f"stg{i}", 128) for i in range(4)]
        assert off[0] <= 13408, off[0]
        self.alias(list(B.values()), self.arena_bufs())
        o_lr, _ = self.lay.fields["s5lamr"]; o_li, _ = self.lay.fields["s5lami"]; o_ls, _ = self.lay.fields["s5lstep"]
        lamr = self.pk[:, o_lr + j * 32:o_lr + j * 32 + 32]
        lami = self.pk[:, o_li + j * 32:o_li + j * 32 + 32]
        lstep = self.pk[:, o_ls + j * 32:o_ls + j * 32 + 32]
        E3 = lambda v: v.rearrange("p (g m) -> p g m", m=8)
        P.dma("sp", bc.rearrange("p (a g k) -> p a g k", a=4, k=16), self.s5bc[j], writes=[B["bc"]])
        P.op("act", lambda e: e.activation(step, lstep, AF.Exp), reads=[self.bpk], writes=[B["step"]])
        P.op("dve", lambda e: e.tensor_tensor(lrd, lamr, step, ALU.mult), reads=[self.bpk, B["step"]], writes=[B["lrd"]])
        P.op("dve", lambda e: e.tensor_tensor(th, lami, step, ALU.mult), reads=[self.bpk, B["step"]], writes=[B["th"]])
        TWO_PI = 6.283185
        for m in range(1, 9):
            P.op("act", lambda e, m=m: e.activation(t1, lrd, AF.Exp, scale=float(m)), reads=[B["lrd"]], writes=[B["t1"]])
            P.op("act", lambda e, m=m: e.activation(t2, lrd, AF.Exp, scale=-float(m)), reads=[B["lrd"]], writes=[B["t2"]])
            for which in (0, 1):
                sh = 0.0 if which == 0 else 0.25
                P.op("dve", lambda e, m=m, sh=sh: e.tensor_scalar(t3, th, m / (2 * np.pi), sh, ALU.mult, ALU.add), reads=[B["th"]], writes=[B["t3"]])
                P.op("dve", lambda e: e.tensor_copy(ti_, t3), reads=[B["t3"]], writes=[B["ti"]])
                P.op("dve", lambda e: e.tensor_copy(t4, ti_), reads=[B["ti"]], writes=[B["t4"]])
                P.op("dve", lambda e: e.tensor_tensor(t3, t3, t4, ALU.subtract), reads=[B["t3"], B["t4"]], writes=[B["t3"]])
                P.op("dve", lambda e: e.tensor_scalar(t4, t3, 0.5, None, ALU.is_gt), reads=[B["t3"]], writes=[B["t4"]])
                P.op("dve", lambda e: e.tensor_tensor(t3, t3, t4, ALU.subtract), reads=[B["t3"], B["t4"]], writes=[B["t3"]])
                P.op("dve", lambda e: e.tensor_scalar(t4, t3, -0.5, None, ALU.is_lt), reads=[B["t3"]], writes=[B["t4"]])
                P.op("dve", lambda e: e.tensor_tensor(t3, t3, t4, ALU.add), reads=[B["t3"], B["t4"]], writes=[B["t3"]])
                P.op("act", lambda e: e.activation(t3, t3, AF.Sin, scale=TWO_PI), reads=[B["t3"]], writes=[B["t3"]])
                if which == 0:
                    P.op("dve", lambda e, m=m: e.tensor_tensor(E3(Ei)[:, :, m - 1], t1, t3, ALU.mult), reads=[B["t1"], B["t3"]], writes=[B["Ei"]])
                    P.op("dve", lambda e, m=m: e.scalar_tensor_tensor(E3(Fi)[:, :, m - 1], t2, -1.0, t3, ALU.mult, ALU.mult), reads=[B["t2"], B["t3"]], writes=[B["Fi"]])
                else:
                    P.op("dve", lambda e, m=m: e.tensor_tensor(E3(Er)[:, :, m - 1], t1, t3, ALU.mult), reads=[B["t1"], B["t3"]], writes=[B["Er"]])
                    P.op("dve", lambda e, m=m: e.tensor_tensor(E3(Fr)[:, :, m - 1], t2, t3, ALU.mult), reads=[B["t2"], B["t3"]], writes=[B["Fr"]])
        co = self.s5co[j]
        bco = self.bs5co[j]
        for (dst, src, m, neg) in ((0, Er, 8, False), (1, Er, 8, False), (2, Ei, 8, False), (3, Ei, 8, True),
                                   (4, Er, 1, False), (5, Er, 1, False), (6, Ei, 1, False), (7, Ei, 1, True)):
            P.op("dve", lambda e, dst=dst, src=src, m=m, neg=neg: e.tensor_scalar(co[:, dst, :], E3(src)[:, :, m - 1], -1.0 if neg else 1.0, None, ALU.mult),
                 reads=[B["Er"], B["Ei"]], writes=[bco])
        a1r, a1i = E3(Er)[:, :, 0], E3(Ei)[:, :, 0]
        rdE = [B["Er"], B["Ei"], self.bpk]
        P.op("dve", lambda e: e.tensor_tensor(t1, lamr, lamr, ALU.mult), reads=[self.bpk], writes=[B["t1"]])
        P.op("dve", lambda e: e.tensor_tensor(t2, lami, lami, ALU.mult), reads=[self.bpk], writes=[B["t2"]])
        P.op("dve", lambda e: e.tensor_tensor(t1, t1, t2, ALU.add), reads=[B["t1"], B["t2"]], writes=[B["t1"]])
        P.op("dve", lambda e: e.reciprocal(t1, t1), reads=[B["t1"]], writes=[B["t1"]])
        P.op("dve", lambda e: e.tensor_scalar(t2, a1r, -1.0, None, ALU.add), reads=rdE, writes=[B["t2"]])
        P.op("dve", lambda e: e.tensor_tensor(t3, t2, lamr, ALU.mult), reads=[B["t2"], self.bpk], writes=[B["t3"]])
        P.op("dve", lambda e: e.tensor_tensor(t4, a1i, lami, ALU.mult), reads=rdE, writes=[B["t4"]])
        P.op("dve", lambda e: e.tensor_tensor(t3, t3, t4, ALU.add), reads=[B["t3"], B["t4"]], writes=[B["t3"]])
        P.op("dve", lambda e: e.tensor_tensor(zr, t3, t1, ALU.mult), reads=[B["t3"], B["t1"]], writes=[B["zr"]])
        P.op("dve", lambda e: e.tensor_tensor(t3, a1i, lamr, ALU.mult), reads=rdE, writes=[B["t3"]])
        P.op("dve", lambda e: e.tensor_tensor(t4, t2, lami, ALU.mult), reads=[B["t2"], self.bpk], writes=[B["t4"]])
        P.op("dve", lambda e: e.tensor_tensor(t3, t3, t4, ALU.subtract), reads=[B["t3"], B["t4"]], writes=[B["t3"]])
        P.op("dve", lambda e: e.tensor_tensor(zi, t3, t1, ALU.mult), reads=[B["t3"], B["t1"]], writes=[B["zi"]])
        bc4 = bc.rearrange("p (a g k) -> p a g k", a=4, k=16)
        Cr_, Ci_, Br_, Bi_ = bc4[:, 0], bc4[:, 1], bc4[:, 2], bc4[:, 3]
        zb = lambda z: z.unsqueeze(2).broadcast_to([128, 32, 16])
        G3 = lambda v: v.rearrange("p (g k) -> p g k", k=16)
        ta3 = ta[:, 0:512].rearrange("p (g k) -> p g k", k=16)
        P.op("dve", lambda e: e.tensor_tensor(G3(Bzr), Br_, zb(zr), ALU.mult), reads=[B["bc"], B["zr"]], writes=[B["Bzr"]])
        P.op("dve", lambda e: e.tensor_tensor(ta3, Bi_, zb(zi), ALU.mult), reads=[B["bc"], B["zi"]], writes=[B["ta"]])
        P.op("dve", lambda e: e.tensor_tensor(G3(Bzr), G3(Bzr), ta3, ALU.subtract), reads=[B["Bzr"], B["ta"]], writes=[B["Bzr"]])
        P.op("dve", lambda e: e.tensor_tensor(G3(Bzi), Bi_, zb(zr), ALU.mult), reads=[B["bc"], B["zr"]], writes=[B["Bzi"]])
        P.op("dve", lambda e: e.tensor_tensor(ta3, Br_, zb(zi), ALU.mult), reads=[B["bc"], B["zi"]], writes=[B["ta"]])
        P.op("dve", lambda e: e.tensor_tensor(G3(Bzi), G3(Bzi), ta3, ALU.add), reads=[B["Bzi"], B["ta"]], writes=[B["Bzi"]])
        o_m2, _ = self.lay.fields["tmask2"]
        tmask2 = self.pk[:, o_m2:o_m2 + 128]
        ident = self.fld("ident")
        M4 = lambda v: v.rearrange("p (g c k) -> p g c k", c=8, k=16)
        M3 = lambda v: v.rearrange("p (g n) -> p g n", n=128)
        bm = self.bs5m[j]
        sc_ = 0
        for r in range(4):
            g0 = r * 8
            bk = lambda X, g0=g0: X[:, g0:g0 + 8, :].unsqueeze(2).broadcast_to([128, 8, 8, 16])
            bm_ = lambda Tb, g0=g0: E3(Tb)[:, g0:g0 + 8, :].unsqueeze(3).broadcast_to([128, 8, 8, 16])
            rdT = [B["Er"], B["Ei"], B["Fr"], B["Fi"], B["bc"], B["Bzr"], B["Bzi"]]
            ta4, tb4 = M4(ta), M4(tb)
            P.op("dve", lambda e, bk=bk, bm_=bm_: e.tensor_tensor(ta4, bk(Cr_), bm_(Er), ALU.mult), reads=rdT, writes=[B["ta"]])
            P.op("dve", lambda e, bk=bk, bm_=bm_: e.tensor_tensor(tb4, bk(Ci_), bm_(Ei), ALU.mult), reads=rdT, writes=[B["tb"]])
            P.op("dve", lambda e: e.tensor_tensor(M4(Pr), ta4, tb4, ALU.subtract), reads=[B["ta"], B["tb"]], writes=[B["Pr"]])
            P.op("dve", lambda e, bk=bk, bm_=bm_: e.tensor_tensor(ta4, bk(Cr_), bm_(Ei), ALU.mult), reads=rdT, writes=[B["ta"]])
            P.op("dve", lambda e, bk=bk, bm_=bm_: e.tensor_tensor(tb4, bk(Ci_), bm_(Er), ALU.mult), reads=rdT, writes=[B["tb"]])
            P.op("dve", lambda e: e.scalar_tensor_tensor(M4(NPi), ta4, -1.0, tb4, ALU.mult, ALU.subtract), reads=[B["ta"], B["tb"]], writes=[B["NPi"]])
            P.op("dve", lambda e, bk=bk, bm_=bm_: e.tensor_tensor(ta4, bk(G3(Bzr)), bm_(Fr), ALU.mult), reads=rdT, writes=[B["ta"]])
            P.op("dve", lambda e, bk=bk, bm_=bm_: e.tensor_tensor(tb4, bk(G3(Bzi)), bm_(Fi), ALU.mult), reads=rdT, writes=[B["tb"]])
            P.op("dve", lambda e: e.tensor_tensor(M4(Rr), ta4, tb4, ALU.subtract), reads=[B["ta"], B["tb"]], writes=[B["Rr"]])
            P.op("dve", lambda e, bk=bk, bm_=bm_: e.tensor_tensor(ta4, bk(G3(Bzi)), bm_(Fr), ALU.mult), reads=rdT, writes=[B["ta"]])
            P.op("dve", lambda e, bk=bk, bm_=bm_: e.tensor_tensor(tb4, bk(G3(Bzr)), bm_(Fi), ALU.mult), reads=rdT, writes=[B["tb"]])
            P.op("dve", lambda e: e.tensor_tensor(M4(Ri), ta4, tb4, ALU.add), reads=[B["ta"], B["tb"]], writes=[B["Ri"]])
            P.dma("sp", self.s5P[j, g0:g0 + 8, :, 0, :].rearrange("g p n -> p g n"), M3(Pr), reads=[B["Pr"]], writes=[bm])
            P.dma("sp", self.s5P[j, g0:g0 + 8, :, 1, :].rearrange("g p n -> p g n"), M3(NPi), reads=[B["NPi"]], writes=[bm])
            for gl in range(8):
                gp = g0 + gl
                for gh in range(2):
                    g = gh * 32 + gp
                    hs = slice(gh * 64, gh * 64 + 64)
                    pp, bp = self.ps()
                    P.op("pe", lambda e, pp=pp, gl=gl, hs=hs: e.matmul(pp[:, 0:128], M3(Rr)[hs, gl, :], M3(Pr)[hs, gl, :], start=True, stop=False), reads=[B["Rr"], B["Pr"]], writes=[bp])
                    P.op("pe", lambda e, pp=pp, gl=gl, hs=hs: e.matmul(pp[:, 0:128], M3(Ri)[hs, gl, :], M3(NPi)[hs, gl, :], start=False, stop=True), reads=[B["Ri"], B["NPi"]], writes=[bp])
                    st = stg[sc_ % 4]; bst = B[f"stg{sc_ % 4}"]; sc_ += 1
                    P.op("dve", lambda e, pp=pp, st=st: e.tensor_tensor(st, pp[:, 0:128], tmask2, ALU.mult), reads=[bp, self.bpk], writes=[bst])
                    P.dma("sp", self.s5T[j, g], st, reads=[bst], writes=[bm])
                    for ri, Rm, bR in ((0, Rr, B["Rr"]), (1, Ri, B["Ri"])):
                        pp, bp = self.ps()
                        P.op("pe", lambda e, pp=pp, gl=gl, Rm=Rm: e.transpose(pp[:, 0:128], M3(Rm)[:, gl, :], ident), reads=[bR, self.bpk], writes=[bp])
                        st = stg[sc_ % 4]; bst = B[f"stg{sc_ % 4}"]; sc_ += 1
                        P.op("dve", lambda e, pp=pp, st=st, gh=gh: e.tensor_copy(st[:, gh * 64:gh * 64 + 64], pp[:, gh * 64:gh * 64 + 64]), reads=[bp], writes=[bst])
                        P.op("dve", lambda e, st=st, gh=gh: e.memset(st[:, (1 - gh) * 64:(1 - gh) * 64 + 64], 0.0), reads=[bst], writes=[bst])
                        P.dma("sp", self.s5R[j, g, ri], st, reads=[bst], writes=[bm])
        self.alias(self.arena_bufs(), list(B.values()))

    def even_mixer(self, l):
        P = self.P
        j = l // 2
        self._emix_j = j
        ar = self.arena
        G, bG = self.Gmod[1], self.bG[1]
        co = self.s5co[j]; bco = self.bs5co[j]
        bm = self.bs5m[j]
        PAIRS = [[0, 1], [2, 3], [4, 5], [6, 7]]
        B = {}
        off = [0]

        def tl(name, n):
            v = ar[:, off[0]:off[0] + n]
            off[0] += n
            B[name] = Buf()
            return v
        V2 = tl("V2", 8192)
        V4 = V2.rearrange("p (j r g) -> p j r g", r=2, g=32)
        vs = tl("vs", 1024)
        vs4 = vs.rearrange("p (r g b) -> p r g b", r=2, b=NS)
        Rb = [tl(f"Rb{i}", 512) for i in range(2)]
        ub = [tl(f"ub{i}", 288) for i in range(2)]
        sc = tl("sc", 64); tt_ = tl("tt", 64); p1 = tl("p1", 64); p2 = tl("p2", 64)
        ust_off = off[0]
        ust = [tl(f"ust{i}", 512) for i in range(2)]
        assert off[0] <= 12096, off[0]
        stg = ar[:, 12096:12096 + 1088]
        bstg = Buf()
        hgS = stg[:, 64:1088]
        B["hgS"] = bstg
        self.alias(list(B.values()), self.arena_bufs())
        ident = self.fld("ident")

        if self.cfg.get("estop", 99) <= 0:
            self.alias(self.arena_bufs(), list(B.values()))
            return
        self.hgrn_pass1(j, hgS, B["hgS"], B)

        if self.cfg.get("estop", 99) <= 1:
            self.alias(self.arena_bufs(), list(B.values()))
            return
        hperm = lambda kc: self.hT[:, kc, 0:NPR].rearrange("p (j c) -> p c j", c=8)
        P.op("dve", lambda e: e.memset(ub[0], 0.0), writes=[B["ub0"]])
        P.op("dve", lambda e: e.memset(ub[1], 0.0), writes=[B["ub1"]])
        P.dma("sp", vs4, self.st_s5[j], writes=[B["vs"]])
        k_ = 0
        for pr in range(4):
            wv, bw = self.wload(self.w_in_ab[j, :, pr * 256:(pr + 1) * 256], NCH, 256)
            for jj in range(2):
                uc = pr * 2 + jj
                for ti in range(3):
                    pp, bp = self.ps()
                    if ti < 2:
                        for c4 in range(4):
                            for kc in range(NCH):
                                P.op("pe", lambda e, pp=pp, jj=jj, kc=kc, wv=wv, ti=ti, c4=c4: e.matmul(pp[:, c4 * 128:(c4 + 1) * 128], wv[:, kc, jj * 128:(jj + 1) * 128], hperm(kc)[:, 4 * ti + c4, :], start=(kc == 0), stop=(kc == NCH - 1)),
                                     reads=[bw, self.bh[kc][0], self.bh[kc][1]], writes=[bp])
                        st = ust[k_ % 2]; bst = B[f"ust{k_ % 2}"]; k_ += 1
                        P.op("act", lambda e, pp=pp, st=st: e.copy(st, pp[:, 0:512]), reads=[bp], writes=[bst])
                        P.dma("sp", self.Ud[uc * 128:(uc + 1) * 128, ti * 512:(ti + 1) * 512], st, reads=[bst], writes=[self.bUd[uc]])
                    else:
                        for kc in range(NCH):
                            P.op("pe", lambda e, pp=pp, jj=jj, kc=kc, wv=wv: e.matmul(pp[:, 0:NS], wv[:, kc, jj * 128:(jj + 1) * 128], self.hT[:, kc, NPR:T], start=(kc == 0), stop=(kc == NCH - 1)),
                                 reads=[bw, self.bh[kc][2]], writes=[bp])
                        st = ust[k_ % 2]; bst = B[f"ust{k_ % 2}"]; k_ += 1
                        P.op("act", lambda e, pp=pp, st=st: e.copy(st[:, 0:NS], pp[:, 0:NS]), reads=[bp], writes=[bst])
                        P.dma("sp", self.Uds[uc * 128:(uc + 1) * 128, :], st[:, 0:NS], reads=[bst], writes=[self.bUd[uc]])

        if self.cfg.get("estop", 99) <= 2:
            self.alias(self.arena_bufs(), list(B.values()))
            return
        def load_u(gp, k):
            u3 = ub[k].rearrange("p (h n) -> p h n", h=2)
            for gh in range(2):
                g = gh * 32 + gp
                for c8 in range(8):
                    P.dma("sp", u3[c8 * 16:(c8 + 1) * 16, gh, 0:128], self.Ud[g * 16:(g + 1) * 16, c8 * 128:(c8 + 1) * 128], reads=[self.bUd[g // 8]], writes=[B[f"ub{k}"]])
                P.dma("sp", u3[0:16, gh, 128:144], self.Uds[g * 16:(g + 1) * 16, :], reads=[self.bUd[g // 8]], writes=[B[f"ub{k}"]])
            return u3

        for gp in range(32):
            k = gp % 2
            R4 = Rb[k].rearrange("p (h r n) -> p h r n", h=2, r=2)
            for gh in range(2):
                g = gh * 32 + gp
                P.dma("sp", R4[:, gh], self.s5R[j, g].rearrange("r p n -> p r n"), reads=[bm], writes=[B[f"Rb{k}"]])
            u3 = load_u(gp, k)
            for ri in range(2):
                pp, bp = self.ps()
                for gh in range(2):
                    P.op("pe", lambda e, pp=pp, R4=R4, u3=u3, gh=gh, ri=ri: e.matmul(pp[:, 0:144], R4[:, gh, ri, :], u3[:, gh, :], start=(gh == 0), stop=(gh == 1)),
                         reads=[B[f"Rb{k}"], B[f"ub{k}"]], writes=[bp])
                P.op("dve", lambda e, pp=pp, ri=ri, gp=gp: e.tensor_copy(V4[:, :, ri, gp], pp[:, 0:128]), reads=[bp], writes=[B["V2"]])
                P.op("dve", lambda e, pp=pp, ri=ri, gp=gp: e.tensor_tensor(vs4[:, ri, gp, :], pp[:, 128:144], vs4[:, ri, gp, :], ALU.add), reads=[bp, B["vs"]], writes=[B["vs"]])

        if self.cfg.get("estop", 99) <= 3:
            self.alias(self.arena_bufs(), list(B.values()))
            return
        A8 = co[:, 0:2, :]
        a8i, na8i = co[:, 2, :], co[:, 3, :]
        sc3 = sc.rearrange("p (r g) -> p r g", r=2)
        t3 = tt_.rearrange("p (r g) -> p r g", r=2)
        p13 = p1.rearrange("p (r g) -> p r g", r=2)
        p23 = p2.rearrange("p (r g) -> p r g", r=2)

        def cstep(t_ap, A, ai, nai, out3, eng="dve"):
            P.op(eng, lambda e: e.tensor_tensor(p13, A, t_ap, ALU.mult), reads=[B["tt"], B["V2"], B["vs"], bco], writes=[B["p1"]])
            P.op(eng, lambda e: e.tensor_tensor(p23[:, 0, :], t_ap[:, 1, :], nai, ALU.mult), reads=[B["tt"], B["V2"], B["vs"], bco], writes=[B["p2"]])
            P.op(eng, lambda e: e.tensor_tensor(p23[:, 1, :], t_ap[:, 0, :], ai, ALU.mult), reads=[B["tt"], B["V2"], B["vs"], bco], writes=[B["p2"]])
            P.op(eng, lambda e: e.tensor_tensor(out3, p13, p23, ALU.add), reads=[B["p1"], B["p2"]], writes=[B["sc"]])

        P.op("dve", lambda e: e.memset(sc, 0.0), writes=[B["sc"]])
        for jj in range(128):
            P.op("dve", lambda e, jj=jj: e.tensor_tensor(t3, sc3, V4[:, jj], ALU.add), reads=[B["sc"], B["V2"]], writes=[B["tt"]])
            cstep(t3, A8, a8i, na8i, sc3)
        P.op("dve", lambda e: e.tensor_copy(stg[:, 0:64], sc), reads=[B["sc"]], writes=[bstg])
        P.dma("sp", self.agin3, stg, reads=[bstg], writes=[self.bag3[0]])
        P.coll("AllGather", ALU.bypass, [self.agin3], [self.agout3], PAIRS, reads=[self.bag3[0]], writes=[self.bag3[1]])
        P.dma("sp", stg, self.agout3[0:128, :], reads=[self.bag3[1]], writes=[bstg])
        P.op("dve", lambda e: e.tensor_scalar(stg, stg, self.pm[:, 0:1], None, ALU.mult), reads=[bstg, self.bpm], writes=[bstg])
        hn4 = ar[:, ust_off:ust_off + 1024].rearrange("p (r g b) -> p r g b", r=2, b=NS)
        tq = Rb[0].rearrange("p (g b) -> p g b", b=NS)
        bcs = lambda v: v.unsqueeze(2).broadcast_to([128, 32, NS])
        A1b = co[:, 4:6, :].unsqueeze(3).broadcast_to([128, 2, 32, NS])
        bhn = [B["ust0"], B["ust1"]]
        P.op("dve", lambda e: e.tensor_tensor(hn4, vs4, A1b, ALU.mult), reads=[B["vs"], bco], writes=bhn)
        P.op("dve", lambda e: e.tensor_tensor(tq, vs4[:, 1], bcs(co[:, 7, :]), ALU.mult), reads=[B["vs"], bco], writes=[B["Rb0"]])
        P.op("dve", lambda e: e.tensor_tensor(hn4[:, 0], hn4[:, 0], tq, ALU.add), reads=bhn + [B["Rb0"]], writes=bhn)
        P.op("dve", lambda e: e.tensor_tensor(tq, vs4[:, 0], bcs(co[:, 6, :]), ALU.mult), reads=[B["vs"], bco], writes=[B["Rb0"]])
        P.op("dve", lambda e: e.tensor_tensor(hn4[:, 1], hn4[:, 1], tq, ALU.add), reads=bhn + [B["Rb0"]], writes=bhn)
        P.dma("sp", self.o_s5s[j], hn4, reads=bhn)
        P.op("dve", lambda e: e.tensor_copy(sc, stg[:, 0:64]), reads=[bstg], writes=[B["sc"]])
        for jj in range(128):
            P.op("dve", lambda e, jj=jj: e.tensor_tensor(V4[:, jj], sc3, V4[:, jj], ALU.add), reads=[B["sc"], B["V2"]], writes=[B["V2"]])
            cstep(V4[:, jj], A8, a8i, na8i, sc3)
        P.dma("sp", self.o_s5p[j], sc3, reads=[B["sc"]])
        self.hg_init = stg[:, 64:1088]
        self.bhg_init = bstg

        if self.cfg.get("estop", 99) <= 4:
            self.alias(self.arena_bufs(), list(B.values()))
            return
        o_d2, _ = self.lay.fields["s5d2"]
        Tb = Rb
        for gp in range(32):
            k = gp % 2
            T4 = Tb[k].rearrange("p (h r n) -> p h r n", h=2, r=2)
            for gh in range(2):
                g = gh * 32 + gp
                P.dma("sp", T4[:, gh, 0, :], self.s5T[j, g], reads=[bm], writes=[B[f"Rb{k}"]])
            P.dma("sp", T4[:, :, 1, :], self.s5P[j, gp].rearrange("p r n -> p r n"), reads=[bm], writes=[B[f"Rb{k}"]])
            u3 = load_u(gp, k)
            for gh in range(2):
                g = gh * 32 + gp
                hs = slice(gh * 64, gh * 64 + 64)
                pp, bp = self.ps()
                rd = [B[f"Rb{k}"], B[f"ub{k}"], B["V2"], B["vs"]]
                P.op("pe", lambda e, pp=pp, T4=T4, u3=u3, gh=gh: e.matmul(pp[:, 0:128], T4[:, gh, 0, :], u3[:, gh, 0:128], start=True, stop=False), reads=rd, writes=[bp])
                P.op("pe", lambda e, pp=pp, T4=T4, hs=hs, gp=gp: e.matmul(pp[:, 0:128], T4[hs, 0, 1, :], V4[hs, :, 0, gp], start=False, stop=False), reads=rd, writes=[bp])
                P.op("pe", lambda e, pp=pp, T4=T4, hs=hs, gp=gp: e.matmul(pp[:, 0:128], T4[hs, 1, 1, :], V4[hs, :, 1, gp], start=False, stop=True), reads=rd, writes=[bp])
                P.op("pe", lambda e, pp=pp, T4=T4, u3=u3, gh=gh: e.matmul(pp[:, 128:144], T4[:, gh, 0, :], u3[:, gh, 128:144], start=True, stop=False), reads=rd, writes=[bp])
                P.op("pe", lambda e, pp=pp, T4=T4, hs=hs, gp=gp: e.matmul(pp[:, 128:144], T4[hs, 0, 1, :], vs4[hs, 0, gp, :], start=False, stop=False), reads=rd, writes=[bp])
                P.op("pe", lambda e, pp=pp, T4=T4, hs=hs, gp=gp: e.matmul(pp[:, 128:144], T4[hs, 1, 1, :], vs4[hs, 1, gp, :], start=False, stop=True), reads=rd, writes=[bp])
                st = ust[g % 2]; bst = B[f"ust{g % 2}"]
                P.op("dve", lambda e, pp=pp, st=st, u3=u3, gh=gh, g=g: e.scalar_tensor_tensor(st[:, 0:144], u3[:, gh, :], self.pk[:, o_d2 + j * 64 + g:o_d2 + j * 64 + g + 1], pp[:, 0:144], ALU.mult, ALU.add),
                     reads=[bp, B[f"ub{k}"], self.bpk], writes=[bst])
                yb16 = st[:, 256:256 + 72].bitcast(BF16)
                P.op("act", lambda e, st=st, yb16=yb16: e.activation(yb16, st[:, 0:144], AF.Gelu_apprx_tanh), reads=[bst], writes=[bst])
                for c8 in range(8):
                    P.dma("sp", self.Yd[g * 16:(g + 1) * 16, c8 * 128:(c8 + 1) * 128], yb16[c8 * 16:(c8 + 1) * 16, 0:128], reads=[bst], writes=[self.bYd[g]])
                P.dma("sp", self.Yds[g * 16:(g + 1) * 16, :], yb16[0:16, 128:144], reads=[bst], writes=[self.bYd[g]])

        if self.cfg.get("estop", 99) <= 5:
            self.alias(self.arena_bufs(), list(B.values()))
            return
        yp = ar[:, 0:4160].bitcast(BF16).rearrange("p (a t) -> p a t", t=T)
        byp = [Buf() for _ in range(8)]
        yg = ar[:, 4160:4160 + 2080].bitcast(BF16).rearrange("p (a t) -> p a t", t=T)
        byg = [[Buf() for _ in range(3)] for _ in range(4)]
        sgm = [ar[:, 6240 + i * 512:6240 + (i + 1) * 512] for i in range(2)]
        bsgm = [Buf(), Buf()]
        self.alias(byp + [b for r in byg for b in r] + bsgm, list(B.values()))
        for uc in range(8):
            P.dma("sp", yp[:, uc, 0:NPR], self.Yd[uc * 128:(uc + 1) * 128, :], reads=self.bYd[uc * 8:(uc + 1) * 8], writes=[byp[uc]])
            P.dma("sp", yp[:, uc, NPR:T], self.Yds[uc * 128:(uc + 1) * 128, :], reads=self.bYd[uc * 8:(uc + 1) * 8], writes=[byp[uc]])
        o_bg, _ = self.lay.fields["bglu"]
        ypp = lambda c, ti: yp[:, c, ti * 512:(ti + 1) * 512]
        for grp in range(2):
            for pr in range(2):
                wv, bw = self.wload(self.w_glu[j, :, (grp * 2 + pr) * 256:(grp * 2 + pr + 1) * 256], 8, 256)
                for jj in range(2):
                    oc = grp * 4 + pr * 2 + jj
                    ol = pr * 2 + jj
                    for ti in range(3):
                        t0, n = TT[ti]
                        pp, bp = self.ps()
                        for kc in range(8):
                            P.op("pe", lambda e, pp=pp, jj=jj, kc=kc, wv=wv, t0=t0, n=n: e.matmul(pp[:, 0:n], wv[:, kc, jj * 128:(jj + 1) * 128], yp[:, kc, t0:t0 + n], start=(kc == 0), stop=(kc == 7)),
                                 reads=[bw, byp[kc]], writes=[bp])
                        sg = sgm[ti % 2]; bsg = bsgm[ti % 2]
                        P.op("act", lambda e, pp=pp, sg=sg, n=n, oc=oc: e.activation(sg[:, 0:n], pp[:, 0:n], AF.Sigmoid, bias=self.pk[:, o_bg + j * 8 + oc:o_bg + j * 8 + oc + 1]), reads=[bp, self.bpk], writes=[bsg])
                        if ti < 2:
                            dst = yg[:, ol, 0:NPR].rearrange("p (j c) -> p c j", c=8)[:, 4 * ti:4 * ti + 4, :]
                            P.op("dve", lambda e, dst=dst, sg=sg, oc=oc, ti=ti: e.tensor_tensor(dst, ypp(oc, ti).rearrange("p (c j) -> p c j", c=4), sg.rearrange("p (c j) -> p c j", c=4), ALU.mult),
                                 reads=[bsg, byp[oc]], writes=[byg[ol][0], byg[ol][1]])
                        else:
                            P.op("dve", lambda e, sg=sg, oc=oc, ol=ol: e.tensor_tensor(yg[:, ol, NPR:T], yp[:, oc, NPR:T], sg[:, 0:NS], ALU.mult), reads=[bsg, byp[oc]], writes=[byg[ol][2]])
            self.outproj_group(self.w_out_ab[j], grp * 4, 4, yg, byg, G, bG)
        self.alias(list(B.values()), byp + [b for r in byg for b in r] + bsgm)
        if self.cfg.get("hgrn", True):
            self.hgrn_pass2(j, B)
        self.alias(self.arena_bufs(), list(B.values()))

    def outproj_group(self, w, k0, nk, yb, byb, G, bG):
        P = self.P
        for ocp in range(8):
            wv, bw = self.wload(w[k0 * 128:(k0 + nk) * 128, ocp * 256:(ocp + 1) * 256], nk, 256)
            for jj in range(2):
                oc = ocp * 2 + jj
                for ti in range(3):
                    t0, n = TT[ti]
                    ps_, bps_ = self.ps()
                    for kc in range(nk):
                        P.op("pe", lambda e, ps_=ps_, jj=jj, kc=kc, wv=wv, t0=t0, n=n: e.matmul(ps_[:, 0:n], wv[:, kc, jj * 128:(jj + 1) * 128], yb[:, kc, t0:t0 + n], start=(kc == 0), stop=(kc == nk - 1)),
                             reads=[bw, byb[kc][ti]], writes=[bps_])
                    self.resid_add(ps_, bps_, oc, ti, G, bG)

    def hg_fpath(self, j, hd, pf, bpf, n, fs, bfs, kf_dst, bkf, lg_dst, blg):
        P = self.P
        lb = self.lbt[:, j, hd:hd + 1]
        oml = self.omlb[:, j, hd:hd + 1]
        P.op("act", lambda e: e.activation(fs[:, 0:n], pf[:, 0:n], AF.Sigmoid), reads=[bpf], writes=[bfs])
        P.op("dve", lambda e: e.tensor_scalar(fs[:, 0:n], fs[:, 0:n], oml, lb, ALU.mult, ALU.add), reads=[bfs, self.blbt], writes=[bfs])
        P.op("dve", lambda e: e.tensor_scalar(kf_dst, fs[:, 0:n], -1.0, 1.0, ALU.mult, ALU.add), reads=[bfs], writes=[bkf])
        P.op("act", lambda e: e.activation(lg_dst, fs[:, 0:n], AF.Ln), reads=[bfs], writes=[blg])

    def hg_vtok(self, j, hd, wv, bw, col0, vt, bvt):
        P = self.P
        for q4 in range(4):
            pp, bp = self.ps()
            for cc in range(4):
                jc = q4 * 4 + cc
                for kc in range(NCH):
                    P.op("pe", lambda e, pp=pp, cc=cc, jc=jc, kc=kc: e.matmul(pp[0:64, cc * 128:(cc + 1) * 128], self.hT[:, kc, jc * 64:(jc + 1) * 64], wv[:, kc, col0:col0 + 128], start=(kc == 0), stop=(kc == NCH - 1)),
                         reads=[bw, self.bh[kc][jc // 8]], writes=[bp])
            P.op("act", lambda e, pp=pp, q4=q4: e.copy(vt.rearrange("p c v -> p (c v)")[0:64, q4 * 512:(q4 + 1) * 512], pp[0:64, 0:512]), reads=[bp], writes=[bvt])

    def hgrn_pass1(self, j, hgS, bhgS, B):
        P = self.P
        ar = self.arena
        X = {}
        off = [0]

        def tl(name, n):
            v = ar[:, off[0]:off[0] + n]
            off[0] += n
            X[name] = Buf()
            return v
        lg = tl("lg", 1024); kf = tl("kf", 1024); gg = tl("gg", 1024); fs = tl("fs", 512); onesf = tl("ones", 512)
        kE = tl("kE", 512).bitcast(BF16)
        vt = tl("vt", 1024).bitcast(BF16).rearrange("p (c v) -> p c v", v=128)
        kt = tl("kt", 1024).bitcast(BF16).rearrange("p (c v) -> p c v", v=128)
        gend = tl("gend", 8)
        assert off[0] <= 8192
        self.alias(list(X.values()), [B["V2"]])
        P.op("dve", lambda e: e.memset(onesf, 1.0), writes=[X["ones"]])
        for hd in range(8):
            if hd % 2 == 0:
                wf, bwf = self.wload(self.w_in_ab[j, :, 2048 + hd * 128:2048 + (hd + 2) * 128], NCH, 256)
                wi, bwi = self.wload(self.w_in_ab[j, :, 3072 + hd * 128:3072 + (hd + 2) * 128], NCH, 256)
            c0 = (hd % 2) * 128
            for ti in range(2):
                t0, n = TT[ti]
                pf, bpf = self.ps()
                for kc in range(NCH):
                    P.op("pe", lambda e, pf=pf, kc=kc, wf=wf, c0=c0, t0=t0, n=n: e.matmul(pf[:, 0:n], wf[:, kc, c0:c0 + 128], self.hT[:, kc, t0:t0 + n], start=(kc == 0), stop=(kc == NCH - 1)),
                         reads=[bwf, self.bh[kc][ti]], writes=[bpf])
                self.hg_fpath(j, hd, pf, bpf, n, fs, X["fs"], kf[:, t0:t0 + n], X["kf"], lg[:, t0:t0 + n], X["lg"])
            for ti in range(2):
                t0, n = TT[ti]
                init = 0.0 if ti == 0 else lg[:, t0 - 1:t0]
                init = 0.0 if ti == 0 else gg[:, t0 - 1:t0]
                P.op("dve", lambda e, t0=t0, n=n, init=init: e.tensor_tensor_scan(gg[:, t0:t0 + n], onesf[:, 0:n], lg[:, t0:t0 + n], init, ALU.mult, ALU.add),
                     reads=[X["lg"], X["ones"], X["gg"]], writes=[X["gg"]])
            P.op("dve", lambda e: e.tensor_scalar(lg, gg, -1.0, gg[:, NPR - 1:NPR], ALU.mult, ALU.add), reads=[X["gg"]], writes=[X["lg"]])
            P.op("act", lambda e: e.activation(lg, lg, AF.Exp), reads=[X["lg"]], writes=[X["lg"]])
            P.op("dve", lambda e: e.tensor_tensor(kE, kf, lg, ALU.mult), reads=[X["lg"], X["kf"]], writes=[X["kE"]])
            self.hg_vtok(j, hd, wi, bwi, c0, vt, X["vt"])
            for q4 in range(4):
                pp, bp = self.ps()
                ppb = pp[:].bitcast(BF16)
                for cc in range(4):
                    jc = q4 * 4 + cc
                    P.op("pe", lambda e, ppb=ppb, cc=cc, jc=jc: e.transpose(ppb[0:64, cc * 128:(cc + 1) * 128], kE[:, jc * 64:(jc + 1) * 64], self.ident_bf[:]),
                         reads=[X["kE"], self.bidb], writes=[bp])
                P.op("act", lambda e, ppb=ppb, q4=q4: e.copy(kt.rearrange("p c v -> p (c v)")[0:64, q4 * 512:(q4 + 1) * 512], ppb[0:64, 0:512]), reads=[bp], writes=[X["kt"]])
            pp, bp = self.ps()
            for jc in range(16):
                P.op("pe", lambda e, pp=pp, jc=jc: e.matmul(pp[:, 0:128], kt[0:64, jc, :], vt[0:64, jc, :], start=(jc == 0), stop=(jc == 15)), reads=[X["kt"], X["vt"]], writes=[bp])
            P.op("act", lambda e, pp=pp, hd=hd: e.copy(hgS[:, hd * 128:(hd + 1) * 128], pp[:, 0:128]), reads=[bp], writes=[bhgS])
        self.alias([B["V2"]], list(X.values()))

    def hgrn_pass2(self, j, B):
        P = self.P
        ar = self.arena
        G, bG = self.Gmod[1], self.bG[1]
        X = {}
        off = [0]

        def tl(name, n):
            v = ar[:, off[0]:off[0] + n]
            off[0] += n
            X[name] = Buf()
            return v
        yg = tl("yg", 2080).bitcast(BF16).rearrange("p (a t) -> p a t", t=T)
        byg = [[Buf() for _ in range(3)] for _ in range(4)]
        lk = tl("lk", 2048); lg = lk[:, 0:1024]; kf = lk[:, 1024:2048]; X["lg"] = X["lk"]; X["kf"] = X["lk"]
        qo = tl("qo", 1040); qf = qo[:, 0:1024]; o = qo; X["qf"] = X["qo"]; X["o"] = X["qo"]
        fs = tl("fs", 512); onesf = fs; X["ones"] = X["fs"]
        dif = tl("dif", 1024); tmp = dif[:, 0:512]; X["tmp"] = X["dif"]
        ex = lg; X["ex"] = X["lk"]
        qg = tl("qg", 512).bitcast(BF16); kg = tl("kg", 512).bitcast(BF16); qq = tl("qq", 512).bitcast(BF16)
        kk4 = [tl(f"kk4_{I}", 128 * (I + 1)).bitcast(BF16).rearrange("p (c s) -> p c s", c=16) for I in range(4)]
        kkz = tl("kkz", 128).bitcast(BF16).rearrange("p (i s) -> p i s", i=4)
        sg = tl("sg", 520).bitcast(BF16)
        vt = tl("vt", 1024).bitcast(BF16).rearrange("p (c v) -> p c v", v=128)
        S = tl("S", 128); Sb = tl("Sb", 64).bitcast(BF16)
        scb = tl("scb", 32).bitcast(BF16); kkt = tl("kkt", 64).bitcast(BF16)
        dd = tl("dd", 64)
        sm = tl("smp", 256)
        sm2 = tl("sm2", 256)
        S0 = lk; X["S0"] = X["lk"]
        assert off[0] <= 12096, off[0]
        mine = list(X.values()) + [b for r in byg for b in r]
        self.alias(mine, list(B.values()))
        o_nw, _ = self.lay.fields["hgnw"]
        o_cm, _ = self.lay.fields["cmask64"]
        cmask = self.pk[0:64, o_cm:o_cm + 64]
        d1, d2, d3 = dd[:, 0:16], dd[:, 16:32], dd[:, 32:48]
        S03 = S0.rearrange("p (b v) -> p b v", v=128)
        qs, fsm, ks, qfs, qk, vsT, osm, rs = [sm[:, i * NS:(i + 1) * NS] for i in range(8)]
        kst = sm[0:NS, 128:256]
        vst = sm2[0:NS, 0:128]
        ksel = sm2[0:NS, 128:256]
        for hd in range(8):
            hl = hd % 4
            wq, bwq = self.wload(self.w_in_ab[j, :, 1024 + hd * 128:1024 + (hd + 1) * 128], NCH, 128)
            wf, bwf = self.wload(self.w_in_ab[j, :, 2048 + hd * 128:2048 + (hd + 1) * 128], NCH, 128)
            c0 = 0
            P.op("dve", lambda e, hd=hd: e.tensor_copy(S, self.hg_init[:, hd * 128:(hd + 1) * 128]), reads=[self.bhg_init], writes=[X["S"]])
            for ti in range(3):
                t0, n = TT[ti]
                pq, bpq = self.ps()
                pf, bpf = self.ps()
                for (pt, bpt, w_, bw_) in ((pq, bpq, wq, bwq), (pf, bpf, wf, bwf)):
                    for kc in range(NCH):
                        P.op("pe", lambda e, pt=pt, kc=kc, w_=w_, c0=c0, t0=t0, n=n: e.matmul(pt[:, 0:n], w_[:, kc, c0:c0 + 128], self.hT[:, kc, t0:t0 + n], start=(kc == 0), stop=(kc == NCH - 1)),
                             reads=[bw_, self.bh[kc][ti]], writes=[bpt])
                if ti < 2:
                    P.op("act", lambda e, pq=pq, t0=t0, n=n: e.copy(qf[:, t0:t0 + n], pq[:, 0:n]), reads=[bpq], writes=[X["qf"]])
                    self.hg_fpath(j, hd, pf, bpf, n, fs, X["fs"], kf[:, t0:t0 + n], X["kf"], lg[:, t0:t0 + n], X["lg"])
                else:
                    P.op("act", lambda e, pq=pq: e.copy(qs, pq[:, 0:NS]), reads=[bpq], writes=[X["smp"]])
                    lb = self.lbt[:, j, hd:hd + 1]; oml = self.omlb[:, j, hd:hd + 1]
                    P.op("act", lambda e, pf=pf: e.activation(fsm, pf[:, 0:NS], AF.Sigmoid), reads=[bpf], writes=[X["smp"]])
                    P.op("dve", lambda e, lb=lb, oml=oml: e.tensor_scalar(fsm, fsm, oml, lb, ALU.mult, ALU.add), reads=[X["smp"], self.blbt], writes=[X["smp"]])
                    P.op("dve", lambda e: e.tensor_scalar(ks, fsm, -1.0, 1.0, ALU.mult, ALU.add), reads=[X["smp"]], writes=[X["smp"]])
                    P.op("dve", lambda e: e.tensor_tensor(qfs, qs, fsm, ALU.mult), reads=[X["smp"]], writes=[X["smp"]])
                    P.op("dve", lambda e: e.tensor_tensor(qk, qs, ks, ALU.mult), reads=[X["smp"]], writes=[X["smp"]])
            wg_, bwg_ = self.wload(self.w_in_ab[j, :, 4096 + hd * 128:4096 + (hd + 1) * 128], NCH, 128)
            for ti in range(3):
                t0, n = TT[ti]
                pg, bpg = self.ps()
                for kc in range(NCH):
                    P.op("pe", lambda e, pg=pg, kc=kc, c0=c0, t0=t0, n=n, wg_=wg_: e.matmul(pg[:, 0:n], wg_[:, kc, c0:c0 + 128], self.hT[:, kc, t0:t0 + n], start=(kc == 0), stop=(kc == NCH - 1)),
                         reads=[bwg_, self.bh[kc][ti]], writes=[bpg])
                P.op("act", lambda e, pg=pg, t0=t0, n=n: e.activation(sg[:, t0:t0 + n], pg[:, 0:n], AF.Silu), reads=[bpg], writes=[X["sg"]])
            wi, bwi = self.wload(self.w_in_ab[j, :, 3072 + hd * 128:3072 + (hd + 1) * 128], NCH, 128)
            self.hg_vtok(j, hd, wi, bwi, c0, vt, X["vt"])
            pv, bpv = self.ps()
            for kc in range(NCH):
                P.op("pe", lambda e, pv=pv, kc=kc, c0=c0, wi=wi: e.matmul(pv[0:NS, 0:128], self.hT[:, kc, NPR:T], wi[:, kc, c0:c0 + 128], start=(kc == 0), stop=(kc == NCH - 1)),
                     reads=[bwi, self.bh[kc][2]], writes=[bpv])
            for kc in range(NCH):
                P.op("pe", lambda e, pv=pv, kc=kc, c0=c0, wi=wi: e.matmul(pv[:, 128:128 + NS], wi[:, kc, c0:c0 + 128], self.hT[:, kc, NPR:T], start=(kc == 0), stop=(kc == NCH - 1)),
                     reads=[bwi, self.bh[kc][2]], writes=[bpv])
            P.op("act", lambda e, pv=pv: e.copy(vst, pv[0:NS, 0:128]), reads=[bpv], writes=[X["sm2"]])
            P.op("act", lambda e, pv=pv: e.copy(vsT, pv[:, 128:128 + NS]), reads=[bpv], writes=[X["smp"]])
            P.op("dve", lambda e: e.memset(onesf, 1.0), reads=[X["fs"]], writes=[X["fs"]])
            for ti in range(2):
                t0, n = TT[ti]
                init = 0.0 if ti == 0 else dif[:, t0 - 1:t0]
                P.op("dve", lambda e, t0=t0, n=n, init=init: e.tensor_tensor_scan(dif[:, t0:t0 + n], onesf[:, 0:n], lg[:, t0:t0 + n], init, ALU.mult, ALU.add),
                     reads=[X["lg"], X["ones"], X["dif"]], writes=[X["dif"]])
            G3 = dif.rearrange("p (c t) -> p c t", t=64)
            ex3 = ex.rearrange("p (c t) -> p c t", t=64)
            q3 = qf.rearrange("p (c t) -> p c t", t=64)
            k3 = kf.rearrange("p (c t) -> p c t", t=64)
            glast = G3[:, :, 63]
            P.op("dve", lambda e: e.tensor_copy(d3[:, 0:1], glast[:, 0:1]), reads=[X["dif"]], writes=[X["dd"]])
            P.op("dve", lambda e: e.tensor_tensor(d3[:, 1:16], glast[:, 1:16], glast[:, 0:15], ALU.subtract), reads=[X["dif"]], writes=[X["dd"]])
            P.op("act", lambda e: e.activation(d3, d3, AF.Exp), reads=[X["dd"]], writes=[X["dd"]])
            P.op("dve", lambda e: e.memset(d1[:, 0:1], 0.0), reads=[X["dd"]], writes=[X["dd"]])
            P.op("dve", lambda e: e.tensor_copy(d1[:, 1:16], glast[:, 0:15]), reads=[X["dif"]], writes=[X["dd"]])
            bc64 = lambda v: v.unsqueeze(2).broadcast_to([128, 16, 64])
            P.op("dve", lambda e: e.tensor_tensor(ex3, G3, bc64(d1), ALU.subtract), reads=[X["dif"], X["dd"]], writes=[X["ex"]])
            P.op("act", lambda e: e.activation(ex, ex, AF.Exp), reads=[X["ex"]], writes=[X["ex"]])
            P.op("dve", lambda e: e.tensor_tensor(qg, qf, ex, ALU.mult), reads=[X["qf"], X["ex"]], writes=[X["qg"]])
            P.op("dve", lambda e: e.tensor_tensor(ex3, G3, bc64(glast), ALU.subtract), reads=[X["dif"]], writes=[X["ex"]])
            P.op("act", lambda e: e.activation(ex, ex, AF.Exp, scale=-1.0), reads=[X["ex"]], writes=[X["ex"]])
            P.op("dve", lambda e: e.tensor_tensor(kg, kf, ex, ALU.mult), reads=[X["kf"], X["ex"]], writes=[X["kg"]])
            qq3 = qq.rearrange("p (c t) -> p c t", t=64)
            for I in range(4):
                w_ = 16 * (I + 1)
                gm = G3[:, :, 16 * I + 7]
                bcw = gm.unsqueeze(2).broadcast_to([128, 16, w_])
                exI = ex[:, 0:16 * w_].rearrange("p (c s) -> p c s", c=16)
                P.op("dve", lambda e, exI=exI, bcw=bcw, w_=w_: e.tensor_tensor(exI, G3[:, :, 0:w_], bcw, ALU.subtract), reads=[X["dif"]], writes=[X["ex"]])
                P.op("dve", lambda e, exI=exI: e.tensor_scalar(exI, exI, -80.0, 80.0, ALU.max, ALU.min), reads=[X["ex"]], writes=[X["ex"]])
                fsI = fs[:, 0:256].rearrange("p (c s) -> p c s", c=16)
                P.op("act", lambda e, exI=exI, fsI=fsI, I=I: e.activation(fsI, exI[:, :, 16 * I:16 * I + 16], AF.Exp), reads=[X["ex"], X["fs"]], writes=[X["fs"]])
                P.op("dve", lambda e, fsI=fsI, I=I: e.tensor_tensor(qq3[:, :, 16 * I:16 * I + 16], q3[:, :, 16 * I:16 * I + 16], fsI, ALU.mult), reads=[X["qf"], X["fs"]], writes=[X["qq"]])
                P.op("act", lambda e, exI=exI: e.activation(exI, exI, AF.Exp, scale=-1.0), reads=[X["ex"]], writes=[X["ex"]])
                P.op("dve", lambda e, exI=exI, I=I, w_=w_: e.tensor_tensor(kk4[I], k3[:, :, 0:w_], exI, ALU.mult), reads=[X["kf"], X["ex"]], writes=[X[f"kk4_{I}"]])
            P.op("dve", lambda e: e.memset(kkz, 0.0), reads=[X["kkz"]], writes=[X["kkz"]])
            for jc in range(16):
                cs = slice(jc * 64, (jc + 1) * 64)
                for I in range(4):
                    P.op("dve", lambda e, I=I, jc=jc: e.tensor_copy(kkz[:, I, 0:16 * (I + 1)], kk4[I][:, jc, :]), reads=[X[f"kk4_{I}"], X["kkz"]], writes=[X["kkz"]])
                ps1, bps1 = self.ps()
                for I in range(4):
                    P.op("pe", lambda e, ps1=ps1, I=I, jc=jc: e.matmul(ps1[0:64, 16 * I:16 * I + 16], kkz[:, I, :], qq[:, jc * 64 + 16 * I:jc * 64 + 16 * I + 16], start=True, stop=True), reads=[X["kkz"], X["qq"]], writes=[bps1])
                P.op("dve", lambda e, ps1=ps1: e.tensor_tensor(scb[0:64, 0:64], ps1[0:64, 0:64], cmask, ALU.mult), reads=[bps1, self.bpk], writes=[X["scb"]])
                ps2, bps2 = self.ps()
                ps2b = ps2[:].bitcast(BF16)
                P.op("pe", lambda e, ps2b=ps2b, cs=cs: e.transpose(ps2b[0:64, 0:128], kg[:, cs], self.ident_bf[:]), reads=[X["kg"], self.bidb], writes=[bps2])
                P.op("act", lambda e, ps2b=ps2b: e.copy(kkt[0:64, 0:128], ps2b[0:64, 0:128]), reads=[bps2], writes=[X["kkt"]])
                P.op("act", lambda e: e.copy(Sb[:, 0:128], S), reads=[X["S"]], writes=[X["Sb"]])
                ps3, bps3 = self.ps()
                P.op("pe", lambda e, ps3=ps3, jc=jc: e.matmul(ps3[:, 0:64], vt[0:64, jc, :], scb[0:64, 0:64], start=True, stop=False), reads=[X["vt"], X["scb"]], writes=[bps3])
                P.op("pe", lambda e, ps3=ps3, cs=cs: e.matmul(ps3[:, 0:64], Sb[:, 0:128], qg[:, cs], start=False, stop=True), reads=[X["Sb"], X["qg"]], writes=[bps3])
                P.op("act", lambda e, ps3=ps3, cs=cs: e.copy(o[:, cs], ps3[:, 0:64]), reads=[bps3, X["qq"], X["qg"]], writes=[X["o"]])
                ps4, bps4 = self.ps()
                P.op("pe", lambda e, ps4=ps4, jc=jc: e.matmul(ps4[:, 0:128], kkt[0:64, 0:128], vt[0:64, jc, :], start=True, stop=True), reads=[X["kkt"], X["vt"]], writes=[bps4])
                P.op("dve", lambda e, ps4=ps4, jc=jc: e.scalar_tensor_tensor(S, S, d3[:, jc:jc + 1], ps4[:, 0:128], ALU.mult, ALU.add), reads=[bps4, X["S"], X["dd"]], writes=[X["S"]])
            P.dma("sp", self.o_hgp[j, hd], S, reads=[X["S"]])
            P.dma("sp", S03, self.st_hg[j, hd], writes=[X["S0"]])
            pqk, bpqk = self.ps()
            P.op("pe", lambda e, pqk=pqk: e.matmul(pqk[:, 0:NS], self.ones_f[:], qk, start=True, stop=True), reads=[X["smp"], self.bonesf], writes=[bpqk])
            P.op("dve", lambda e, pqk=pqk: e.tensor_tensor(osm, vsT, pqk[:, 0:NS], ALU.mult), reads=[bpqk, X["smp"]], writes=[X["smp"]])
            po, bpo = self.ps()
            for b in range(NS):
                P.op("pe", lambda e, po=po, b=b: e.matmul(po[:, b:b + 1], S03[:, b, :], qfs[:, b:b + 1], start=True, stop=True), reads=[X["S0"], X["smp"]], writes=[bpo])
            P.op("dve", lambda e, po=po: e.tensor_tensor(o[:, NPR:T], osm, po[:, 0:NS], ALU.add), reads=[bpo, X["smp"]], writes=[X["o"]])
            pk_, bpk_ = self.ps()
            P.op("pe", lambda e, pk_=pk_: e.transpose(pk_[0:NS, 0:128], ks, self.fld("ident")), reads=[X["smp"], self.bpk], writes=[bpk_])
            P.op("act", lambda e, pk_=pk_: e.copy(kst, pk_[0:NS, 0:128]), reads=[bpk_], writes=[X["smp"]])
            io, _ = self.lay.fields["ident"]
            for b in range(NS):
                P.op("dve", lambda e, b=b: e.tensor_scalar(ksel, kst, self.pk[0:NS, io + b:io + b + 1], None, ALU.mult), reads=[X["smp"], self.bpk], writes=[X["sm2"]])
                pd, bpd = self.ps()
                P.op("pe", lambda e, pd=pd: e.matmul(pd[:, 0:128], ksel, vst, start=True, stop=True), reads=[X["sm2"]], writes=[bpd])
                P.op("dve", lambda e, pd=pd, b=b: e.scalar_tensor_tensor(S03[:, b, :], S03[:, b, :], fsm[:, b:b + 1], pd[:, 0:128], ALU.mult, ALU.add), reads=[bpd, X["S0"], X["smp"]], writes=[X["S0"]])
            P.dma("sp", self.o_hgs[j, hd], S03, reads=[X["S0"]])
            for ti in range(3):
                t0, n = TT[ti]
                P.op("act", lambda e, t0=t0, n=n: e.activation(tmp[:, 0:n], o[:, t0:t0 + n], AF.Square), reads=[X["o"]], writes=[X["tmp"]])
                pn, bpn = self.ps()
                P.op("pe", lambda e, pn=pn, n=n: e.matmul(pn[:, 0:n], self.ones_f[:], tmp[:, 0:n], start=True, stop=True), reads=[X["tmp"], self.bonesf], writes=[bpn])
                P.op("act", lambda e, pn=pn, n=n: e.activation(tmp[:, 0:n], pn[:, 0:n], AF.Sqrt, bias=self.eps_t[:, 0:1], scale=1.0 / 128), reads=[bpn, self.beps], writes=[X["tmp"]])
                P.op("dve", lambda e, n=n: e.reciprocal(tmp[:, 0:n], tmp[:, 0:n]), reads=[X["tmp"]], writes=[X["tmp"]])
                P.op("dve", lambda e, t0=t0, n=n: e.scalar_tensor_tensor(tmp[:, 0:n], o[:, t0:t0 + n], self.pk[:, o_nw + j:o_nw + j + 1], tmp[:, 0:n], ALU.mult, ALU.mult), reads=[X["o"], X["tmp"], self.bpk], writes=[X["tmp"]])
                P.op("dve", lambda e, t0=t0, n=n, hl=hl: e.tensor_tensor(yg[:, hl, t0:t0 + n], tmp[:, 0:n], sg[:, t0:t0 + n], ALU.mult), reads=[X["tmp"], X["sg"]], writes=[byg[hl][ti]])
            if hl == 3:
                self.outproj_group(self.w_out_ab[j], 8 + (hd // 4) * 4, 4, yg, byg, G, bG)
        self.alias(list(B.values()), mine)

    def build(self):
        P = self.P
        cfg = self.cfg
        self.eps_t = P.sbuf([128, 1], F32, "eps_t")
        self.beps = Buf()
        P.op("dve", lambda e: e.memset(self.eps_t[:], EPS), writes=[self.beps])
        self.setup()
        if cfg.get("mixer", True) and cfg.get("odd", True) and cfg.get("layers", DEPTH) > 1:
            self.odd_init()
        if cfg.get("mixer", True) and cfg.get("even", True) and cfg.get("layers", DEPTH) > 0:
            self.even_init()
        self.load_x()
        nl = cfg.get("layers", DEPTH)
        self.ada_pending = self.ada_items(0) if nl > 0 else []
        self.pump_ada(72)
        for l in range(nl):
            self.derive_mod(l, 0, 0.5)
            self.norm_mod(l, 0)
            if cfg.get("ffn", True):
                self.ffn(l, 0, 0)
            self.derive_mod(l, 1, 1.0)
            self.norm_mod(l, 1)
            if cfg.get("mixer", True):
                self.mixer(l)
            self.derive_mod(l, 2, 0.5)
            self.norm_mod(l, 2)
            if l + 1 < nl:
                self.ada_pending = self.ada_items(l + 1)
            if cfg.get("ffn", True):
                self.ffn(l, 1, 2, pump=1)
            self.pump_ada(72)
        self.final()
        return P.build()

    def mixer(self, l):
        if l % 2 == 1:
            if self.cfg.get("odd", True):
                self.odd_mixer(l)
        else:
            if self.cfg.get("even", True):
                self.even_mixer(l)


_CACHE = {}


def _get_nc(cfg):
    key = tuple(sorted(cfg.items()))
    if key not in _CACHE:
        _CACHE[key] = K(cfg).build()
    return _CACHE[key]


def _gate_band(w):
    dense = np.zeros((2560, 2560), np.float32)
    for n in range(16):
        dense[n * 160:(n + 1) * 160, n * 160:(n + 1) * 160] = w[n]
    out = np.zeros((20, 128, 3, 128), np.float32)
    for m in range(20):
        lo = 160 * ((128 * m) // 160)
        kc0 = lo // 128
        for s_ in range(3):
            kc = kc0 + s_
            if kc > 19:
                continue
            out[m, :, s_, :] = dense[kc * 128:(kc + 1) * 128, m * 128:(m + 1) * 128]
    return out


def kernel(cfg=None, **inp):
    cfg = dict(cfg or {})
    inp = {k: np.asarray(v) for k, v in inp.items()}
    pk = Pack()
    make_pack(pk, inp)
    pka = pk.array()
    xp, xs = inp["x_prompt"], inp["x_sample"]
    nl = max(1, cfg.get("layers", DEPTH))
    nf = nl if cfg.get("ffn", True) else 1
    use_odd = cfg.get("mixer", True) and cfg.get("odd", True) and nl > 1
    use_even = cfg.get("mixer", True) and cfg.get("even", True)
    shared = {"pk": pka, "w_ada": inp["w_ada"][:nl], "w_ffn_gu": inp["w_ffn_gu"][:nf], "w_ffn_d": inp["w_ffn_d"][:nf]}
    if use_odd:
        wg = np.stack([np.concatenate([_gate_band(inp["w_gate_a"][j]), _gate_band(inp["w_gate_x"][j])], axis=2).reshape(20, 128, 768)
                       for j in range(2)])
        shared.update({"w_in_c": inp["w_in_c"], "w_out_c": inp["w_out_c"], "wgate": np.ascontiguousarray(wg)})
    nje = 2 if nl > 2 else 1
    if use_even:
        def pl(a):
            return a
        bc = np.zeros((nje, 128, 4, 32, 16), np.float32)
        for j in range(nje):
            for q, key in enumerate(("s5_c_re", "s5_c_im")):
                bc[j, :, q] = inp[key][j].reshape(2, 32, 16, 64).transpose(0, 3, 1, 2).reshape(128, 32, 16)
            for q, key in enumerate(("s5_b_re", "s5_b_im")):
                bc[j, :, 2 + q] = inp[key][j].reshape(2, 32, 64, 16).transpose(0, 2, 1, 3).reshape(128, 32, 16)
        shared.update({"w_in_ab": inp["w_in_ab"][:nje], "s5_w_glu": inp["s5_w_glu"][:nje], "w_out_ab": inp["w_out_ab"][:nje], "s5bc": bc})
    in_maps = []
    for c in range(NCORES):
        seq, half = c // 2, c % 2
        xin = np.concatenate([xp[seq, half * NPR:(half + 1) * NPR], xs[c * NS:(c + 1) * NS, 0]], axis=0)
        cin = np.concatenate([inp["c_prompt"][seq:seq + 1], inp["c_sample"][c * NS:(c + 1) * NS]], axis=0)
        d = dict(shared)
        if use_even:
            st = np.zeros((nje, 128, 2, 32, NS), np.float32)
            for q, key in enumerate(("state_s5_re", "state_s5_im")):
                a = inp[key][:nje, c * NS:(c + 1) * NS]
                st[:, :, q] = a.reshape(nje, NS, 2, 32, 64).transpose(0, 2, 4, 3, 1).reshape(nje, 128, 32, NS)
            d["st_s5"] = st
            hg = inp["state_hgrn"][:nje, c * NS:(c + 1) * NS]
            d["st_hg"] = np.ascontiguousarray(hg.transpose(0, 2, 3, 1, 4))
            d["pm"] = np.full((128, 1), float(half), np.float32)
        d["xin"] = np.ascontiguousarray(xin, np.float32)
        d["cin"] = np.ascontiguousarray(cin, np.float32)
        if use_odd:
            sl = inp["state_lru"][:, c * NS:(c + 1) * NS]
            d["st_lru"] = np.ascontiguousarray(sl.reshape(2, NS, 20, 128).transpose(0, 3, 2, 1))
            scv = inp["state_conv"][:, c * NS:(c + 1) * NS]
            d["st_conv"] = np.ascontiguousarray(scv.reshape(2, NS, 3, 20, 128).transpose(0, 4, 3, 2, 1))
            d["pm"] = np.full((128, 1), float(half), np.float32)
        in_maps.append(d)
    nc = _get_nc(cfg)
    res = run_bass_kernel_spmd(nc, in_maps, core_ids=list(range(NCORES)))
    r = res.results
    f32 = np.float32
    y_prompt = np.zeros((4, 2048, D), f32)
    y_sample = np.zeros((128, 1, D), f32)
    s5r_p = np.zeros((2, 4, 64, 64), f32); s5i_p = np.zeros((2, 4, 64, 64), f32)
    hg_p = np.zeros((2, 4, 8, 128, 128), f32)
    lru_p = np.zeros((2, 4, 2560), f32); conv_p = np.zeros((2, 4, 3, 2560), f32)
    s5r_s = np.zeros((2, 128, 64, 64), f32); s5i_s = np.zeros((2, 128, 64, 64), f32)
    hg_s = np.zeros((2, 128, 8, 128, 128), f32)
    lru_s = np.zeros((2, 128, 2560), f32); conv_s = np.zeros((2, 128, 3, 2560), f32)
    for c in range(NCORES):
        seq, half = c // 2, c % 2
        rc = r[c]
        y_prompt[seq, half * NPR:(half + 1) * NPR] = rc["y"][0:NPR]
        y_sample[c * NS:(c + 1) * NS, 0] = rc["y"][NPR:T]
        bsl = slice(c * NS, (c + 1) * NS)
        if use_even:
            a = rc["o_s5s"].reshape(nje, 2, 64, 2, 32, NS)
            a = a.transpose(0, 3, 5, 1, 4, 2).reshape(nje, 2, NS, 64, 64)
            s5r_s[:nje, bsl] = a[:, 0]; s5i_s[:nje, bsl] = a[:, 1]
            hg_s[:nje, bsl] = rc["o_hgs"].transpose(0, 3, 1, 2, 4)
            if half == 1:
                a = rc["o_s5p"].reshape(nje, 2, 64, 2, 32).transpose(0, 3, 1, 4, 2).reshape(nje, 2, 64, 64)
                s5r_p[:nje, seq] = a[:, 0]; s5i_p[:nje, seq] = a[:, 1]
                hg_p[:nje, seq] = rc["o_hgp"]
        if use_odd:
            lru_s[:, bsl] = rc["o_lru_s"].transpose(0, 3, 2, 1).reshape(2, NS, 2560)
            conv_s[:, bsl] = rc["o_conv_s"].transpose(0, 4, 3, 2, 1).reshape(2, NS, 3, 2560)
            if half == 1:
                lru_p[:, seq] = rc["o_lru_p"].transpose(0, 2, 1).reshape(2, 2560)
                conv_p[:, seq] = rc["o_conv_p"].transpose(0, 3, 2, 1).reshape(2, 3, 2560)
    return (y_prompt, y_sample, s5r_p, s5i_p, hg_p, lru_p, conv_p, s5r_s, s5i_s, hg_s, lru_s, conv_s)
```

```python
import numpy as np
import concourse.bass as bass
import concourse.mybir as mybir
from concourse.bass_utils import run_bass_kernel_spmd
from contextlib import ExitStack

F32 = mybir.dt.float32
BF16 = mybir.dt.bfloat16
I32 = mybir.dt.int32
AF = mybir.ActivationFunctionType
ALU = mybir.AluOpType
AX = mybir.AxisListType

EPOCH = 30000
COMPUTE = ("pe", "act", "dve", "pool", "sp")

D = 2048
NPR = 1024
NS = 16
T = NPR + NS
DEPTH = 4
DFF = 5376
NCH = 16
TT = [(0, 512), (512, 512), (1024, 16)]
EPS = 1e-6
NCORES = 8


class Buf:
    __slots__ = ("w", "r")

    def __init__(self):
        self.w = {}
        self.r = {}


class Rec:
    __slots__ = ("fn", "waits", "marked", "dma")

    def __init__(self, fn):
        self.fn = fn
        self.waits = []
        self.marked = False
        self.dma = None


class Prog:
    def __init__(self, n_dma_sems=32, n_epochs=6):
        self.nc = bass.Bass("TRN2", target_bir_lowering=False)
        self.es = ExitStack()
        self.recs = {k: [] for k in COMPUTE}
        self.waited = {k: {} for k in COMPUTE}
        self.n_dma_sems = n_dma_sems
        self.n_epochs = n_epochs
        self.dma_tot = [0] * n_dma_sems
        self.dma_next = 0
        self._uid = 0
        self.colls = []

    def uid(self, p):
        self._uid += 1
        return f"{p}_{self._uid}"

    def sbuf(self, shape, dt, name=None):
        return self.es.enter_context(self.nc.sbuf_tensor(name or self.uid("sb"), list(shape), dt))

    def psum(self, shape, dt, name=None):
        return self.es.enter_context(self.nc.psum_tensor(name or self.uid("ps"), list(shape), dt))

    def dram_in(self, name, shape, dt=F32):
        return self.nc.dram_tensor(name, list(shape), dt, kind="ExternalInput").ap()

    def dram_out(self, name, shape, dt=F32):
        return self.nc.dram_tensor(name, list(shape), dt, kind="ExternalOutput").ap()

    def dram_tmp(self, name, shape, dt=F32):
        return self.nc.dram_tensor(name, list(shape), dt).ap()

    def _collect(self, reads, writes):
        deps = {}
        for b in reads:
            for k, v in b.w.items():
                if deps.get(k, -1) < v:
                    deps[k] = v
        for b in writes:
            for k, v in b.w.items():
                if deps.get(k, -1) < v:
                    deps[k] = v
            for k, v in b.r.items():
                if deps.get(k, -1) < v:
                    deps[k] = v
        return deps

    def _add_waits(self, eng, rec, deps):
        wd = self.waited[eng]
        for k, v in deps.items():
            if k == "pe" and eng == "pe":
                continue
            if wd.get(k, -1) >= v:
                continue
            wd[k] = v
            rec.waits.append((k, v))
            if isinstance(k, str):
                self.recs[k][v].marked = True

    def op(self, eng, fn, reads=(), writes=()):
        rec = Rec(fn)
        self._add_waits(eng, rec, self._collect(reads, writes))
        idx = len(self.recs[eng])
        self.recs[eng].append(rec)
        for b in reads:
            b.r[eng] = idx
        for b in writes:
            b.w = {eng: idx}
            b.r = {}
        return idx

    def dma(self, eng, out, in_, reads=(), writes=(), **kw):
        nsw = 8
        if eng == "pool":
            self.dma_next_sw = (getattr(self, "dma_next_sw", -1) + 1) % nsw
            s = self.n_dma_sems - nsw + self.dma_next_sw
        else:
            s = self.dma_next
            self.dma_next = (self.dma_next + 1) % (self.n_dma_sems - nsw)
        prev = self.dma_tot[s]
        tot = prev + 16
        self.dma_tot[s] = tot
        key = ("dma", s)

        def fn(e, out=out, in_=in_, kw=kw):
            return e.dma_start(out=out, in_=in_, **kw)

        rec = Rec(fn)
        rec.dma = (s, tot)
        deps = self._collect(reads, writes)
        if prev > 0:
            deps[key] = max(deps.get(key, -1), prev)
        self._add_waits(eng, rec, deps)
        self.recs[eng].append(rec)
        for b in reads:
            b.r[key] = tot
        for b in writes:
            b.w = {key: tot}
            b.r = {}

    def coll(self, kind, alu, ins, outs, groups, reads=(), writes=()):
        cid = len(self.colls)
        self.colls.append(None)
        key = ("cc", cid)

        def fn(e):
            return e.collective_compute(kind, alu, replica_groups=groups, ins=ins, outs=outs)

        rec = Rec(fn)
        rec.dma = ("cc", cid)
        self._add_waits("pool", rec, self._collect(reads, writes))
        self.recs["pool"].append(rec)
        for b in reads:
            b.r[key] = 1
        for b in writes:
            b.w = {key: 1}
            b.r = {}

    def build(self):
        nc = self.nc
        es = self.es
        sems = {k: [es.enter_context(nc.semaphore(f"s_{k}_{i}")) for i in range(self.n_epochs)] for k in COMPUTE}
        dsem = [es.enter_context(nc.semaphore(f"s_dma_{i}")) for i in range(self.n_dma_sems)]
        csem = [es.enter_context(nc.semaphore(f"s_cc_{i}")) for i in range(len(self.colls))]
        val = {}
        for k in COMPUTE:
            c = 0
            arr = []
            for r in self.recs[k]:
                if r.marked:
                    c += 1
                arr.append(c)
            val[k] = arr
            assert c < EPOCH * self.n_epochs, (k, c)
        fin = Rec(None)
        for s in range(self.n_dma_sems):
            if self.dma_tot[s] > 0:
                fin.waits.append((("dma", s), self.dma_tot[s]))
        self.recs["sp"].append(fin)
        val["sp"].append(val["sp"][-1] if val["sp"] else 0)

        def emit(k, e):
            for i, r in enumerate(self.recs[k]):
                for (wk, wv) in r.waits:
                    if isinstance(wk, str):
                        v = val[wk][wv]
                        e.wait_ge(sems[wk][(v - 1) // EPOCH], (v - 1) % EPOCH + 1)
                    elif wk[0] == "cc":
                        e.wait_ge(csem[wk[1]], 1)
                    else:
                        e.wait_ge(dsem[wk[1]], wv)
                if r.fn is None:
                    continue
                inst = r.fn(e)
                if r.dma is not None and r.dma[0] == "cc":
                    inst.then_inc(csem[r.dma[1]])
                elif r.dma is not None:
                    inst.then_inc(dsem[r.dma[0]], 16)
                elif r.marked:
                    v = val[k][i]
                    inst.then_inc(sems[k][(v - 1) // EPOCH], 1)

        with nc.Block() as block:
            @block.tensor
            def _(e):
                emit("pe", e)

            @block.scalar
            def _(e):
                emit("act", e)

            @block.vector
            def _(e):
                emit("dve", e)

            @block.gpsimd
            def _(e):
                emit("pool", e)

            @block.sync
            def _(e):
                emit("sp", e)
        es.close()
        return nc


def fm(v):
    v = np.asarray(v, np.float32)
    n = v.shape[-1] // 128
    lead = v.shape[:-1]
    r = v.reshape(lead + (n, 128))
    r = np.moveaxis(r, -1, 0)
    return np.ascontiguousarray(r.reshape(128, -1))


class Pack:
    def __init__(self):
        self.fields = {}
        self.cols = 0
        self.data = []

    def add(self, name, arr):
        arr = np.ascontiguousarray(arr, np.float32).reshape(128, -1)
        self.fields[name] = (self.cols, arr.shape[1])
        self.cols += arr.shape[1]
        self.data.append(arr)

    def array(self):
        return np.ascontiguousarray(np.concatenate(self.data, axis=1))


def pack_layout():
    pk = Pack()
    make_pack(pk, None)
    return pk


def make_pack(pk, inp):
    z = lambda *s: np.zeros(s, np.float32)
    g = (lambda k: inp[k]) if inp is not None else None
    pk.add("ident", np.eye(128, dtype=np.float32))
    pk.add("nw", fm(g("norm_w")) if inp is not None else z(128, 4 * 3 * 16))
    pk.add("fnw", fm(g("final_norm_w")) if inp is not None else z(128, 16))
    pk.add("bada", fm(g("b_ada")) if inp is not None else z(128, 4 * 144))
    pk.add("convw", fm(g("conv_w")) if inp is not None else z(128, 2 * 4 * 20))
    pk.add("convb", fm(g("conv_b")) if inp is not None else z(128, 2 * 20))
    pk.add("bga", fm(g("b_gate_a")) if inp is not None else z(128, 2 * 20))
    pk.add("bgx", fm(g("b_gate_x")) if inp is not None else z(128, 2 * 20))
    pk.add("lam", fm(g("lru_lambda")) if inp is not None else z(128, 2 * 20))
    pairl = lambda a: np.stack([a[j].reshape(2, 32, 64).transpose(0, 2, 1).reshape(128, 32) for j in range(2)], axis=1)
    pk.add("s5lamr", pairl(g("s5_lam_re")) if inp is not None else z(128, 64))
    pk.add("s5lami", pairl(g("s5_lam_im")) if inp is not None else z(128, 64))
    pk.add("s5lstep", np.stack([np.broadcast_to(g("s5_log_step")[j].reshape(2, 1, 32), (2, 64, 32)).reshape(128, 32) for j in range(2)], axis=1) if inp is not None else z(128, 64))
    pk.add("s5d2", np.stack([np.broadcast_to(g("s5_d")[j].reshape(1, 64, 16).transpose(0, 2, 1), (8, 16, 64)).reshape(128, 64) for j in range(2)], axis=1) if inp is not None else z(128, 128))
    pk.add("bglu", fm(g("s5_b_glu")) if inp is not None else z(128, 16))
    pk.add("hglog", fm(g("hg_lb_logits")) if inp is not None else z(128, 16))
    pk.add("hgnw", np.ascontiguousarray(g("hg_norm_w").T) if inp is not None else z(128, 2))
    r_ = np.arange(128)
    pk.add("tmask2", -((r_[None, :] // 16) < (r_[:, None] // 16)).astype(np.float32))
    cm = np.zeros((128, 64), np.float32); cm[0:64] = (np.arange(64)[None, :] >= np.arange(64)[:, None])
    pk.add("cmask64", cm)


class K:
    def __init__(self, cfg):
        self.cfg = cfg
        P = self.P = Prog()
        self.lay = pack_layout()
        self.xin = P.dram_in("xin", [T, D])
        self.cin = P.dram_in("cin", [NS + 1, D])
        self.pkd = P.dram_in("pk", [128, self.lay.cols])
        nl = max(1, cfg.get("layers", DEPTH))
        nf = nl if cfg.get("ffn", True) else 1
        self.w_ada = P.dram_in("w_ada", [nl, D, 9 * D])
        self.w_gu = P.dram_in("w_ffn_gu", [nf, 2, D, 2 * DFF])
        self.w_d = P.dram_in("w_ffn_d", [nf, 2, DFF, D])
        self.y = P.dram_out("y", [T, D])
        self.xres = P.sbuf([128, NCH, T], F32, "xres")
        self.bx = [[Buf() for _ in TT] for _ in range(NCH)]
        self.hT = P.sbuf([128, NCH, T], BF16, "hT")
        self.bh = [[Buf() for _ in TT] for _ in range(NCH)]
        self.arena = P.sbuf([128, 13408], F32, "arena")
        self.hid = self.arena[:, 0:7280].bitcast(BF16).rearrange("p (a t) -> p a t", t=T)
        self.bhid = [[Buf() for _ in TT] for _ in range(14)]
        self.pk = P.sbuf([128, self.lay.cols], F32, "pk_sb")
        self.bpk = Buf()
        self.mT = P.sbuf([128, 144, NS + 1], F32, "mT")
        self.bm = [Buf() for _ in range(9)]
        self.scT = P.sbuf([128, NCH, NS + 1], BF16, "scT")
        self.bsc = Buf()
        self.ones_bf = P.sbuf([128, 128], BF16, "ones_bf")
        self.bones = Buf()
        self.scr = self.arena[:, 7280:7280 + 5136]
        self.sm = self.arena[:, 7280 + 5136:13408]
        self.bscr = [Buf() for _ in range(8)]
        self.NSLOT = 3
        self.wsl = [P.sbuf([128, 4096], BF16, f"wsl{i}") for i in range(self.NSLOT)]
        self.bws = [Buf() for _ in range(self.NSLOT)]
        self.wnext = 0
        self.xslots = []
        self.Amod = P.sbuf([128, NCH, NS + 1], F32, "Amod")
        self.bA = Buf()
        self.Gmod = [P.sbuf([128, NCH, NS + 1], F32, f"Gmod{i}") for i in range(3)]
        self.bG = [Buf() for _ in range(3)]
        self.rstd = P.sbuf([128, 512], F32, "rstd")
        self.brstd = Buf()
        self.tmpA = [P.sbuf([128, 512], F32, f"tmpA{i}") for i in range(2)]
        self.btmpA = [Buf(), Buf()]
        self.tmpB = [P.sbuf([128, 512], BF16, f"tmpB{i}") for i in range(2)]
        self.btmpB = [Buf(), Buf()]
        self.tmpS = P.sbuf([128, NCH, NS], F32, "tmpS")
        self.btmpS = Buf()
        self.rr = 0
        self.pst = [P.psum([128, 512], F32, f"ps{i}") for i in range(7)]
        self.bps = [Buf() for _ in range(7)]
        self.psn = 0
        self.ps_ada = P.psum([128, 512], F32, "ps_ada")
        self.bps_ada = Buf()
        self.ada_pending = []

    def ps(self):
        i = self.psn
        self.psn = (self.psn + 1) % 7
        return self.pst[i], self.bps[i]

    def fld(self, name, a=0, b=None):
        o, w = self.lay.fields[name]
        if b is None:
            b = w
        return self.pk[:, o + a:o + b]

    def wload(self, src, kc, ncols):
        nsl = self.NSLOT + len(self.xslots)
        i = self.wnext % nsl
        self.wnext = (i + 1) % nsl
        if i < self.NSLOT:
            tile, buf = self.wsl[i], self.bws[i]
        else:
            tile, buf = self.xslots[i - self.NSLOT]
        view = tile[:, 0:kc * ncols].rearrange("p (c n) -> p c n", n=ncols)
        self.P.dma("pool", view, src.rearrange("(c p) n -> p c n", p=128), writes=[buf])
        return view, buf

    def ewise_eng(self):
        self.rr += 1
        return "act" if self.rr % 2 else "dve"

    def setup(self):
        P = self.P
        P.dma("sp", self.pk[:], self.pkd, writes=[self.bpk])
        P.op("dve", lambda e: e.memset(self.ones_bf[:], 1.0), writes=[self.bones])
        ident = self.fld("ident")
        cst = self.scr[0:NS + 1, 0:D]
        bc = self.bscr[0]
        P.dma("sp", cst, self.cin, writes=[bc])
        P.op("act", lambda e: e.activation(cst, cst, AF.Silu), reads=[bc], writes=[bc])
        pp, bp = self.ps()
        for c in range(NCH):
            P.op("pe", lambda e, c=c: e.transpose(pp[:, c * 17:(c + 1) * 17], cst[:, c * 128:(c + 1) * 128], ident[0:NS + 1, 0:NS + 1]),
                 reads=[bc, self.bpk], writes=[bp])
        P.op("dve", lambda e: e.tensor_copy(self.scT[:], pp[:, 0:NCH * 17].rearrange("p (c t) -> p c t", t=17)),
             reads=[bp], writes=[self.bsc])

    def load_x(self):
        P = self.P
        ident = self.fld("ident")
        stv = self.scr[:, 0:4096].rearrange("p (a f) -> p a f", f=D)
        bst = self.bscr[0]
        for g2 in range(NPR // 256):
            ti = (g2 * 256) // 512
            P.dma("sp", stv, self.xin[g2 * 256:(g2 + 1) * 256, :].rearrange("(a p) f -> p a f", p=128), writes=[bst])
            for c in range(NCH):
                pp, bp = self.ps()
                for a in range(2):
                    P.op("pe", lambda e, pp=pp, a=a, c=c: e.transpose(pp[:, a * 128:(a + 1) * 128], stv[:, a, c * 128:(c + 1) * 128], ident),
                         reads=[bst, self.bpk], writes=[bp])
                eng = self.ewise_eng()
                dst = self.xres[:, c, g2 * 256:(g2 + 1) * 256]
                if eng == "act":
                    P.op("act", lambda e, pp=pp, dst=dst: e.copy(dst, pp[:, 0:256]), reads=[bp], writes=[self.bx[c][ti]])
                else:
                    P.op("dve", lambda e, pp=pp, dst=dst: e.tensor_copy(dst, pp[:, 0:256]), reads=[bp], writes=[self.bx[c][ti]])
            self.pump_ada(8)
        sst = self.scr[0:NS, 0:D]
        P.dma("sp", sst, self.xin[NPR:T, :], writes=[self.bscr[0]])
        pp, bp = self.ps()
        for c in range(NCH):
            P.op("pe", lambda e, c=c: e.transpose(pp[:, c * NS:(c + 1) * NS], sst[:, c * 128:(c + 1) * 128], ident[0:NS, 0:NS]),
                 reads=[self.bscr[0], self.bpk], writes=[bp])
        P.op("dve", lambda e: e.tensor_copy(self.xres[:, :, NPR:T], pp[:, 0:NCH * NS].rearrange("p (c t) -> p c t", t=NS)),
             reads=[bp], writes=[self.bx[c][2] for c in range(NCH)])

    def ada_items(self, l):
        P = self.P
        items = []
        for blk in range(72):
            def item(blk=blk):
                oc0 = blk * 2
                wv, bw = self.wload(self.w_ada[l, :, oc0 * 128:(oc0 + 2) * 128], NCH, 256)
                grp = oc0 // 16
                for j in range(2):
                    oc = oc0 + j
                    loc = oc % 16
                    dst = self.ps_ada[:, loc * 17:(loc + 1) * 17]
                    for kc in range(NCH):
                        P.op("pe", lambda e, dst=dst, wv=wv, j=j, kc=kc: e.matmul(dst, wv[:, kc, j * 128:(j + 1) * 128], self.scT[:, kc, :], start=(kc == 0), stop=(kc == NCH - 1)),
                             reads=[bw, self.bsc], writes=[self.bps_ada])
                if oc0 % 16 == 14:
                    o, _ = self.lay.fields["bada"]
                    bias = self.pk[:, o + l * 144 + grp * 16:o + l * 144 + grp * 16 + 16].unsqueeze(2).broadcast_to([128, 16, NS + 1])
                    P.op("dve", lambda e, grp=grp, bias=bias: e.tensor_tensor(self.mT[:, grp * 16:(grp + 1) * 16, :], self.ps_ada[:, 0:16 * 17].rearrange("p (c t) -> p c t", t=17), bias, ALU.add),
                         reads=[self.bps_ada, self.bpk], writes=[self.bm[grp]])
            items.append(item)
        return items

    def pump_ada(self, n):
        for _ in range(n):
            if self.ada_pending:
                self.ada_pending.pop(0)()

    def derive_mod(self, l, s, gscale):
        P = self.P
        o, _ = self.lay.fields["nw"]
        nwb = self.pk[:, o + (l * 3 + s) * 16:o + (l * 3 + s) * 16 + 16].unsqueeze(2).broadcast_to([128, 16, NS + 1])
        sc = self.mT[:, (3 * s + 1) * 16:(3 * s + 2) * 16, :]
        P.op("dve", lambda e: e.scalar_tensor_tensor(self.Amod[:], sc, 1.0, nwb, ALU.add, ALU.mult),
             reads=[self.bm[3 * s + 1], self.bpk], writes=[self.bA])
        gt = self.mT[:, (3 * s + 2) * 16:(3 * s + 3) * 16, :]
        P.op("dve", lambda e: e.tensor_scalar(self.Gmod[s][:], gt, float(gscale), None, ALU.mult),
             reads=[self.bm[3 * s + 2]], writes=[self.bG[s]])

    def rms_stats(self, ti):
        P = self.P
        t0, n = TT[ti]
        pp, bp = self.ps()
        for c in range(NCH):
            k = c % 2
            P.op("act", lambda e, c=c, k=k: e.activation(self.tmpB[k][:, 0:n], self.xres[:, c, t0:t0 + n], AF.Square),
                 reads=[self.bx[c][ti]], writes=[self.btmpB[k]])
            P.op("pe", lambda e, c=c, k=k: e.matmul(pp[:, 0:n], self.ones_bf[:], self.tmpB[k][:, 0:n], start=(c == 0), stop=(c == NCH - 1)),
                 reads=[self.btmpB[k], self.bones], writes=[bp])
        return pp, bp

    def norm_mod(self, l, s):
        P = self.P
        shift = lambda c, a, b: self.mT[:, 3 * s * 16 + c, a:b]
        for ti in range(3):
            t0, n = TT[ti]
            pp, bp = self.rms_stats(ti)
            P.op("act", lambda e, pp=pp, n=n: e.activation(self.rstd[:, 0:n], pp[:, 0:n], AF.Sqrt, bias=self.eps_t[:, 0:1], scale=1.0 / D),
                 reads=[bp, self.beps], writes=[self.brstd])
            P.op("dve", lambda e, n=n: e.reciprocal(self.rstd[:, 0:n], self.rstd[:, 0:n]), reads=[self.brstd], writes=[self.brstd])
            if ti < 2:
                for c in range(NCH):
                    k = c % 2
                    P.op("dve", lambda e, c=c, k=k, t0=t0, n=n: e.scalar_tensor_tensor(self.tmpA[k][:], self.xres[:, c, t0:t0 + n], self.Amod[:, c, 0:1], self.rstd[:], ALU.mult, ALU.mult),
                         reads=[self.bx[c][ti], self.bA, self.brstd], writes=[self.btmpA[k]])
                    P.op("act", lambda e, c=c, k=k, t0=t0, n=n: e.activation(self.hT[:, c, t0:t0 + n], self.tmpA[k][:], AF.Identity, bias=shift(c, 0, 1)),
                         reads=[self.btmpA[k], self.bm[3 * s]], writes=[self.bh[c][ti]])
            else:
                xs = self.xres[:, :, NPR:T]
                rb = self.rstd[:, 0:NS].unsqueeze(1).broadcast_to([128, NCH, NS])
                allx = [self.bx[c][2] for c in range(NCH)]
                P.op("dve", lambda e: e.tensor_tensor(self.tmpS[:], xs, rb, ALU.mult), reads=allx + [self.brstd], writes=[self.btmpS])
                P.op("dve", lambda e: e.tensor_tensor(self.tmpS[:], self.tmpS[:], self.Amod[:, :, 1:NS + 1], ALU.mult), reads=[self.btmpS, self.bA], writes=[self.btmpS])
                P.op("dve", lambda e: e.tensor_tensor(self.hT[:, :, NPR:T], self.tmpS[:], self.mT[:, 3 * s * 16:3 * s * 16 + 16, 1:NS + 1], ALU.add),
                     reads=[self.btmpS, self.bm[3 * s]], writes=[self.bh[c][2] for c in range(NCH)])

    def resid_add(self, pp, bp, oc, ti, G, bG):
        P = self.P
        t0, n = TT[ti]
        if ti < 2:
            P.op("dve", lambda e: e.scalar_tensor_tensor(self.xres[:, oc, t0:t0 + n], pp[:, 0:n], G[:, oc, 0:1], self.xres[:, oc, t0:t0 + n], ALU.mult, ALU.add),
                 reads=[bp, bG, self.bx[oc][ti]], writes=[self.bx[oc][ti]])
        else:
            tmp = self.tmpS[:, 0, :]
            P.op("dve", lambda e: e.tensor_tensor(tmp, pp[:, 0:n], G[:, oc, 1:NS + 1], ALU.mult), reads=[bp, bG], writes=[self.btmpS])
            P.op("dve", lambda e: e.tensor_tensor(self.xres[:, oc, t0:t0 + n], self.xres[:, oc, t0:t0 + n], tmp, ALU.add),
                 reads=[self.btmpS, self.bx[oc][ti]], writes=[self.bx[oc][ti]])

    def ffn(self, l, w, s, pump=0):
        P = self.P
        G, bG = self.Gmod[s], self.bG[s]
        xb = [Buf(), Buf()]
        self.alias(xb, self.arena_bufs())
        self.xslots = [(self.arena[:, 7280 + i * 2048:7280 + (i + 1) * 2048].bitcast(BF16), xb[i]) for i in range(2)]
        self._ffn_body(l, w, s, pump, G, bG)
        self.xslots = []
        self.wnext = self.wnext % self.NSLOT
        for ab in self.arena_bufs():
            for b_ in xb:
                for d in (b_.w, b_.r):
                    for k_, v_ in d.items():
                        if ab.r.get(k_, -1) < v_:
                            ab.r[k_] = v_

    def _ffn_body(self, l, w, s, pump, G, bG):
        P = self.P
        for third in range(3):
            for hp in range(7):
                hc0 = third * 14 + hp * 2
                wg, bwg = self.wload(self.w_gu[l, w, :, hc0 * 128:(hc0 + 2) * 128], NCH, 256)
                wu, bwu = self.wload(self.w_gu[l, w, :, DFF + hc0 * 128:DFF + (hc0 + 2) * 128], NCH, 256)
                for j in range(2):
                    hl = hp * 2 + j
                    for ti in range(3):
                        t0, n = TT[ti]
                        pg, bpg = self.ps()
                        pu, bpu = self.ps()
                        for kc in range(NCH):
                            P.op("pe", lambda e, pg=pg, kc=kc, j=j, wg=wg, t0=t0, n=n: e.matmul(pg[:, 0:n], wg[:, kc, j * 128:(j + 1) * 128], self.hT[:, kc, t0:t0 + n], start=(kc == 0), stop=(kc == NCH - 1)),
                                 reads=[bwg, self.bh[kc][ti]], writes=[bpg])
                        for kc in range(NCH):
                            P.op("pe", lambda e, pu=pu, kc=kc, j=j, wu=wu, t0=t0, n=n: e.matmul(pu[:, 0:n], wu[:, kc, j * 128:(j + 1) * 128], self.hT[:, kc, t0:t0 + n], start=(kc == 0), stop=(kc == NCH - 1)),
                                 reads=[bwu, self.bh[kc][ti]], writes=[bpu])
                        k = (hl * 3 + ti) % 2
                        P.op("act", lambda e, pg=pg, k=k, n=n: e.activation(self.tmpA[k][:, 0:n], pg[:, 0:n], AF.Silu), reads=[bpg], writes=[self.btmpA[k]])
                        P.op("dve", lambda e, pu=pu, k=k, n=n, hl=hl, t0=t0: e.tensor_tensor(self.hid[:, hl, t0:t0 + n], self.tmpA[k][:, 0:n], pu[:, 0:n], ALU.mult),
                             reads=[self.btmpA[k], bpu], writes=[self.bhid[hl][ti]])
                self.pump_ada(pump)
            for ocp in range(8):
                wd, bwd = self.wload(self.w_d[l, w, third * 1792:(third + 1) * 1792, ocp * 256:(ocp + 1) * 256], 14, 256)
                for j in range(2):
                    oc = ocp * 2 + j
                    for ti in range(3):
                        t0, n = TT[ti]
                        pp, bp = self.ps()
                        for kc in range(14):
                            P.op("pe", lambda e, pp=pp, kc=kc, j=j, wd=wd, t0=t0, n=n: e.matmul(pp[:, 0:n], wd[:, kc, j * 128:(j + 1) * 128], self.hid[:, kc, t0:t0 + n], start=(kc == 0), stop=(kc == 13)),
                                 reads=[bwd, self.bhid[kc][ti]], writes=[bp])
                        self.resid_add(pp, bp, oc, ti, G, bG)
                self.pump_ada(pump)

    def final(self):
        P = self.P
        ident = self.fld("ident")
        fo, _ = self.lay.fields["fnw"]
        rst = self.scr[:, 0:1040]
        brst = self.bscr[0]
        for ti in range(3):
            t0, n = TT[ti]
            pp, bp = self.rms_stats(ti)
            P.op("act", lambda e, pp=pp, n=n, t0=t0: e.activation(rst[:, t0:t0 + n], pp[:, 0:n], AF.Sqrt, bias=self.eps_t[:, 0:1], scale=1.0 / D),
                 reads=[bp, self.beps], writes=[brst])
        P.op("dve", lambda e: e.reciprocal(rst, rst), reads=[brst], writes=[brst])
        for c in range(NCH):
            for ti in range(3):
                t0, n = TT[ti]
                P.op("dve", lambda e, c=c, t0=t0, n=n: e.scalar_tensor_tensor(self.xres[:, c, t0:t0 + n], self.xres[:, c, t0:t0 + n], self.pk[:, fo + c:fo + c + 1], rst[:, t0:t0 + n], ALU.mult, ALU.mult),
                     reads=[self.bx[c][ti], self.bpk, brst], writes=[self.bx[c][ti]])
        ost = [self.scr[:, 1040:3088], self.scr[:, 3088:5136]]
        bost = [self.bscr[1], self.bscr[2]]
        ntile = NPR // 128
        for a in range(ntile + 1):
            rows = 128 if a < ntile else NS
            k = a % 2
            ti = (a * 128) // 512 if a < ntile else 2
            for q in range(4):
                pp, bp = self.ps()
                for cc in range(4):
                    c = q * 4 + cc
                    P.op("pe", lambda e, pp=pp, cc=cc, c=c, a=a, rows=rows: e.transpose(pp[0:rows, cc * 128:(cc + 1) * 128], self.xres[:, c, a * 128:a * 128 + rows], ident),
                         reads=[self.bx[c][ti], self.bpk], writes=[bp])
                eng = self.ewise_eng()
                dst = ost[k][0:rows, q * 512:(q + 1) * 512]
                if eng == "act":
                    P.op("act", lambda e, pp=pp, dst=dst, rows=rows: e.copy(dst, pp[0:rows, :]), reads=[bp], writes=[bost[k]])
                else:
                    P.op("dve", lambda e, pp=pp, dst=dst, rows=rows: e.tensor_copy(dst, pp[0:rows, :]), reads=[bp], writes=[bost[k]])
            P.dma("sp", self.y[a * 128:a * 128 + rows, :], ost[k][0:rows, :], reads=[bost[k]])

    def alias(self, new_bufs, old_bufs):
        w = {}
        for ob in old_bufs:
            for d in (ob.w, ob.r):
                for k, v in d.items():
                    if w.get(k, -1) < v:
                        w[k] = v
        for nb in new_bufs:
            nb.w = {}
            nb.r = dict(w)

    def arena_bufs(self):
        return [b for row in self.bhid for b in row] + list(self.bscr)

    def odd_init(self):
        P = self.P
        cfg = self.cfg
        self.w_in_c = P.dram_in("w_in_c", [2, D, 5120])
        self.w_out_c = P.dram_in("w_out_c", [2, 2560, D])
        self.wgate = P.dram_in("wgate", [2, 20, 128, 768])
        self.st_lru = P.dram_in("st_lru", [2, 128, 20, NS])
        self.st_conv = P.dram_in("st_conv", [2, 128, 20, 3, NS])
        self.pmd = P.dram_in("pm", [128, 1])
        self.o_lru_p = P.dram_out("o_lru_p", [2, 128, 20])
        self.o_lru_s = P.dram_out("o_lru_s", [2, 128, 20, NS])
        self.o_conv_p = P.dram_out("o_conv_p", [2, 128, 20, 3])
        self.o_conv_s = P.dram_out("o_conv_s", [2, 128, 20, 3, NS])
        self.a_d = P.dram_tmp("a_d", [20, 128, NPR])
        self.b_d = P.dram_tmp("b_d", [20, 128, NPR])
        self.ba_d = [[Buf(), Buf()] for _ in range(20)]
        self.bb_d = [[Buf(), Buf()] for _ in range(20)]
        self.agin1 = P.dram_tmp("agin1", [128, 60]); self.agout1 = P.dram_tmp("agout1", [256, 60])
        self.agin2 = P.dram_tmp("agin2", [128, 20]); self.agout2 = P.dram_tmp("agout2", [256, 20])
        self.bag = [Buf() for _ in range(4)]
        self.pm = P.sbuf([128, 1], F32, "pm_sb"); self.bpm = Buf()
        P.dma("sp", self.pm[:], self.pmd, writes=[self.bpm])
        self.sm = self.arena[:, 10807:13408]
        self.bsm = {}
        off = [0]

        def carve(name, n):
            v = self.sm[:, off[0]:off[0] + n]
            off[0] += n
            self.bsm[name] = Buf()
            return v
        self.clam = carve("clam", 20)
        self.tails = carve("tails", 60)
        self.halo = carve("halo", 60)
        self.hend = carve("hend", 20)
        self.hinit = carve("hinit", 20)
        self.hcur = carve("hcur", 20)
        self.xs_all = carve("xs_all", 320)
        self.xc_s = carve("xc_s", 320)
        self.hs_all = carve("hs_all", 320)
        self.h0s = carve("h0s", 320)
        self.tsm = carve("tsm", 80)
        self.tsm2 = carve("tsm2", 80)
        self.tsm_big = carve("cbuf", 960)
        self.bcbuf = self.bsm["cbuf"]
        assert off[0] <= 2601, off[0]

    def band_tiles(self, m):
        lo = 160 * ((128 * m) // 160)
        kc0 = lo // 128
        res = []
        for s in range(3):
            kc = kc0 + s
            if kc > 19:
                continue
            blocks_out = set(range((128 * m) // 160, (128 * m + 127) // 160 + 1))
            blocks_in = set(range((128 * kc) // 160, (128 * kc + 127) // 160 + 1))
            if blocks_out & blocks_in:
                res.append((s, kc))
        return kc0, res

    def odd_mixer(self, l):
        P = self.P
        j = l // 2
        o_cw, _ = self.lay.fields["convw"]
        o_cb, _ = self.lay.fields["convb"]
        o_ga, _ = self.lay.fields["bga"]
        o_gx, _ = self.lay.fields["bgx"]
        o_lm, _ = self.lay.fields["lam"]
        cw = lambda tap, m: self.pk[:, o_cw + j * 80 + tap * 20 + m:o_cw + j * 80 + tap * 20 + m + 1]
        cwv = lambda tap, m0, k: self.pk[:, o_cw + j * 80 + tap * 20 + m0:o_cw + j * 80 + tap * 20 + m0 + k]
        cb = lambda m: self.pk[:, o_cb + j * 20 + m:o_cb + j * 20 + m + 1]
        cbv = lambda m0, k: self.pk[:, o_cb + j * 20 + m0:o_cb + j * 20 + m0 + k]
        bga = lambda m: self.pk[:, o_ga + j * 20 + m:o_ga + j * 20 + m + 1]
        bgx = lambda m: self.pk[:, o_gx + j * 20 + m:o_gx + j * 20 + m + 1]
        bs = self.bsm
        G, bG = self.Gmod[1], self.bG[1]
        xcb = self.arena[:, 0:2600].bitcast(BF16).rearrange("p (a t) -> p a t", t=T)
        bxcb = [[Buf() for _ in range(3)] for _ in range(5)]
        tmp = [self.arena[:, 2600 + i * 512:2600 + (i + 1) * 512] for i in range(6)]
        btmp = [Buf() for _ in range(6)]
        xp = self.arena[:, 5672:5672 + 5 * (NPR + 3)].rearrange("p (a t) -> p a t", t=NPR + 3)
        bxp = [[Buf() for _ in range(2)] for _ in range(5)]
        bxph = Buf()
        mine = [b for r in bxcb for b in r] + btmp + [b for r in bxp for b in r] + [bxph] + list(self.bsm.values())
        self.alias(mine, self.arena_bufs())

        lam = self.pk[:, o_lm + j * 20:o_lm + j * 20 + 20]
        P.op("act", lambda e: e.activation(self.clam, lam, AF.Sigmoid), reads=[self.bpk], writes=[bs["clam"]])
        P.op("act", lambda e: e.activation(self.clam, self.clam, AF.Ln), reads=[bs["clam"]], writes=[bs["clam"]])
        P.op("dve", lambda e: e.tensor_scalar(self.clam, self.clam, 8.0, None, ALU.mult), reads=[bs["clam"]], writes=[bs["clam"]])
        cbuf = self.tsm_big
        P.dma("sp", self.h0s.rearrange("p (a t) -> p a t", t=NS), self.st_lru[j], writes=[bs["h0s"]])
        P.dma("sp", cbuf[:].rearrange("p (a k t) -> p a k t", k=3, t=NS), self.st_conv[j], writes=[self.bcbuf])

        pp, bp = self.ps()
        for pr in range(10):
            wv, bw = self.wload(self.w_in_c[j, :, 2560 + pr * 256:2560 + (pr + 1) * 256], NCH, 256)
            for jj in range(2):
                m = pr * 2 + jj
                for kc in range(NCH):
                    P.op("pe", lambda e, m=m, jj=jj, kc=kc, wv=wv: e.matmul(pp[:, m * 3:(m + 1) * 3], wv[:, kc, jj * 128:(jj + 1) * 128], self.hT[:, kc, NPR - 3:NPR], start=(kc == 0), stop=(kc == NCH - 1)),
                         reads=[bw, self.bh[kc][1]], writes=[bp])
        P.op("dve", lambda e: e.tensor_copy(self.tails, pp[:, 0:60]), reads=[bp], writes=[bs["tails"]])
        P.dma("sp", self.o_conv_p[j].rearrange("p a k -> p (a k)"), self.tails, reads=[bs["tails"]])
        P.dma("sp", self.agin1, self.tails, reads=[bs["tails"]], writes=[self.bag[0]])
        P.coll("AllGather", ALU.bypass, [self.agin1], [self.agout1], [[0, 1], [2, 3], [4, 5], [6, 7]], reads=[self.bag[0]], writes=[self.bag[1]])
        P.dma("sp", self.halo, self.agout1[0:128, :], reads=[self.bag[1]], writes=[bs["halo"]])
        P.op("dve", lambda e: e.tensor_scalar(self.halo, self.halo, self.pm[:, 0:1], None, ALU.mult), reads=[bs["halo"], self.bpm], writes=[bs["halo"]])

        xs3 = self.xs_all.rearrange("p (a t) -> p a t", t=NS)
        xcs3 = self.xc_s.rearrange("p (a t) -> p a t", t=NS)
        hs3 = self.hs_all.rearrange("p (a t) -> p a t", t=NS)
        h0s3 = self.h0s.rearrange("p (a t) -> p a t", t=NS)
        cb4 = cbuf[:].rearrange("p (a k t) -> p a k t", k=3, t=NS)
        cnt = [0]
        for g in range(4):
            m0 = g * 5
            P.op("dve", lambda e, m0=m0: e.tensor_copy(xp[:, :, 0:3], self.halo[:, m0 * 3:(m0 + 5) * 3].rearrange("p (a k) -> p a k", k=3)),
                 reads=[bs["halo"]], writes=[bxph])
            for (ma, nb) in ((0, 2), (2, 2), (4, 1)):
                wv, bw = self.wload(self.w_in_c[j, :, 2560 + (m0 + ma) * 128:2560 + (m0 + ma + nb) * 128], NCH, nb * 128)
                for jj in range(nb):
                    ml = ma + jj
                    m = m0 + ml
                    for ti in range(3):
                        t0, n = TT[ti]
                        ps_, bps_ = self.ps()
                        for kc in range(NCH):
                            P.op("pe", lambda e, ps_=ps_, jj=jj, kc=kc, wv=wv, t0=t0, n=n: e.matmul(ps_[:, 0:n], wv[:, kc, jj * 128:(jj + 1) * 128], self.hT[:, kc, t0:t0 + n], start=(kc == 0), stop=(kc == NCH - 1)),
                                 reads=[bw, self.bh[kc][ti]], writes=[bps_])
                        if ti < 2:
                            P.op("act", lambda e, ps_=ps_, ml=ml, t0=t0, n=n: e.copy(xp[:, ml, 3 + t0:3 + t0 + n], ps_[:, 0:n]), reads=[bps_], writes=[bxp[ml][ti]])
                        else:
                            P.op("act", lambda e, ps_=ps_, m=m: e.copy(xs3[:, m, :], ps_[:, 0:NS]), reads=[bps_], writes=[bs["xs_all"]])
            for ml in range(5):
                m = m0 + ml
                for ti in range(2):
                    t0, n = TT[ti]
                    tx = tmp[4]; btx = btmp[4]
                    rd = [bxp[ml][0], bxp[ml][1], bxph, self.bpk]
                    P.op("dve", lambda e, ml=ml, m=m, t0=t0, n=n, tx=tx: e.tensor_scalar(tx, xp[:, ml, t0:t0 + n], cw(0, m), cb(m), ALU.mult, ALU.add), reads=rd, writes=[btx])
                    P.op("dve", lambda e, ml=ml, m=m, t0=t0, n=n, tx=tx: e.scalar_tensor_tensor(tx, xp[:, ml, t0 + 1:t0 + 1 + n], cw(1, m), tx, ALU.mult, ALU.add), reads=rd + [btx], writes=[btx])
                    P.op("dve", lambda e, ml=ml, m=m, t0=t0, n=n, tx=tx: e.scalar_tensor_tensor(tx, xp[:, ml, t0 + 2:t0 + 2 + n], cw(2, m), tx, ALU.mult, ALU.add), reads=rd + [btx], writes=[btx])
                    P.op("dve", lambda e, ml=ml, m=m, t0=t0, n=n, tx=tx: e.scalar_tensor_tensor(xcb[:, ml, t0:t0 + n], xp[:, ml, t0 + 3:t0 + 3 + n], cw(3, m), tx, ALU.mult, ALU.add), reads=rd + [btx], writes=[bxcb[ml][ti]])
            bc_ = lambda v: v.unsqueeze(2).broadcast_to([128, 5, NS])
            t5 = self.tsm.rearrange("p (a t) -> p a t", t=NS)
            t5b = self.tsm2.rearrange("p (a t) -> p a t", t=NS)
            P.op("dve", lambda e, m0=m0: e.tensor_tensor(t5, cb4[:, m0:m0 + 5, 0, :], bc_(cwv(0, m0, 5)), ALU.mult), reads=[self.bcbuf, self.bpk], writes=[bs["tsm"]])
            for tap in (1, 2):
                P.op("dve", lambda e, m0=m0, tap=tap: e.tensor_tensor(t5b, cb4[:, m0:m0 + 5, tap, :], bc_(cwv(tap, m0, 5)), ALU.mult), reads=[self.bcbuf, self.bpk], writes=[bs["tsm2"]])
                P.op("dve", lambda e: e.tensor_tensor(t5, t5, t5b, ALU.add), reads=[bs["tsm"], bs["tsm2"]], writes=[bs["tsm"]])
            P.op("dve", lambda e, m0=m0: e.tensor_tensor(t5b, xs3[:, m0:m0 + 5, :], bc_(cwv(3, m0, 5)), ALU.mult), reads=[bs["xs_all"], self.bpk], writes=[bs["tsm2"]])
            P.op("dve", lambda e: e.tensor_tensor(t5, t5, t5b, ALU.add), reads=[bs["tsm"], bs["tsm2"]], writes=[bs["tsm"]])
            P.op("dve", lambda e, m0=m0: e.tensor_tensor(xcs3[:, m0:m0 + 5, :], t5, bc_(cbv(m0, 5)), ALU.add), reads=[bs["tsm"], self.bpk], writes=[bs["xc_s"]])
            P.op("dve", lambda e, m0=m0: e.tensor_copy(xcb[:, :, NPR:T], xcs3[:, m0:m0 + 5, :]), reads=[bs["xc_s"]], writes=[bxcb[a][2] for a in range(5)])
            for ml in range(5):
                m = m0 + ml
                kc0, nz = self.band_tiles(m)
                i = self.wnext % self.NSLOT
                self.wnext = (i + 1) % self.NSLOT
                wv = self.wsl[i][:, 0:768].rearrange("p (s n) -> p s n", n=128)
                bw = self.bws[i]
                P.dma("pool", self.wsl[i][:, 0:768], self.wgate[j, m], writes=[bw])
                for ti in range(3):
                    t0, n = TT[ti]
                    pa, bpa = self.ps()
                    px, bpx = self.ps()
                    for gi, (pt, bpt) in enumerate(((pa, bpa), (px, bpx))):
                        for q, (s, kc) in enumerate(nz):
                            P.op("pe", lambda e, pt=pt, gi=gi, s=s, kc=kc, wv=wv, t0=t0, n=n, q=q, m0=m0: e.matmul(pt[:, 0:n], wv[:, gi * 3 + s, :], xcb[:, kc - m0, t0:t0 + n], start=(q == 0), stop=(q == len(nz) - 1)),
                                 reads=[bw, bxcb[kc - m0][ti]], writes=[bpt])
                    if ti < 2:
                        k2 = cnt[0] % 2
                        cnt[0] += 1
                        ta, bta = tmp[0 + k2], btmp[0 + k2]
                        tb, btb = tmp[2 + k2], btmp[2 + k2]
                        tq, btq = tmp[4], btmp[4]
                        txc, btxc = tmp[5], btmp[5]
                        th, bth = tmp[5], btmp[5]
                        P.op("act", lambda e, pa=pa, ta=ta, m=m: e.activation(ta, pa[:, 0:512], AF.Sigmoid, bias=bga(m)), reads=[bpa, self.bpk], writes=[bta])
                        P.op("act", lambda e, px=px, tb=tb, m=m: e.activation(tb, px[:, 0:512], AF.Sigmoid, bias=bgx(m)), reads=[bpx, self.bpk], writes=[btb])
                        P.op("act", lambda e, ta=ta, m=m: e.activation(ta, ta, AF.Exp, scale=self.clam[:, m:m + 1]), reads=[bta, bs["clam"]], writes=[bta])
                        P.op("dve", lambda e, ta=ta, tq=tq: e.tensor_tensor(tq, ta, ta, ALU.mult), reads=[bta], writes=[btq])
                        P.op("dve", lambda e, tq=tq: e.tensor_scalar(tq, tq, -1.0, 1.0, ALU.mult, ALU.add), reads=[btq], writes=[btq])
                        P.op("act", lambda e, tq=tq: e.activation(tq, tq, AF.Sqrt), reads=[btq], writes=[btq])
                        rd = [bxp[ml][0], bxp[ml][1], bxph, self.bpk]
                        P.op("dve", lambda e, ml=ml, m=m, t0=t0, n=n, txc=txc: e.tensor_scalar(txc, xp[:, ml, t0:t0 + n], cw(0, m), cb(m), ALU.mult, ALU.add), reads=rd, writes=[btxc])
                        for tap in (1, 2, 3):
                            P.op("dve", lambda e, ml=ml, m=m, t0=t0, n=n, txc=txc, tap=tap: e.scalar_tensor_tensor(txc, xp[:, ml, t0 + tap:t0 + tap + n], cw(tap, m), txc, ALU.mult, ALU.add), reads=rd + [btxc], writes=[btxc])
                        P.op("dve", lambda e, tb=tb, tq=tq: e.tensor_tensor(tb, tb, tq, ALU.mult), reads=[btb, btq], writes=[btb])
                        P.op("dve", lambda e, tb=tb, txc=txc: e.tensor_tensor(tb, tb, txc, ALU.mult), reads=[btb, btxc], writes=[btb])
                        P.dma("sp", self.a_d[m, :, t0:t0 + n], ta, reads=[bta], writes=[self.ba_d[m][ti]])
                        P.dma("sp", self.b_d[m, :, t0:t0 + n], tb, reads=[btb], writes=[self.bb_d[m][ti]])
                        init = 0.0 if ti == 0 else self.hend[:, m:m + 1]
                        P.op("dve", lambda e, ta=ta, tb=tb, th=th, init=init: e.tensor_tensor_scan(th, ta, tb, init, ALU.mult, ALU.add),
                             reads=[bta, btb, bs["hend"]], writes=[bth])
                        P.op("act", lambda e, th=th, m=m: e.copy(self.hend[:, m:m + 1], th[:, 511:512]), reads=[bth], writes=[bs["hend"]])
                    else:
                        sa, sb_ = self.tsm[:, 0:NS], self.tsm[:, NS:2 * NS]
                        sq_ = self.tsm[:, 2 * NS:3 * NS]
                        bt = bs["tsm"]
                        P.op("act", lambda e, pa=pa, m=m: e.activation(sa, pa[:, 0:NS], AF.Sigmoid, bias=bga(m)), reads=[bpa, self.bpk], writes=[bt])
                        P.op("act", lambda e, px=px, m=m: e.activation(sb_, px[:, 0:NS], AF.Sigmoid, bias=bgx(m)), reads=[bpx, self.bpk], writes=[bt])
                        P.op("act", lambda e, m=m: e.activation(sa, sa, AF.Exp, scale=self.clam[:, m:m + 1]), reads=[bt, bs["clam"]], writes=[bt])
                        P.op("dve", lambda e: e.tensor_tensor(sq_, sa, sa, ALU.mult), reads=[bt], writes=[bt])
                        P.op("dve", lambda e: e.tensor_scalar(sq_, sq_, -1.0, 1.0, ALU.mult, ALU.add), reads=[bt], writes=[bt])
                        P.op("act", lambda e: e.activation(sq_, sq_, AF.Sqrt), reads=[bt], writes=[bt])
                        P.op("dve", lambda e: e.tensor_tensor(sb_, sb_, sq_, ALU.mult), reads=[bt], writes=[bt])
                        P.op("dve", lambda e, m=m: e.tensor_tensor(sb_, sb_, xcs3[:, m, :], ALU.mult), reads=[bt, bs["xc_s"]], writes=[bt])
                        P.op("dve", lambda e, m=m: e.tensor_tensor(sa, sa, h0s3[:, m, :], ALU.mult), reads=[bt, bs["h0s"]], writes=[bt])
                        P.op("dve", lambda e, m=m: e.tensor_tensor(hs3[:, m, :], sa, sb_, ALU.add), reads=[bt], writes=[bs["hs_all"]])
        P.dma("sp", self.agin2, self.hend, reads=[bs["hend"]], writes=[self.bag[2]])
        P.coll("AllGather", ALU.bypass, [self.agin2], [self.agout2], [[0, 1], [2, 3], [4, 5], [6, 7]], reads=[self.bag[2]], writes=[self.bag[3]])
        P.dma("sp", self.hinit, self.agout2[0:128, :], reads=[self.bag[3]], writes=[bs["hinit"]])
        P.op("dve", lambda e: e.tensor_scalar(self.hcur, self.hinit, self.pm[:, 0:1], None, ALU.mult), reads=[bs["hinit"], self.bpm], writes=[bs["hcur"]])
        P.dma("sp", self.o_lru_s[j], hs3, reads=[bs["hs_all"]])
        P.dma("sp", self.o_conv_s[j, :, :, 0:2, :], cb4[:, :, 1:3, :], reads=[self.bcbuf])
        P.dma("sp", self.o_conv_s[j, :, :, 2, :], xs3, reads=[bs["xs_all"]])

        ybuf = xcb
        by = bxcb
        for g in range(4):
            m0 = g * 5
            for (ma, nb) in ((0, 2), (2, 2), (4, 1)):
                wv, bw = self.wload(self.w_in_c[j, :, (m0 + ma) * 128:(m0 + ma + nb) * 128], NCH, nb * 128)
                for jj in range(nb):
                    ml = ma + jj
                    m = m0 + ml
                    for ti in range(3):
                        t0, n = TT[ti]
                        ps_, bps_ = self.ps()
                        for kc in range(NCH):
                            P.op("pe", lambda e, ps_=ps_, jj=jj, kc=kc, wv=wv, t0=t0, n=n: e.matmul(ps_[:, 0:n], wv[:, kc, jj * 128:(jj + 1) * 128], self.hT[:, kc, t0:t0 + n], start=(kc == 0), stop=(kc == NCH - 1)),
                                 reads=[bw, self.bh[kc][ti]], writes=[bps_])
                        if ti < 2:
                            k2 = cnt[0] % 2
                            cnt[0] += 1
                            ta, bta = tmp[0 + k2], btmp[0 + k2]
                            tb, btb = tmp[2 + k2], btmp[2 + k2]
                            tg, btg = tmp[4], btmp[4]
                            th, bth = tmp[5], btmp[5]
                            P.dma("sp", ta, self.a_d[m, :, t0:t0 + n], reads=[self.ba_d[m][ti]], writes=[bta])
                            P.dma("sp", tb, self.b_d[m, :, t0:t0 + n], reads=[self.bb_d[m][ti]], writes=[btb])
                            P.op("act", lambda e, ps_=ps_, tg=tg: e.activation(tg, ps_[:, 0:512], AF.Gelu_apprx_tanh), reads=[bps_], writes=[btg])
                            P.op("dve", lambda e, ta=ta, tb=tb, th=th, m=m: e.tensor_tensor_scan(th, ta, tb, self.hcur[:, m:m + 1], ALU.mult, ALU.add),
                                 reads=[bta, btb, bs["hcur"]], writes=[bth])
                            P.op("act", lambda e, th=th, m=m: e.copy(self.hcur[:, m:m + 1], th[:, 511:512]), reads=[bth], writes=[bs["hcur"]])
                            P.op("dve", lambda e, tg=tg, th=th, ml=ml, t0=t0, n=n: e.tensor_tensor(ybuf[:, ml, t0:t0 + n], tg, th, ALU.mult), reads=[btg, bth], writes=[by[ml][ti]])
                        else:
                            sg = self.tsm[:, 0:NS]
                            P.op("act", lambda e, ps_=ps_: e.activation(sg, ps_[:, 0:NS], AF.Gelu_apprx_tanh), reads=[bps_], writes=[bs["tsm"]])
                            P.op("dve", lambda e, ml=ml, m=m: e.tensor_tensor(ybuf[:, ml, NPR:T], sg, hs3[:, m, :], ALU.mult), reads=[bs["tsm"], bs["hs_all"]], writes=[by[ml][2]])
            for ocp in range(8):
                wv, bw = self.wload(self.w_out_c[j, m0 * 128:(m0 + 5) * 128, ocp * 256:(ocp + 1) * 256], 5, 256)
                for jj in range(2):
                    oc = ocp * 2 + jj
                    for ti in range(3):
                        t0, n = TT[ti]
                        ps_, bps_ = self.ps()
                        for kc in range(5):
                            P.op("pe", lambda e, ps_=ps_, jj=jj, kc=kc, wv=wv, t0=t0, n=n: e.matmul(ps_[:, 0:n], wv[:, kc, jj * 128:(jj + 1) * 128], ybuf[:, kc, t0:t0 + n], start=(kc == 0), stop=(kc == 4)),
                                 reads=[bw, by[kc][ti]], writes=[bps_])
                        self.resid_add(ps_, bps_, oc, ti, G, bG)
        P.dma("sp", self.o_lru_p[j], self.hcur, reads=[bs["hcur"]])
        self.alias(self.arena_bufs(), mine)

    def even_init(self):
        P = self.P
        nj = 2 if self.cfg.get("layers", DEPTH) > 2 else 1
        self.nj_even = nj
        self.w_in_ab = P.dram_in("w_in_ab", [nj, D, 5120])
        self.w_glu = P.dram_in("s5_w_glu", [nj, 1024, 1024])
        self.w_out_ab = P.dram_in("w_out_ab", [nj, D, D])
        self.s5bc = P.dram_in("s5bc", [nj, 128, 4, 32, 16])
        self.st_s5 = P.dram_in("st_s5", [nj, 128, 2, 32, NS])
        self.st_hg = P.dram_in("st_hg", [nj, 8, 128, NS, 128])
        if not hasattr(self, "pm"):
            self.pmd = P.dram_in("pm", [128, 1])
            self.pm = P.sbuf([128, 1], F32, "pm_sb"); self.bpm = Buf()
            P.dma("sp", self.pm[:], self.pmd, writes=[self.bpm])
        self.o_s5p = P.dram_out("o_s5p", [nj, 128, 2, 32])
        self.o_s5s = P.dram_out("o_s5s", [nj, 128, 2, 32, NS])
        self.o_hgp = P.dram_out("o_hgp", [nj, 8, 128, 128])
        self.o_hgs = P.dram_out("o_hgs", [nj, 8, 128, NS, 128])
        self.s5T = P.dram_tmp("s5T", [nj, 64, 128, 128])
        self.s5R = P.dram_tmp("s5R", [nj, 64, 2, 128, 128])
        self.s5P = P.dram_tmp("s5P", [nj, 32, 128, 2, 128])
        self.bs5m = [Buf() for _ in range(nj)]
        self.U2 = P.dram_tmp("U2", [64, 128, 128]); self.Uds = P.dram_tmp("Uds", [1024, NS])
        self.Yd = P.dram_tmp("Yd", [1024, 1024], BF16); self.Yds = P.dram_tmp("Yds", [1024, NS], BF16)
        self.bUd = [Buf() for _ in range(8)]; self.bYd = [Buf() for _ in range(64)]
        self.agin3 = P.dram_tmp("agin3", [128, 1088]); self.agout3 = P.dram_tmp("agout3", [256, 1088])
        self.bag3 = [Buf(), Buf()]
        self.s5co = [P.sbuf([128, 8, 32], F32, f"s5co{j}") for j in range(nj)]
        self.bs5co = [Buf() for _ in range(nj)]
        self.lbt = P.sbuf([128, 2, 8], F32, "lbt"); self.blbt = Buf()
        self.omlb = P.sbuf([128, 2, 8], F32, "omlb")
        self.ident_bf = P.sbuf([128, 128], BF16, "ident_bf"); self.bidb = Buf()
        self.ones_f = P.sbuf([128, 128], F32, "ones_f"); self.bonesf = Buf()
        P.op("dve", lambda e: e.tensor_copy(self.ident_bf[:], self.fld("ident")), reads=[self.bpk], writes=[self.bidb])
        P.op("dve", lambda e: e.memset(self.ones_f[:], 1.0), writes=[self.bonesf])
        for j in range(nj):
            if self.cfg.get("eprebuild", True):
                self.s5_prebuild(j)
        o, _ = self.lay.fields["hglog"]
        P.op("dve", lambda e: e.memset(self.lbt[:, 0, :], 0.0), writes=[self.blbt])
        P.op("dve", lambda e: e.tensor_tensor(self.lbt[:, 1, :], self.pk[:, o + 8:o + 16], self.pk[:, o:o + 8], ALU.subtract), reads=[self.bpk, self.blbt], writes=[self.blbt])
        P.op("act", lambda e: e.activation(self.lbt[:, 1, :], self.lbt[:, 1, :], AF.Sigmoid), reads=[self.blbt], writes=[self.blbt])
        P.op("dve", lambda e: e.tensor_scalar(self.omlb[:], self.lbt[:], -1.0, 1.0, ALU.mult, ALU.add), reads=[self.blbt], writes=[self.blbt])

    def s5_prebuild(self, j):
        P = self.P
        ar = self.arena
        B = {}
        off = [0]

        def tl(name, n):
            v = ar[:, off[0]:off[0] + n]
            off[0] += n
            B[name] = Buf()
            return v
        step = tl("step", 32); lrd = tl("lrd", 32); th = tl("th", 32)
        Er = tl("Er", 256); Ei = tl("Ei", 256); Fr = tl("Fr", 256); Fi = tl("Fi", 256)
        t1 = tl("t1", 32); t2 = tl("t2", 32); t3 = tl("t3", 32); t4 = tl("t4", 32)
        ti_ = ar[:, off[0]:off[0] + 32].bitcast(I32); off[0] += 32; B["ti"] = Buf()
        zr = tl("zr", 32); zi = tl("zi", 32)
        bc = tl("bc", 2048)
        Bzr = tl("Bzr", 512); Bzi = tl("Bzi", 512)
        Pr = tl("Pr", 1024); NPi = tl("NPi", 1024); Rr = tl("Rr", 1024); Ri = tl("Ri", 1024)
        ta = tl("ta", 1024); tb = tl("tb", 1024)
        stg = [tl(f"stg{i}", 128) for i in range(4)]
        assert off[0] <= 13408, off[0]
        self.alias(list(B.values()), self.arena_bufs())
        o_lr, _ = self.lay.fields["s5lamr"]; o_li, _ = self.lay.fields["s5lami"]; o_ls, _ = self.lay.fields["s5lstep"]
        lamr = self.pk[:, o_lr + j * 32:o_lr + j * 32 + 32]
        lami = self.pk[:, o_li + j * 32:o_li + j * 32 + 32]
        lstep = self.pk[:, o_ls + j * 32:o_ls + j * 32 + 32]
        E3 = lambda v: v.rearrange("p (g m) -> p g m", m=8)
        P.dma("sp", bc.rearrange("p (a g k) -> p a g k", a=4, k=16), self.s5bc[j], writes=[B["bc"]])
        P.op("act", lambda e: e.activation(step, lstep, AF.Exp), reads=[self.bpk], writes=[B["step"]])
        P.op("dve", lambda e: e.tensor_tensor(lrd, lamr, step, ALU.mult), reads=[self.bpk, B["step"]], writes=[B["lrd"]])
        P.op("dve", lambda e: e.tensor_tensor(th, lami, step, ALU.mult), reads=[self.bpk, B["step"]], writes=[B["th"]])
        TWO_PI = 6.283185
        for m in range(1, 9):
            P.op("act", lambda e, m=m: e.activation(t1, lrd, AF.Exp, scale=float(m)), reads=[B["lrd"]], writes=[B["t1"]])
            P.op("act", lambda e, m=m: e.activation(t2, lrd, AF.Exp, scale=-float(m)), reads=[B["lrd"]], writes=[B["t2"]])
            for which in (0, 1):
                sh = 0.0 if which == 0 else 0.25
                P.op("dve", lambda e, m=m, sh=sh: e.tensor_scalar(t3, th, m / (2 * np.pi), sh, ALU.mult, ALU.add), reads=[B["th"]], writes=[B["t3"]])
                P.op("dve", lambda e: e.tensor_copy(ti_, t3), reads=[B["t3"]], writes=[B["ti"]])
                P.op("dve", lambda e: e.tensor_copy(t4, ti_), reads=[B["ti"]], writes=[B["t4"]])
                P.op("dve", lambda e: e.tensor_tensor(t3, t3, t4, ALU.subtract), reads=[B["t3"], B["t4"]], writes=[B["t3"]])
                P.op("dve", lambda e: e.tensor_scalar(t4, t3, 0.5, None, ALU.is_gt), reads=[B["t3"]], writes=[B["t4"]])
                P.op("dve", lambda e: e.tensor_tensor(t3, t3, t4, ALU.subtract), reads=[B["t3"], B["t4"]], writes=[B["t3"]])
                P.op("dve", lambda e: e.tensor_scalar(t4, t3, -0.5, None, ALU.is_lt), reads=[B["t3"]], writes=[B["t4"]])
                P.op("dve", lambda e: e.tensor_tensor(t3, t3, t4, ALU.add), reads=[B["t3"], B["t4"]], writes=[B["t3"]])
                P.op("act", lambda e: e.activation(t3, t3, AF.Sin, scale=TWO_PI), reads=[B["t3"]], writes=[B["t3"]])
                if which == 0:
                    P.op("dve", lambda e, m=m: e.tensor_tensor(E3(Ei)[:, :, m - 1], t1, t3, ALU.mult), reads=[B["t1"], B["t3"]], writes=[B["Ei"]])
                    P.op("dve", lambda e, m=m: e.scalar_tensor_tensor(E3(Fi)[:, :, m - 1], t2, -1.0, t3, ALU.mult, ALU.mult), reads=[B["t2"], B["t3"]], writes=[B["Fi"]])
                else:
                    P.op("dve", lambda e, m=m: e.tensor_tensor(E3(Er)[:, :, m - 1], t1, t3, ALU.mult), reads=[B["t1"], B["t3"]], writes=[B["Er"]])
                    P.op("dve", lambda e, m=m: e.tensor_tensor(E3(Fr)[:, :, m - 1], t2, t3, ALU.mult), reads=[B["t2"], B["t3"]], writes=[B["Fr"]])
        co = self.s5co[j]
        bco = self.bs5co[j]
        for (dst, src, m, neg) in ((0, Er, 8, False), (1, Er, 8, False), (2, Ei, 8, False), (3, Ei, 8, True),
                                   (4, Er, 1, False), (5, Er, 1, False), (6, Ei, 1, False), (7, Ei, 1, True)):
            P.op("dve", lambda e, dst=dst, src=src, m=m, neg=neg: e.tensor_scalar(co[:, dst, :], E3(src)[:, :, m - 1], -1.0 if neg else 1.0, None, ALU.mult),
                 reads=[B["Er"], B["Ei"]], writes=[bco])
        a1r, a1i = E3(Er)[:, :, 0], E3(Ei)[:, :, 0]
        rdE = [B["Er"], B["Ei"], self.bpk]
        P.op("dve", lambda e: e.tensor_tensor(t1, lamr, lamr, ALU.mult), reads=[self.bpk], writes=[B["t1"]])
        P.op("dve", lambda e: e.tensor_tensor(t2, lami, lami, ALU.mult), reads=[self.bpk], writes=[B["t2"]])
        P.op("dve", lambda e: e.tensor_tensor(t1, t1, t2, ALU.add), reads=[B["t1"], B["t2"]], writes=[B["t1"]])
        P.op("dve", lambda e: e.reciprocal(t1, t1), reads=[B["t1"]], writes=[B["t1"]])
        P.op("dve", lambda e: e.tensor_scalar(t2, a1r, -1.0, None, ALU.add), reads=rdE, writes=[B["t2"]])
        P.op("dve", lambda e: e.tensor_tensor(t3, t2, lamr, ALU.mult), reads=[B["t2"], self.bpk], writes=[B["t3"]])
        P.op("dve", lambda e: e.tensor_tensor(t4, a1i, lami, ALU.mult), reads=rdE, writes=[B["t4"]])
        P.op("dve", lambda e: e.tensor_tensor(t3, t3, t4, ALU.add), reads=[B["t3"], B["t4"]], writes=[B["t3"]])
        P.op("dve", lambda e: e.tensor_tensor(zr, t3, t1, ALU.mult), reads=[B["t3"], B["t1"]], writes=[B["zr"]])
        P.op("dve", lambda e: e.tensor_tensor(t3, a1i, lamr, ALU.mult), reads=rdE, writes=[B["t3"]])
        P.op("dve", lambda e: e.tensor_tensor(t4, t2, lami, ALU.mult), reads=[B["t2"], self.bpk], writes=[B["t4"]])
        P.op("dve", lambda e: e.tensor_tensor(t3, t3, t4, ALU.subtract), reads=[B["t3"], B["t4"]], writes=[B["t3"]])
        P.op("dve", lambda e: e.tensor_tensor(zi, t3, t1, ALU.mult), reads=[B["t3"], B["t1"]], writes=[B["zi"]])
        bc4 = bc.rearrange("p (a g k) -> p a g k", a=4, k=16)
        Cr_, Ci_, Br_, Bi_ = bc4[:, 0], bc4[:, 1], bc4[:, 2], bc4[:, 3]
        zb = lambda z: z.unsqueeze(2).broadcast_to([128, 32, 16])
        G3 = lambda v: v.rearrange("p (g k) -> p g k", k=16)
        ta3 = ta[:, 0:512].rearrange("p (g k) -> p g k", k=16)
        P.op("dve", lambda e: e.tensor_tensor(G3(Bzr), Br_, zb(zr), ALU.mult), reads=[B["bc"], B["zr"]], writes=[B["Bzr"]])
        P.op("dve", lambda e: e.tensor_tensor(ta3, Bi_, zb(zi), ALU.mult), reads=[B["bc"], B["zi"]], writes=[B["ta"]])
        P.op("dve", lambda e: e.tensor_tensor(G3(Bzr), G3(Bzr), ta3, ALU.subtract), reads=[B["Bzr"], B["ta"]], writes=[B["Bzr"]])
        P.op("dve", lambda e: e.tensor_tensor(G3(Bzi), Bi_, zb(zr), ALU.mult), reads=[B["bc"], B["zr"]], writes=[B["Bzi"]])
        P.op("dve", lambda e: e.tensor_tensor(ta3, Br_, zb(zi), ALU.mult), reads=[B["bc"], B["zi"]], writes=[B["ta"]])
        P.op("dve", lambda e: e.tensor_tensor(G3(Bzi), G3(Bzi), ta3, ALU.add), reads=[B["Bzi"], B["ta"]], writes=[B["Bzi"]])
        o_m2, _ = self.lay.fields["tmask2"]
        tmask2 = self.pk[:, o_m2:o_m2 + 128]
        ident = self.fld("ident")
        M4 = lambda v: v.rearrange("p (g c k) -> p g c k", c=8, k=16)
        M3 = lambda v: v.rearrange("p (g n) -> p g n", n=128)
        bm = self.bs5m[j]
        sc_ = 0
        for r in range(4):
            g0 = r * 8
            bk = lambda X, g0=g0: X[:, g0:g0 + 8, :].unsqueeze(2).broadcast_to([128, 8, 8, 16])
            bm_ = lambda Tb, g0=g0: E3(Tb)[:, g0:g0 + 8, :].unsqueeze(3).broadcast_to([128, 8, 8, 16])
            rdT = [B["Er"], B["Ei"], B["Fr"], B["Fi"], B["bc"], B["Bzr"], B["Bzi"]]
            ta4, tb4 = M4(ta), M4(tb)
            P.op("dve", lambda e, bk=bk, bm_=bm_: e.tensor_tensor(ta4, bk(Cr_), bm_(Er), ALU.mult), reads=rdT, writes=[B["ta"]])
            P.op("dve", lambda e, bk=bk, bm_=bm_: e.tensor_tensor(tb4, bk(Ci_), bm_(Ei), ALU.mult), reads=rdT, writes=[B["tb"]])
            P.op("dve", lambda e: e.tensor_tensor(M4(Pr), ta4, tb4, ALU.subtract), reads=[B["ta"], B["tb"]], writes=[B["Pr"]])
            P.op("dve", lambda e, bk=bk, bm_=bm_: e.tensor_tensor(ta4, bk(Cr_), bm_(Ei), ALU.mult), reads=rdT, writes=[B["ta"]])
            P.op("dve", lambda e, bk=bk, bm_=bm_: e.tensor_tensor(tb4, bk(Ci_), bm_(Er), ALU.mult), reads=rdT, writes=[B["tb"]])
            P.op("dve", lambda e: e.scalar_tensor_tensor(M4(NPi), ta4, -1.0, tb4, ALU.mult, ALU.subtract), reads=[B["ta"], B["tb"]], writes=[B["NPi"]])
            P.op("dve", lambda e, bk=bk, bm_=bm_: e.tensor_tensor(ta4, bk(G3(Bzr)), bm_(Fr), ALU.mult), reads=rdT, writes=[B["ta"]])
            P.op("dve", lambda e, bk=bk, bm_=bm_: e.tensor_tensor(tb4, bk(G3(Bzi)), bm_(Fi), ALU.mult), reads=rdT, writes=[B["tb"]])
            P.op("dve", lambda e: e.tensor_tensor(M4(Rr), ta4, tb4, ALU.subtract), reads=[B["ta"], B["tb"]], writes=[B["Rr"]])
            P.op("dve", lambda e, bk=bk, bm_=bm_: e.tensor_tensor(ta4, bk(G3(Bzi)), bm_(Fr), ALU.mult), reads=rdT, writes=[B["ta"]])
            P.op("dve", lambda e, bk=bk, bm_=bm_: e.tensor_tensor(tb4, bk(G3(Bzr)), bm_(Fi), ALU.mult), reads=rdT, writes=[B["tb"]])
            P.op("dve", lambda e: e.tensor_tensor(M4(Ri), ta4, tb4, ALU.add), reads=[B["ta"], B["tb"]], writes=[B["Ri"]])
            P.dma("sp", self.s5P[j, g0:g0 + 8, :, 0, :].rearrange("g p n -> p g n"), M3(Pr), reads=[B["Pr"]], writes=[bm])
            P.dma("sp", self.s5P[j, g0:g0 + 8, :, 1, :].rearrange("g p n -> p g n"), M3(NPi), reads=[B["NPi"]], writes=[bm])
            for gl in range(8):
                gp = g0 + gl
                for gh in range(2):
                    g = gh * 32 + gp
                    hs = slice(gh * 64, gh * 64 + 64)
                    pp, bp = self.ps()
                    P.op("pe", lambda e, pp=pp, gl=gl, hs=hs: e.matmul(pp[:, 0:128], M3(Rr)[hs, gl, :], M3(Pr)[hs, gl, :], start=True, stop=False), reads=[B["Rr"], B["Pr"]], writes=[bp])
                    P.op("pe", lambda e, pp=pp, gl=gl, hs=hs: e.matmul(pp[:, 0:128], M3(Ri)[hs, gl, :], M3(NPi)[hs, gl, :], start=False, stop=True), reads=[B["Ri"], B["NPi"]], writes=[bp])
                    st = stg[sc_ % 4]; bst = B[f"stg{sc_ % 4}"]; sc_ += 1
                    P.op("dve", lambda e, pp=pp, st=st: e.tensor_tensor(st, pp[:, 0:128], tmask2, ALU.mult), reads=[bp, self.bpk], writes=[bst])
                    P.dma("sp", self.s5T[j, g], st, reads=[bst], writes=[bm])
                    for ri, Rm, bR in ((0, Rr, B["Rr"]), (1, Ri, B["Ri"])):
                        pp, bp = self.ps()
                        P.op("pe", lambda e, pp=pp, gl=gl, Rm=Rm: e.transpose(pp[:, 0:128], M3(Rm)[:, gl, :], ident), reads=[bR, self.bpk], writes=[bp])
                        st = stg[sc_ % 4]; bst = B[f"stg{sc_ % 4}"]; sc_ += 1
                        P.op("dve", lambda e, pp=pp, st=st, gh=gh: e.tensor_copy(st[:, gh * 64:gh * 64 + 64], pp[:, gh * 64:gh * 64 + 64]), reads=[bp], writes=[bst])
                        P.op("dve", lambda e, st=st, gh=gh: e.memset(st[:, (1 - gh) * 64:(1 - gh) * 64 + 64], 0.0), reads=[bst], writes=[bst])
                        P.dma("sp", self.s5R[j, g, ri], st, reads=[bst], writes=[bm])
        self.alias(self.arena_bufs(), list(B.values()))

    def even_mixer(self, l):
        P = self.P
        j = l // 2
        self._emix_j = j
        ar = self.arena
        G, bG = self.Gmod[1], self.bG[1]
        co = self.s5co[j]; bco = self.bs5co[j]
        bm = self.bs5m[j]
        PAIRS = [[0, 1], [2, 3], [4, 5], [6, 7]]
        B = {}
        off = [0]

        def tl(name, n):
            v = ar[:, off[0]:off[0] + n]
            off[0] += n
            B[name] = Buf()
            return v
        V2 = tl("V2", 8192)
        V4 = V2.rearrange("p (j r g) -> p j r g", r=2, g=32)
        vs = tl("vs", 1024)
        vs4 = vs.rearrange("p (r g b) -> p r g b", r=2, b=NS)
        Rb = [tl(f"Rb{i}", 512) for i in range(2)]
        ub = [tl(f"ub{i}", 288) for i in range(2)]
        sc = tl("sc", 64); tt_ = tl("tt", 64); p1 = tl("p1", 64); p2 = tl("p2", 64)
        ust_off = off[0]
        ust = [tl(f"ust{i}", 512) for i in range(2)]
        assert off[0] <= 12096, off[0]
        stg = ar[:, 12096:12096 + 1088]
        bstg = Buf()
        hgS = stg[:, 64:1088]
        B["hgS"] = bstg
        self.alias(list(B.values()), self.arena_bufs())
        ident = self.fld("ident")

        if self.cfg.get("estop", 99) <= 0:
            self.alias(self.arena_bufs(), list(B.values()))
            return
        self.hgrn_pass1(j, hgS, B["hgS"], B)

        if self.cfg.get("estop", 99) <= 1:
            self.alias(self.arena_bufs(), list(B.values()))
            return
        hperm = lambda kc: self.hT[:, kc, 0:NPR].rearrange("p (j c) -> p c j", c=8)
        P.op("dve", lambda e: e.memset(ub[0], 0.0), writes=[B["ub0"]])
        P.op("dve", lambda e: e.memset(ub[1], 0.0), writes=[B["ub1"]])
        P.dma("sp", vs4, self.st_s5[j], writes=[B["vs"]])
        k_ = 0
        for pr in range(4):
            wv, bw = self.wload(self.w_in_ab[j, :, pr * 256:(pr + 1) * 256], NCH, 256)
            for jj in range(2):
                uc = pr * 2 + jj
                for ti in range(3):
                    pp, bp = self.ps()
                    if ti < 2:
                        for c4 in range(4):
                            for kc in range(NCH):
                                P.op("pe", lambda e, pp=pp, jj=jj, kc=kc, wv=wv, ti=ti, c4=c4: e.matmul(pp[:, c4 * 128:(c4 + 1) * 128], wv[:, kc, jj * 128:(jj + 1) * 128], hperm(kc)[:, 4 * ti + c4, :], start=(kc == 0), stop=(kc == NCH - 1)),
                                     reads=[bw, self.bh[kc][0], self.bh[kc][1]], writes=[bp])
                        st = ust[k_ % 2]; bst = B[f"ust{k_ % 2}"]; k_ += 1
                        P.op("act", lambda e, pp=pp, st=st: e.copy(st, pp[:, 0:512]), reads=[bp], writes=[bst])
                        for c4 in range(4):
                            cc = 4 * ti + c4
                            P.dma("sp", self.U2[uc * 8:(uc + 1) * 8, cc * 16:(cc + 1) * 16, :], st[:, c4 * 128:(c4 + 1) * 128], reads=[bst], writes=[self.bUd[uc]])
                    else:
                        for kc in range(NCH):
                            P.op("pe", lambda e, pp=pp, jj=jj, kc=kc, wv=wv: e.matmul(pp[:, 0:NS], wv[:, kc, jj * 128:(jj + 1) * 128], self.hT[:, kc, NPR:T], start=(kc == 0), stop=(kc == NCH - 1)),
                                 reads=[bw, self.bh[kc][2]], writes=[bp])
                        st = ust[k_ % 2]; bst = B[f"ust{k_ % 2}"]; k_ += 1
                        P.op("act", lambda e, pp=pp, st=st: e.copy(st[:, 0:NS], pp[:, 0:NS]), reads=[bp], writes=[bst])
                        P.dma("sp", self.Uds[uc * 128:(uc + 1) * 128, :], st[:, 0:NS], reads=[bst], writes=[self.bUd[uc]])

        if self.cfg.get("estop", 99) <= 2:
            self.alias(self.arena_bufs(), list(B.values()))
            return
        def load_u(gp, k):
            u3 = ub[k].rearrange("p (h n) -> p h n", h=2)
            for gh in range(2):
                g = gh * 32 + gp
                P.dma("sp", u3[:, gh, 0:128], self.U2[g], reads=[self.bUd[g // 8]], writes=[B[f"ub{k}"]])
                P.dma("sp", u3[0:16, gh, 128:144], self.Uds[g * 16:(g + 1) * 16, :], reads=[self.bUd[g // 8]], writes=[B[f"ub{k}"]])
            return u3

        for gp in range(32):
            k = gp % 2
            R4 = Rb[k].rearrange("p (h r n) -> p h r n", h=2, r=2)
            for gh in range(2):
                g = gh * 32 + gp
                P.dma("sp", R4[:, gh], self.s5R[j, g].rearrange("r p n -> p r n"), reads=[bm], writes=[B[f"Rb{k}"]])
            u3 = load_u(gp, k)
            for ri in range(2):
                pp, bp = self.ps()
                for gh in range(2):
                    P.op("pe", lambda e, pp=pp, R4=R4, u3=u3, gh=gh, ri=ri: e.matmul(pp[:, 0:144], R4[:, gh, ri, :], u3[:, gh, :], start=(gh == 0), stop=(gh == 1)),
                         reads=[B[f"Rb{k}"], B[f"ub{k}"]], writes=[bp])
                P.op("dve", lambda e, pp=pp, ri=ri, gp=gp: e.tensor_copy(V4[:, :, ri, gp], pp[:, 0:128]), reads=[bp], writes=[B["V2"]])
                P.op("dve", lambda e, pp=pp, ri=ri, gp=gp: e.tensor_tensor(vs4[:, ri, gp, :], pp[:, 128:144], vs4[:, ri, gp, :], ALU.add), reads=[bp, B["vs"]], writes=[B["vs"]])

        if self.cfg.get("estop", 99) <= 3:
            self.alias(self.arena_bufs(), list(B.values()))
            return
        A8 = co[:, 0:2, :]
        a8i, na8i = co[:, 2, :], co[:, 3, :]
        sc3 = sc.rearrange("p (r g) -> p r g", r=2)
        t3 = tt_.rearrange("p (r g) -> p r g", r=2)
        p13 = p1.rearrange("p (r g) -> p r g", r=2)
        p23 = p2.rearrange("p (r g) -> p r g", r=2)

        def cstep(t_ap, A, ai, nai, out3, eng="dve"):
            P.op(eng, lambda e: e.tensor_tensor(p13, A, t_ap, ALU.mult), reads=[B["tt"], B["V2"], B["vs"], bco], writes=[B["p1"]])
            P.op(eng, lambda e: e.tensor_tensor(p23[:, 0, :], t_ap[:, 1, :], nai, ALU.mult), reads=[B["tt"], B["V2"], B["vs"], bco], writes=[B["p2"]])
            P.op(eng, lambda e: e.tensor_tensor(p23[:, 1, :], t_ap[:, 0, :], ai, ALU.mult), reads=[B["tt"], B["V2"], B["vs"], bco], writes=[B["p2"]])
            P.op(eng, lambda e: e.tensor_tensor(out3, p13, p23, ALU.add), reads=[B["p1"], B["p2"]], writes=[B["sc"]])

        P.op("dve", lambda e: e.memset(sc, 0.0), writes=[B["sc"]])
        for jj in range(128):
            P.op("dve", lambda e, jj=jj: e.tensor_tensor(t3, sc3, V4[:, jj], ALU.add), reads=[B["sc"], B["V2"]], writes=[B["tt"]])
            cstep(t3, A8, a8i, na8i, sc3)
        P.op("dve", lambda e: e.tensor_copy(stg[:, 0:64], sc), reads=[B["sc"]], writes=[bstg])
        P.dma("sp", self.agin3, stg, reads=[bstg], writes=[self.bag3[0]])
        P.coll("AllGather", ALU.bypass, [self.agin3], [self.agout3], PAIRS, reads=[self.bag3[0]], writes=[self.bag3[1]])
        P.dma("sp", stg, self.agout3[0:128, :], reads=[self.bag3[1]], writes=[bstg])
        P.op("dve", lambda e: e.tensor_scalar(stg, stg, self.pm[:, 0:1], None, ALU.mult), reads=[bstg, self.bpm], writes=[bstg])
        hn4 = ar[:, ust_off:ust_off + 1024].rearrange("p (r g b) -> p r g b", r=2, b=NS)
        tq = Rb[0].rearrange("p (g b) -> p g b", b=NS)
        bcs = lambda v: v.unsqueeze(2).broadcast_to([128, 32, NS])
        A1b = co[:, 4:6, :].unsqueeze(3).broadcast_to([128, 2, 32, NS])
        bhn = [B["ust0"], B["ust1"]]
        P.op("dve", lambda e: e.tensor_tensor(hn4, vs4, A1b, ALU.mult), reads=[B["vs"], bco], writes=bhn)
        P.op("dve", lambda e: e.tensor_tensor(tq, vs4[:, 1], bcs(co[:, 7, :]), ALU.mult), reads=[B["vs"], bco], writes=[B["Rb0"]])
        P.op("dve", lambda e: e.tensor_tensor(hn4[:, 0], hn4[:, 0], tq, ALU.add), reads=bhn + [B["Rb0"]], writes=bhn)
        P.op("dve", lambda e: e.tensor_tensor(tq, vs4[:, 0], bcs(co[:, 6, :]), ALU.mult), reads=[B["vs"], bco], writes=[B["Rb0"]])
        P.op("dve", lambda e: e.tensor_tensor(hn4[:, 1], hn4[:, 1], tq, ALU.add), reads=bhn + [B["Rb0"]], writes=bhn)
        P.dma("sp", self.o_s5s[j], hn4, reads=bhn)
        P.op("dve", lambda e: e.tensor_copy(sc, stg[:, 0:64]), reads=[bstg], writes=[B["sc"]])
        for jj in range(128):
            P.op("dve", lambda e, jj=jj: e.tensor_tensor(V4[:, jj], sc3, V4[:, jj], ALU.add), reads=[B["sc"], B["V2"]], writes=[B["V2"]])
            cstep(V4[:, jj], A8, a8i, na8i, sc3)
        P.dma("sp", self.o_s5p[j], sc3, reads=[B["sc"]])
        self.hg_init = stg[:, 64:1088]
        self.bhg_init = bstg

        if self.cfg.get("estop", 99) <= 4:
            self.alias(self.arena_bufs(), list(B.values()))
            return
        o_d2, _ = self.lay.fields["s5d2"]
        Tb = Rb
        for gp in range(32):
            k = gp % 2
            T4 = Tb[k].rearrange("p (h r n) -> p h r n", h=2, r=2)
            for gh in range(2):
                g = gh * 32 + gp
                P.dma("sp", T4[:, gh, 0, :], self.s5T[j, g], reads=[bm], writes=[B[f"Rb{k}"]])
            P.dma("sp", T4[:, :, 1, :], self.s5P[j, gp].rearrange("p r n -> p r n"), reads=[bm], writes=[B[f"Rb{k}"]])
            u3 = load_u(gp, k)
            for gh in range(2):
                g = gh * 32 + gp
                hs = slice(gh * 64, gh * 64 + 64)
                pp, bp = self.ps()
                rd = [B[f"Rb{k}"], B[f"ub{k}"], B["V2"], B["vs"]]
                P.op("pe", lambda e, pp=pp, T4=T4, u3=u3, gh=gh: e.matmul(pp[:, 0:128], T4[:, gh, 0, :], u3[:, gh, 0:128], start=True, stop=False), reads=rd, writes=[bp])
                P.op("pe", lambda e, pp=pp, T4=T4, hs=hs, gp=gp: e.matmul(pp[:, 0:128], T4[hs, 0, 1, :], V4[hs, :, 0, gp], start=False, stop=False), reads=rd, writes=[bp])
                P.op("pe", lambda e, pp=pp, T4=T4, hs=hs, gp=gp: e.matmul(pp[:, 0:128], T4[hs, 1, 1, :], V4[hs, :, 1, gp], start=False, stop=True), reads=rd, writes=[bp])
                P.op("pe", lambda e, pp=pp, T4=T4, u3=u3, gh=gh: e.matmul(pp[:, 128:144], T4[:, gh, 0, :], u3[:, gh, 128:144], start=True, stop=False), reads=rd, writes=[bp])
                P.op("pe", lambda e, pp=pp, T4=T4, hs=hs, gp=gp: e.matmul(pp[:, 128:144], T4[hs, 0, 1, :], vs4[hs, 0, gp, :], start=False, stop=False), reads=rd, writes=[bp])
                P.op("pe", lambda e, pp=pp, T4=T4, hs=hs, gp=gp: e.matmul(pp[:, 128:144], T4[hs, 1, 1, :], vs4[hs, 1, gp, :], start=False, stop=True), reads=rd, writes=[bp])
                st = ust[g % 2]; bst = B[f"ust{g % 2}"]
                P.op("dve", lambda e, pp=pp, st=st, u3=u3, gh=gh, g=g: e.scalar_tensor_tensor(st[:, 0:144], u3[:, gh, :], self.pk[:, o_d2 + j * 64 + g:o_d2 + j * 64 + g + 1], pp[:, 0:144], ALU.mult, ALU.add),
                     reads=[bp, B[f"ub{k}"], self.bpk], writes=[bst])
                yb16 = st[:, 256:256 + 72].bitcast(BF16)
                P.op("act", lambda e, st=st, yb16=yb16: e.activation(yb16, st[:, 0:144], AF.Gelu_apprx_tanh), reads=[bst], writes=[bst])
                P.dma("sp", self.Yd[g * 16:(g + 1) * 16, :].rearrange("k (c j) -> c k j", c=8), yb16[:, 0:128], reads=[bst], writes=[self.bYd[g]])
                P.dma("sp", self.Yds[g * 16:(g + 1) * 16, :], yb16[0:16, 128:144], reads=[bst], writes=[self.bYd[g]])

        if self.cfg.get("estop", 99) <= 5:
            self.alias(self.arena_bufs(), list(B.values()))
            return
        yp = ar[:, 0:4160].bitcast(BF16).rearrange("p (a t) -> p a t", t=T)
        byp = [Buf() for _ in range(8)]
        yg = ar[:, 4160:4160 + 2080].bitcast(BF16).rearrange("p (a t) -> p a t", t=T)
        byg = [[Buf() for _ in range(3)] for _ in range(4)]
        sgm = [ar[:, 6240 + i * 512:6240 + (i + 1) * 512] for i in range(2)]
        bsgm = [Buf(), Buf()]
        self.alias(byp + [b for r in byg for b in r] + bsgm, list(B.values()))
        for uc in range(8):
            P.dma("sp", yp[:, uc, 0:NPR], self.Yd[uc * 128:(uc + 1) * 128, :], reads=self.bYd[uc * 8:(uc + 1) * 8], writes=[byp[uc]])
            P.dma("sp", yp[:, uc, NPR:T], self.Yds[uc * 128:(uc + 1) * 128, :], reads=self.bYd[uc * 8:(uc + 1) * 8], writes=[byp[uc]])
        o_bg, _ = self.lay.fields["bglu"]
        ypp = lambda c, ti: yp[:, c, ti * 512:(ti + 1) * 512]
        for grp in range(2):
            for pr in range(2):
                wv, bw = self.wload(self.w_glu[j, :, (grp * 2 + pr) * 256:(grp * 2 + pr + 1) * 256], 8, 256)
                for jj in range(2):
                    oc = grp * 4 + pr * 2 + jj
                    ol = pr * 2 + jj
                    for ti in range(3):
                        t0, n = TT[ti]
                        pp, bp = self.ps()
                        for kc in range(8):
                            P.op("pe", lambda e, pp=pp, jj=jj, kc=kc, wv=wv, t0=t0, n=n: e.matmul(pp[:, 0:n], wv[:, kc, jj * 128:(jj + 1) * 128], yp[:, kc, t0:t0 + n], start=(kc == 0), stop=(kc == 7)),
                                 reads=[bw, byp[kc]], writes=[bp])
                        sg = sgm[ti % 2]; bsg = bsgm[ti % 2]
                        P.op("act", lambda e, pp=pp, sg=sg, n=n, oc=oc: e.activation(sg[:, 0:n], pp[:, 0:n], AF.Sigmoid, bias=self.pk[:, o_bg + j * 8 + oc:o_bg + j * 8 + oc + 1]), reads=[bp, self.bpk], writes=[bsg])
                        if ti < 2:
                            dst = yg[:, ol, 0:NPR].rearrange("p (j c) -> p c j", c=8)[:, 4 * ti:4 * ti + 4, :]
                            P.op("dve", lambda e, dst=dst, sg=sg, oc=oc, ti=ti: e.tensor_tensor(dst, ypp(oc, ti).rearrange("p (c j) -> p c j", c=4), sg.rearrange("p (c j) -> p c j", c=4), ALU.mult),
                                 reads=[bsg, byp[oc]], writes=[byg[ol][0], byg[ol][1]])
                        else:
                            P.op("dve", lambda e, sg=sg, oc=oc, ol=ol: e.tensor_tensor(yg[:, ol, NPR:T], yp[:, oc, NPR:T], sg[:, 0:NS], ALU.mult), reads=[bsg, byp[oc]], writes=[byg[ol][2]])
            self.outproj_group(self.w_out_ab[j], grp * 4, 4, yg, byg, G, bG)
        self.alias(list(B.values()), byp + [b for r in byg for b in r] + bsgm)
        if self.cfg.get("hgrn", True):
            self.hgrn_pass2(j, B)
        self.alias(self.arena_bufs(), list(B.values()))

    def outproj_group(self, w, k0, nk, yb, byb, G, bG):
        P = self.P
        for ocp in range(8):
            wv, bw = self.wload(w[k0 * 128:(k0 + nk) * 128, ocp * 256:(ocp + 1) * 256], nk, 256)
            for jj in range(2):
                oc = ocp * 2 + jj
                for ti in range(3):
                    t0, n = TT[ti]
                    ps_, bps_ = self.ps()
                    for kc in range(nk):
                        P.op("pe", lambda e, ps_=ps_, jj=jj, kc=kc, wv=wv, t0=t0, n=n: e.matmul(ps_[:, 0:n], wv[:, kc, jj * 128:(jj + 1) * 128], yb[:, kc, t0:t0 + n], start=(kc == 0), stop=(kc == nk - 1)),
                             reads=[bw, byb[kc][ti]], writes=[bps_])
                    self.resid_add(ps_, bps_, oc, ti, G, bG)

    def hg_fpath(self, j, hd, pf, bpf, n, fs, bfs, kf_dst, bkf, lg_dst, blg):
        P = self.P
        lb = self.lbt[:, j, hd:hd + 1]
        oml = self.omlb[:, j, hd:hd + 1]
        P.op("act", lambda e: e.activation(fs[:, 0:n], pf[:, 0:n], AF.Sigmoid), reads=[bpf], writes=[bfs])
        P.op("dve", lambda e: e.tensor_scalar(fs[:, 0:n], fs[:, 0:n], oml, lb, ALU.mult, ALU.add), reads=[bfs, self.blbt], writes=[bfs])
        P.op("dve", lambda e: e.tensor_scalar(kf_dst, fs[:, 0:n], -1.0, 1.0, ALU.mult, ALU.add), reads=[bfs], writes=[bkf])
        P.op("act", lambda e: e.activation(lg_dst, fs[:, 0:n], AF.Ln), reads=[bfs], writes=[blg])

    def hg_vtok(self, j, hd, wv, bw, col0, vt, bvt):
        P = self.P
        for q4 in range(4):
            pp, bp = self.ps()
            for cc in range(4):
                jc = q4 * 4 + cc
                for kc in range(NCH):
                    P.op("pe", lambda e, pp=pp, cc=cc, jc=jc, kc=kc: e.matmul(pp[0:64, cc * 128:(cc + 1) * 128], self.hT[:, kc, jc * 64:(jc + 1) * 64], wv[:, kc, col0:col0 + 128], start=(kc == 0), stop=(kc == NCH - 1)),
                         reads=[bw, self.bh[kc][jc // 8]], writes=[bp])
            P.op("act", lambda e, pp=pp, q4=q4: e.copy(vt.rearrange("p c v -> p (c v)")[0:64, q4 * 512:(q4 + 1) * 512], pp[0:64, 0:512]), reads=[bp], writes=[bvt])

    def hgrn_pass1(self, j, hgS, bhgS, B):
        P = self.P
        ar = self.arena
        X = {}
        off = [0]

        def tl(name, n):
            v = ar[:, off[0]:off[0] + n]
            off[0] += n
            X[name] = Buf()
            return v
        lg = tl("lg", 1024); kf = tl("kf", 1024); gg = tl("gg", 1024); fs = tl("fs", 512); onesf = tl("ones", 512)
        kE = tl("kE", 512).bitcast(BF16)
        vt = tl("vt", 1024).bitcast(BF16).rearrange("p (c v) -> p c v", v=128)
        kt = tl("kt", 1024).bitcast(BF16).rearrange("p (c v) -> p c v", v=128)
        gend = tl("gend", 8)
        assert off[0] <= 8192
        self.alias(list(X.values()), [B["V2"]])
        P.op("dve", lambda e: e.memset(onesf, 1.0), writes=[X["ones"]])
        for hd in range(8):
            if hd % 2 == 0:
                wf, bwf = self.wload(self.w_in_ab[j, :, 2048 + hd * 128:2048 + (hd + 2) * 128], NCH, 256)
                wi, bwi = self.wload(self.w_in_ab[j, :, 3072 + hd * 128:3072 + (hd + 2) * 128], NCH, 256)
            c0 = (hd % 2) * 128
            for ti in range(2):
                t0, n = TT[ti]
                pf, bpf = self.ps()
                for kc in range(NCH):
                    P.op("pe", lambda e, pf=pf, kc=kc, wf=wf, c0=c0, t0=t0, n=n: e.matmul(pf[:, 0:n], wf[:, kc, c0:c0 + 128], self.hT[:, kc, t0:t0 + n], start=(kc == 0), stop=(kc == NCH - 1)),
                         reads=[bwf, self.bh[kc][ti]], writes=[bpf])
                self.hg_fpath(j, hd, pf, bpf, n, fs, X["fs"], kf[:, t0:t0 + n], X["kf"], lg[:, t0:t0 + n], X["lg"])
            for ti in range(2):
                t0, n = TT[ti]
                init = 0.0 if ti == 0 else lg[:, t0 - 1:t0]
                init = 0.0 if ti == 0 else gg[:, t0 - 1:t0]
                P.op("dve", lambda e, t0=t0, n=n, init=init: e.tensor_tensor_scan(gg[:, t0:t0 + n], onesf[:, 0:n], lg[:, t0:t0 + n], init, ALU.mult, ALU.add),
                     reads=[X["lg"], X["ones"], X["gg"]], writes=[X["gg"]])
            P.op("dve", lambda e: e.tensor_scalar(lg, gg, -1.0, gg[:, NPR - 1:NPR], ALU.mult, ALU.add), reads=[X["gg"]], writes=[X["lg"]])
            P.op("act", lambda e: e.activation(lg, lg, AF.Exp), reads=[X["lg"]], writes=[X["lg"]])
            P.op("dve", lambda e: e.tensor_tensor(kE, kf, lg, ALU.mult), reads=[X["lg"], X["kf"]], writes=[X["kE"]])
            self.hg_vtok(j, hd, wi, bwi, c0, vt, X["vt"])
            for q4 in range(4):
                pp, bp = self.ps()
                ppb = pp[:].bitcast(BF16)
                for cc in range(4):
                    jc = q4 * 4 + cc
                    P.op("pe", lambda e, ppb=ppb, cc=cc, jc=jc: e.transpose(ppb[0:64, cc * 128:(cc + 1) * 128], kE[:, jc * 64:(jc + 1) * 64], self.ident_bf[:]),
                         reads=[X["kE"], self.bidb], writes=[bp])
                P.op("act", lambda e, ppb=ppb, q4=q4: e.copy(kt.rearrange("p c v -> p (c v)")[0:64, q4 * 512:(q4 + 1) * 512], ppb[0:64, 0:512]), reads=[bp], writes=[X["kt"]])
            pp, bp = self.ps()
            for jc in range(16):
                P.op("pe", lambda e, pp=pp, jc=jc: e.matmul(pp[:, 0:128], kt[0:64, jc, :], vt[0:64, jc, :], start=(jc == 0), stop=(jc == 15)), reads=[X["kt"], X["vt"]], writes=[bp])
            P.op("act", lambda e, pp=pp, hd=hd: e.copy(hgS[:, hd * 128:(hd + 1) * 128], pp[:, 0:128]), reads=[bp], writes=[bhgS])
        self.alias([B["V2"]], list(X.values()))

    def hgrn_pass2(self, j, B):
        P = self.P
        ar = self.arena
        G, bG = self.Gmod[1], self.bG[1]
        X = {}
        off = [0]

        def tl(name, n):
            v = ar[:, off[0]:off[0] + n]
            off[0] += n
            X[name] = Buf()
            return v
        yg = tl("yg", 2080).bitcast(BF16).rearrange("p (a t) -> p a t", t=T)
        byg = [[Buf() for _ in range(3)] for _ in range(4)]
        lk = tl("lk", 2048); lg = lk[:, 0:1024]; kf = lk[:, 1024:2048]; X["lg"] = X["lk"]; X["kf"] = X["lk"]
        qo = tl("qo", 1040); qf = qo[:, 0:1024]; o = qo; X["qf"] = X["qo"]; X["o"] = X["qo"]
        fs = tl("fs", 512); onesf = fs; X["ones"] = X["fs"]
        dif = tl("dif", 1024); tmp = dif[:, 0:512]; X["tmp"] = X["dif"]
        ex = lg; X["ex"] = X["lk"]
        qg = tl("qg", 512).bitcast(BF16); kg = tl("kg", 512).bitcast(BF16); qq = tl("qq", 512).bitcast(BF16)
        kk4 = [tl(f"kk4_{I}", 128 * (I + 1)).bitcast(BF16).rearrange("p (c s) -> p c s", c=16) for I in range(4)]
        kkz = tl("kkz", 128).bitcast(BF16).rearrange("p (i s) -> p i s", i=4)
        sg = tl("sg", 520).bitcast(BF16)
        vt = tl("vt", 1024).bitcast(BF16).rearrange("p (c v) -> p c v", v=128)
        S = tl("S", 128); Sb = tl("Sb", 64).bitcast(BF16)
        scb = tl("scb", 32).bitcast(BF16); kkt = tl("kkt", 64).bitcast(BF16)
        dd = tl("dd", 64)
        sm = tl("smp", 256)
        sm2 = tl("sm2", 256)
        S0 = lk; X["S0"] = X["lk"]
        assert off[0] <= 12096, off[0]
        mine = list(X.values()) + [b for r in byg for b in r]
        self.alias(mine, list(B.values()))
        o_nw, _ = self.lay.fields["hgnw"]
        o_cm, _ = self.lay.fields["cmask64"]
        cmask = self.pk[0:64, o_cm:o_cm + 64]
        d1, d2, d3 = dd[:, 0:16], dd[:, 16:32], dd[:, 32:48]
        S03 = S0.rearrange("p (b v) -> p b v", v=128)
        qs, fsm, ks, qfs, qk, vsT, osm, rs = [sm[:, i * NS:(i + 1) * NS] for i in range(8)]
        kst = sm[0:NS, 128:256]
        vst = sm2[0:NS, 0:128]
        ksel = sm2[0:NS, 128:256]
        for hd in range(8):
            hl = hd % 4
            wq, bwq = self.wload(self.w_in_ab[j, :, 1024 + hd * 128:1024 + (hd + 1) * 128], NCH, 128)
            wf, bwf = self.wload(self.w_in_ab[j, :, 2048 + hd * 128:2048 + (hd + 1) * 128], NCH, 128)
            c0 = 0
            P.op("dve", lambda e, hd=hd: e.tensor_copy(S, self.hg_init[:, hd * 128:(hd + 1) * 128]), reads=[self.bhg_init], writes=[X["S"]])
            for ti in range(3):
                t0, n = TT[ti]
                pq, bpq = self.ps()
                pf, bpf = self.ps()
                for (pt, bpt, w_, bw_) in ((pq, bpq, wq, bwq), (pf, bpf, wf, bwf)):
                    for kc in range(NCH):
                        P.op("pe", lambda e, pt=pt, kc=kc, w_=w_, c0=c0, t0=t0, n=n: e.matmul(pt[:, 0:n], w_[:, kc, c0:c0 + 128], self.hT[:, kc, t0:t0 + n], start=(kc == 0), stop=(kc == NCH - 1)),
                             reads=[bw_, self.bh[kc][ti]], writes=[bpt])
                if ti < 2:
                    P.op("act", lambda e, pq=pq, t0=t0, n=n: e.copy(qf[:, t0:t0 + n], pq[:, 0:n]), reads=[bpq], writes=[X["qf"]])
                    self.hg_fpath(j, hd, pf, bpf, n, fs, X["fs"], kf[:, t0:t0 + n], X["kf"], lg[:, t0:t0 + n], X["lg"])
                else:
                    P.op("act", lambda e, pq=pq: e.copy(qs, pq[:, 0:NS]), reads=[bpq], writes=[X["smp"]])
                    lb = self.lbt[:, j, hd:hd + 1]; oml = self.omlb[:, j, hd:hd + 1]
                    P.op("act", lambda e, pf=pf: e.activation(fsm, pf[:, 0:NS], AF.Sigmoid), reads=[bpf], writes=[X["smp"]])
                    P.op("dve", lambda e, lb=lb, oml=oml: e.tensor_scalar(fsm, fsm, oml, lb, ALU.mult, ALU.add), reads=[X["smp"], self.blbt], writes=[X["smp"]])
                    P.op("dve", lambda e: e.tensor_scalar(ks, fsm, -1.0, 1.0, ALU.mult, ALU.add), reads=[X["smp"]], writes=[X["smp"]])
                    P.op("dve", lambda e: e.tensor_tensor(qfs, qs, fsm, ALU.mult), reads=[X["smp"]], writes=[X["smp"]])
                    P.op("dve", lambda e: e.tensor_tensor(qk, qs, ks, ALU.mult), reads=[X["smp"]], writes=[X["smp"]])
            wg_, bwg_ = self.wload(self.w_in_ab[j, :, 4096 + hd * 128:4096 + (hd + 1) * 128], NCH, 128)
            for ti in range(3):
                t0, n = TT[ti]
                pg, bpg = self.ps()
                for kc in range(NCH):
                    P.op("pe", lambda e, pg=pg, kc=kc, c0=c0, t0=t0, n=n, wg_=wg_: e.matmul(pg[:, 0:n], wg_[:, kc, c0:c0 + 128], self.hT[:, kc, t0:t0 + n], start=(kc == 0), stop=(kc == NCH - 1)),
                         reads=[bwg_, self.bh[kc][ti]], writes=[bpg])
                P.op("act", lambda e, pg=pg, t0=t0, n=n: e.activation(sg[:, t0:t0 + n], pg[:, 0:n], AF.Silu), reads=[bpg], writes=[X["sg"]])
            wi, bwi = self.wload(self.w_in_ab[j, :, 3072 + hd * 128:3072 + (hd + 1) * 128], NCH, 128)
            self.hg_vtok(j, hd, wi, bwi, c0, vt, X["vt"])
            pv, bpv = self.ps()
            for kc in range(NCH):
                P.op("pe", lambda e, pv=pv, kc=kc, c0=c0, wi=wi: e.matmul(pv[0:NS, 0:128], self.hT[:, kc, NPR:T], wi[:, kc, c0:c0 + 128], start=(kc == 0), stop=(kc == NCH - 1)),
                     reads=[bwi, self.bh[kc][2]], writes=[bpv])
            for kc in range(NCH):
                P.op("pe", lambda e, pv=pv, kc=kc, c0=c0, wi=wi: e.matmul(pv[:, 128:128 + NS], wi[:, kc, c0:c0 + 128], self.hT[:, kc, NPR:T], start=(kc == 0), stop=(kc == NCH - 1)),
                     reads=[bwi, self.bh[kc][2]], writes=[bpv])
            P.op("act", lambda e, pv=pv: e.copy(vst, pv[0:NS, 0:128]), reads=[bpv], writes=[X["sm2"]])
            P.op("act", lambda e, pv=pv: e.copy(vsT, pv[:, 128:128 + NS]), reads=[bpv], writes=[X["smp"]])
            P.op("dve", lambda e: e.memset(onesf, 1.0), reads=[X["fs"]], writes=[X["fs"]])
            for ti in range(2):
                t0, n = TT[ti]
                init = 0.0 if ti == 0 else dif[:, t0 - 1:t0]
                P.op("dve", lambda e, t0=t0, n=n, init=init: e.tensor_tensor_scan(dif[:, t0:t0 + n], onesf[:, 0:n], lg[:, t0:t0 + n], init, ALU.mult, ALU.add),
                     reads=[X["lg"], X["ones"], X["dif"]], writes=[X["dif"]])
            G3 = dif.rearrange("p (c t) -> p c t", t=64)
            ex3 = ex.rearrange("p (c t) -> p c t", t=64)
            q3 = qf.rearrange("p (c t) -> p c t", t=64)
            k3 = kf.rearrange("p (c t) -> p c t", t=64)
            glast = G3[:, :, 63]
            P.op("dve", lambda e: e.tensor_copy(d3[:, 0:1], glast[:, 0:1]), reads=[X["dif"]], writes=[X["dd"]])
            P.op("dve", lambda e: e.tensor_tensor(d3[:, 1:16], glast[:, 1:16], glast[:, 0:15], ALU.subtract), reads=[X["dif"]], writes=[X["dd"]])
            P.op("act", lambda e: e.activation(d3, d3, AF.Exp), reads=[X["dd"]], writes=[X["dd"]])
            P.op("dve", lambda e: e.memset(d1[:, 0:1], 0.0), reads=[X["dd"]], writes=[X["dd"]])
            P.op("dve", lambda e: e.tensor_copy(d1[:, 1:16], glast[:, 0:15]), reads=[X["dif"]], writes=[X["dd"]])
            bc64 = lambda v: v.unsqueeze(2).broadcast_to([128, 16, 64])
            P.op("dve", lambda e: e.tensor_tensor(ex3, G3, bc64(d1), ALU.subtract), reads=[X["dif"], X["dd"]], writes=[X["ex"]])
            P.op("act", lambda e: e.activation(ex, ex, AF.Exp), reads=[X["ex"]], writes=[X["ex"]])
            P.op("dve", lambda e: e.tensor_tensor(qg, qf, ex, ALU.mult), reads=[X["qf"], X["ex"]], writes=[X["qg"]])
            P.op("dve", lambda e: e.tensor_tensor(ex3, G3, bc64(glast), ALU.subtract), reads=[X["dif"]], writes=[X["ex"]])
            P.op("act", lambda e: e.activation(ex, ex, AF.Exp, scale=-1.0), reads=[X["ex"]], writes=[X["ex"]])
            P.op("dve", lambda e: e.tensor_tensor(kg, kf, ex, ALU.mult), reads=[X["kf"], X["ex"]], writes=[X["kg"]])
            qq3 = qq.rearrange("p (c t) -> p c t", t=64)
            for I in range(4):
                w_ = 16 * (I + 1)
                gm = G3[:, :, 16 * I + 7]
                bcw = gm.unsqueeze(2).broadcast_to([128, 16, w_])
                exI = ex[:, 0:16 * w_].rearrange("p (c s) -> p c s", c=16)
                P.op("dve", lambda e, exI=exI, bcw=bcw, w_=w_: e.tensor_tensor(exI, G3[:, :, 0:w_], bcw, ALU.subtract), reads=[X["dif"]], writes=[X["ex"]])
                P.op("dve", lambda e, exI=exI: e.tensor_scalar(exI, exI, -80.0, 80.0, ALU.max, ALU.min), reads=[X["ex"]], writes=[X["ex"]])
                fsI = fs[:, 0:256].rearrange("p (c s) -> p c s", c=16)
                P.op("act", lambda e, exI=exI, fsI=fsI, I=I: e.activation(fsI, exI[:, :, 16 * I:16 * I + 16], AF.Exp), reads=[X["ex"], X["fs"]], writes=[X["fs"]])
                P.op("dve", lambda e, fsI=fsI, I=I: e.tensor_tensor(qq3[:, :, 16 * I:16 * I + 16], q3[:, :, 16 * I:16 * I + 16], fsI, ALU.mult), reads=[X["qf"], X["fs"]], writes=[X["qq"]])
                P.op("act", lambda e, exI=exI: e.activation(exI, exI, AF.Exp, scale=-1.0), reads=[X["ex"]], writes=[X["ex"]])
                P.op("dve", lambda e, exI=exI, I=I, w_=w_: e.tensor_tensor(kk4[I], k3[:, :, 0:w_], exI, ALU.mult), reads=[X["kf"], X["ex"]], writes=[X[f"kk4_{I}"]])
            P.op("dve", lambda e: e.memset(kkz, 0.0), reads=[X["kkz"]], writes=[X["kkz"]])
            for jc in range(16):
                cs = slice(jc * 64, (jc + 1) * 64)
                for I in range(4):
                    P.op("dve", lambda e, I=I, jc=jc: e.tensor_copy(kkz[:, I, 0:16 * (I + 1)], kk4[I][:, jc, :]), reads=[X[f"kk4_{I}"], X["kkz"]], writes=[X["kkz"]])
                ps1, bps1 = self.ps()
                for I in range(4):
                    P.op("pe", lambda e, ps1=ps1, I=I, jc=jc: e.matmul(ps1[0:64, 16 * I:16 * I + 16], kkz[:, I, :], qq[:, jc * 64 + 16 * I:jc * 64 + 16 * I + 16], start=True, stop=True), reads=[X["kkz"], X["qq"]], writes=[bps1])
                P.op("dve", lambda e, ps1=ps1: e.tensor_tensor(scb[0:64, 0:64], ps1[0:64, 0:64], cmask, ALU.mult), reads=[bps1, self.bpk], writes=[X["scb"]])
                ps2, bps2 = self.ps()
                ps2b = ps2[:].bitcast(BF16)
                P.op("pe", lambda e, ps2b=ps2b, cs=cs: e.transpose(ps2b[0:64, 0:128], kg[:, cs], self.ident_bf[:]), reads=[X["kg"], self.bidb], writes=[bps2])
                P.op("act", lambda e, ps2b=ps2b: e.copy(kkt[0:64, 0:128], ps2b[0:64, 0:128]), reads=[bps2], writes=[X["kkt"]])
                P.op("act", lambda e: e.copy(Sb[:, 0:128], S), reads=[X["S"]], writes=[X["Sb"]])
                ps3, bps3 = self.ps()
                P.op("pe", lambda e, ps3=ps3, jc=jc: e.matmul(ps3[:, 0:64], vt[0:64, jc, :], scb[0:64, 0:64], start=True, stop=False), reads=[X["vt"], X["scb"]], writes=[bps3])
                P.op("pe", lambda e, ps3=ps3, cs=cs: e.matmul(ps3[:, 0:64], Sb[:, 0:128], qg[:, cs], start=False, stop=True), reads=[X["Sb"], X["qg"]], writes=[bps3])
                P.op("act", lambda e, ps3=ps3, cs=cs: e.copy(o[:, cs], ps3[:, 0:64]), reads=[bps3, X["qq"], X["qg"]], writes=[X["o"]])
                ps4, bps4 = self.ps()
                P.op("pe", lambda e, ps4=ps4, jc=jc: e.matmul(ps4[:, 0:128], kkt[0:64, 0:128], vt[0:64, jc, :], start=True, stop=True), reads=[X["kkt"], X["vt"]], writes=[bps4])
                P.op("dve", lambda e, ps4=ps4, jc=jc: e.scalar_tensor_tensor(S, S, d3[:, jc:jc + 1], ps4[:, 0:128], ALU.mult, ALU.add), reads=[bps4, X["S"], X["dd"]], writes=[X["S"]])
            P.dma("sp", self.o_hgp[j, hd], S, reads=[X["S"]])
            P.dma("sp", S03, self.st_hg[j, hd], writes=[X["S0"]])
            pqk, bpqk = self.ps()
            P.op("pe", lambda e, pqk=pqk: e.matmul(pqk[:, 0:NS], self.ones_f[:], qk, start=True, stop=True), reads=[X["smp"], self.bonesf], writes=[bpqk])
            P.op("dve", lambda e, pqk=pqk: e.tensor_tensor(osm, vsT, pqk[:, 0:NS], ALU.mult), reads=[bpqk, X["smp"]], writes=[X["smp"]])
            po, bpo = self.ps()
            for b in range(NS):
                P.op("pe", lambda e, po=po, b=b: e.matmul(po[:, b:b + 1], S03[:, b, :], qfs[:, b:b + 1], start=True, stop=True), reads=[X["S0"], X["smp"]], writes=[bpo])
            P.op("dve", lambda e, po=po: e.tensor_tensor(o[:, NPR:T], osm, po[:, 0:NS], ALU.add), reads=[bpo, X["smp"]], writes=[X["o"]])
            pk_, bpk_ = self.ps()
            P.op("pe", lambda e, pk_=pk_: e.transpose(pk_[0:NS, 0:128], ks, self.fld("ident")), reads=[X["smp"], self.bpk], writes=[bpk_])
            P.op("act", lambda e, pk_=pk_: e.copy(kst, pk_[0:NS, 0:128]), reads=[bpk_], writes=[X["smp"]])
            io, _ = self.lay.fields["ident"]
            for b in range(NS):
                P.op("dve", lambda e, b=b: e.tensor_scalar(ksel, kst, self.pk[0:NS, io + b:io + b + 1], None, ALU.mult), reads=[X["smp"], self.bpk], writes=[X["sm2"]])
                pd, bpd = self.ps()
                P.op("pe", lambda e, pd=pd: e.matmul(pd[:, 0:128], ksel, vst, start=True, stop=True), reads=[X["sm2"]], writes=[bpd])
                P.op("dve", lambda e, pd=pd, b=b: e.scalar_tensor_tensor(S03[:, b, :], S03[:, b, :], fsm[:, b:b + 1], pd[:, 0:128], ALU.mult, ALU.add), reads=[bpd, X["S0"], X["smp"]], writes=[X["S0"]])
            P.dma("sp", self.o_hgs[j, hd], S03, reads=[X["S0"]])
            for ti in range(3):
                t0, n = TT[ti]
                P.op("act", lambda e, t0=t0, n=n: e.activation(tmp[:, 0:n], o[:, t0:t0 + n], AF.Square), reads=[X["o"]], writes=[X["tmp"]])
                pn, bpn = self.ps()
                P.op("pe", lambda e, pn=pn, n=n: e.matmul(pn[:, 0:n], self.ones_f[:], tmp[:, 0:n], start=True, stop=True), reads=[X["tmp"], self.bonesf], writes=[bpn])
                P.op("act", lambda e, pn=pn, n=n: e.activation(tmp[:, 0:n], pn[:, 0:n], AF.Sqrt, bias=self.eps_t[:, 0:1], scale=1.0 / 128), reads=[bpn, self.beps], writes=[X["tmp"]])
                P.op("dve", lambda e, n=n: e.reciprocal(tmp[:, 0:n], tmp[:, 0:n]), reads=[X["tmp"]], writes=[X["tmp"]])
                P.op("dve", lambda e, t0=t0, n=n: e.scalar_tensor_tensor(tmp[:, 0:n], o[:, t0:t0 + n], self.pk[:, o_nw + j:o_nw + j + 1], tmp[:, 0:n], ALU.mult, ALU.mult), reads=[X["o"], X["tmp"], self.bpk], writes=[X["tmp"]])
                P.op("dve", lambda e, t0=t0, n=n, hl=hl: e.tensor_tensor(yg[:, hl, t0:t0 + n], tmp[:, 0:n], sg[:, t0:t0 + n], ALU.mult), reads=[X["tmp"], X["sg"]], writes=[byg[hl][ti]])
            if hl == 3:
                self.outproj_group(self.w_out_ab[j], 8 + (hd // 4) * 4, 4, yg, byg, G, bG)
        self.alias(list(B.values()), mine)

    def build(self):
        P = self.P
        cfg = self.cfg
        self.eps_t = P.sbuf([128, 1], F32, "eps_t")
        self.beps = Buf()
        P.op("dve", lambda e: e.memset(self.eps_t[:], EPS), writes=[self.beps])
        self.setup()
        if cfg.get("mixer", True) and cfg.get("odd", True) and cfg.get("layers", DEPTH) > 1:
            self.odd_init()
        if cfg.get("mixer", True) and cfg.get("even", True) and cfg.get("layers", DEPTH) > 0:
            self.even_init()
        nl = cfg.get("layers", DEPTH)
        self.ada_pending = self.ada_items(0) if nl > 0 else []
        self.load_x()
        self.pump_ada(72)
        for l in range(nl):
            self.derive_mod(l, 0, 0.5)
            self.norm_mod(l, 0)
            if cfg.get("ffn", True):
                self.ffn(l, 0, 0)
            self.derive_mod(l, 1, 1.0)
            self.norm_mod(l, 1)
            if cfg.get("mixer", True):
                self.mixer(l)
            self.derive_mod(l, 2, 0.5)
            self.norm_mod(l, 2)
            if l + 1 < nl:
                self.ada_pending = self.ada_items(l + 1)
            if cfg.get("ffn", True):
                self.ffn(l, 1, 2, pump=1)
            self.pump_ada(72)
        self.final()
        return P.build()

    def mixer(self, l):
        if l % 2 == 1:
            if self.cfg.get("odd", True):
                self.odd_mixer(l)
        else:
            if self.cfg.get("even", True):
                self.even_mixer(l)


_CACHE = {}


def _get_nc(cfg):
    key = tuple(sorted(cfg.items()))
    if key not in _CACHE:
        _CACHE[key] = K(cfg).build()
    return _CACHE[key]


def _gate_band(w):
    dense = np.zeros((2560, 2560), np.float32)
    for n in range(16):
        dense[n * 160:(n + 1) * 160, n * 160:(n + 1) * 160] = w[n]
    out = np.zeros((20, 128, 3, 128), np.float32)
    for m in range(20):
        lo = 160 * ((128 * m) // 160)
        kc0 = lo // 128
        for s_ in range(3):
            kc = kc0 + s_
            if kc > 19:
                continue
            out[m, :, s_, :] = dense[kc * 128:(kc + 1) * 128, m * 128:(m + 1) * 128]
    return out


def kernel(cfg=None, **inp):
    cfg = dict(cfg or {})
    inp = {k: np.asarray(v) for k, v in inp.items()}
    pk = Pack()
    make_pack(pk, inp)
    pka = pk.array()
    xp, xs = inp["x_prompt"], inp["x_sample"]
    nl = max(1, cfg.get("layers", DEPTH))
    nf = nl if cfg.get("ffn", True) else 1
    use_odd = cfg.get("mixer", True) and cfg.get("odd", True) and nl > 1
    use_even = cfg.get("mixer", True) and cfg.get("even", True)
    shared = {"pk": pka, "w_ada": inp["w_ada"][:nl], "w_ffn_gu": inp["w_ffn_gu"][:nf], "w_ffn_d": inp["w_ffn_d"][:nf]}
    if use_odd:
        wg = np.stack([np.concatenate([_gate_band(inp["w_gate_a"][j]), _gate_band(inp["w_gate_x"][j])], axis=2).reshape(20, 128, 768)
                       for j in range(2)])
        shared.update({"w_in_c": inp["w_in_c"], "w_out_c": inp["w_out_c"], "wgate": np.ascontiguousarray(wg)})
    nje = 2 if nl > 2 else 1
    if use_even:
        def pl(a):
            return a
        bc = np.zeros((nje, 128, 4, 32, 16), np.float32)
        for j in range(nje):
            for q, key in enumerate(("s5_c_re", "s5_c_im")):
                bc[j, :, q] = inp[key][j].reshape(2, 32, 16, 64).transpose(0, 3, 1, 2).reshape(128, 32, 16)
            for q, key in enumerate(("s5_b_re", "s5_b_im")):
                bc[j, :, 2 + q] = inp[key][j].reshape(2, 32, 64, 16).transpose(0, 2, 1, 3).reshape(128, 32, 16)
        shared.update({"w_in_ab": inp["w_in_ab"][:nje], "s5_w_glu": inp["s5_w_glu"][:nje], "w_out_ab": inp["w_out_ab"][:nje], "s5bc": bc})
    in_maps = []
    for c in range(NCORES):
        seq, half = c // 2, c % 2
        xin = np.concatenate([xp[seq, half * NPR:(half + 1) * NPR], xs[c * NS:(c + 1) * NS, 0]], axis=0)
        cin = np.concatenate([inp["c_prompt"][seq:seq + 1], inp["c_sample"][c * NS:(c + 1) * NS]], axis=0)
        d = dict(shared)
        if use_even:
            st = np.zeros((nje, 128, 2, 32, NS), np.float32)
            for q, key in enumerate(("state_s5_re", "state_s5_im")):
                a = inp[key][:nje, c * NS:(c + 1) * NS]
                st[:, :, q] = a.reshape(nje, NS, 2, 32, 64).transpose(0, 2, 4, 3, 1).reshape(nje, 128, 32, NS)
            d["st_s5"] = st
            hg = inp["state_hgrn"][:nje, c * NS:(c + 1) * NS]
            d["st_hg"] = np.ascontiguousarray(hg.transpose(0, 2, 3, 1, 4))
            d["pm"] = np.full((128, 1), float(half), np.float32)
        d["xin"] = np.ascontiguousarray(xin, np.float32)
        d["cin"] = np.ascontiguousarray(cin, np.float32)
        if use_odd:
            sl = inp["state_lru"][:, c * NS:(c + 1) * NS]
            d["st_lru"] = np.ascontiguousarray(sl.reshape(2, NS, 20, 128).transpose(0, 3, 2, 1))
            scv = inp["state_conv"][:, c * NS:(c + 1) * NS]
            d["st_conv"] = np.ascontiguousarray(scv.reshape(2, NS, 3, 20, 128).transpose(0, 4, 3, 2, 1))
            d["pm"] = np.full((128, 1), float(half), np.float32)
        in_maps.append(d)
    nc = _get_nc(cfg)
    res = run_bass_kernel_spmd(nc, in_maps, core_ids=list(range(NCORES)))
    r = res.results
    f32 = np.float32
    y_prompt = np.zeros((4, 2048, D), f32)
    y_sample = np.zeros((128, 1, D), f32)
    s5r_p = np.zeros((2, 4, 64, 64), f32); s5i_p = np.zeros((2, 4, 64, 64), f32)
    hg_p = np.zeros((2, 4, 8, 128, 128), f32)
    lru_p = np.zeros((2, 4, 2560), f32); conv_p = np.zeros((2, 4, 3, 2560), f32)
    s5r_s = np.zeros((2, 128, 64, 64), f32); s5i_s = np.zeros((2, 128, 64, 64), f32)
    hg_s = np.zeros((2, 128, 8, 128, 128), f32)
    lru_s = np.zeros((2, 128, 2560), f32); conv_s = np.zeros((2, 128, 3, 2560), f32)
    for c in range(NCORES):
        seq, half = c // 2, c % 2
        rc = r[c]
        y_prompt[seq, half * NPR:(half + 1) * NPR] = rc["y"][0:NPR]
        y_sample[c * NS:(c + 1) * NS, 0] = rc["y"][NPR:T]
        bsl = slice(c * NS, (c + 1) * NS)
        if use_even:
            a = rc["o_s5s"].reshape(nje, 2, 64, 2, 32, NS)
            a = a.transpose(0, 3, 5, 1, 4, 2).reshape(nje, 2, NS, 64, 64)
            s5r_s[:nje, bsl] = a[:, 0]; s5i_s[:nje, bsl] = a[:, 1]
            hg_s[:nje, bsl] = rc["o_hgs"].transpose(0, 3, 1, 2, 4)
            if half == 1:
                a = rc["o_s5p"].reshape(nje, 2, 64, 2, 32).transpose(0, 3, 1, 4, 2).reshape(nje, 2, 64, 64)
                s5r_p[:nje, seq] = a[:, 0]; s5i_p[:nje, seq] = a[:, 1]
                hg_p[:nje, seq] = rc["o_hgp"]
        if use_odd:
            lru_s[:, bsl] = rc["o_lru_s"].transpose(0, 3, 2, 1).reshape(2, NS, 2560)
            conv_s[:, bsl] = rc["o_conv_s"].transpose(0, 4, 3, 2, 1).reshape(2, NS, 3, 2560)
            if half == 1:
                lru_p[:, seq] = rc["o_lru_p"].transpose(0, 2, 1).reshape(2, 2560)
                conv_p[:, seq] = rc["o_conv_p"].transpose(0, 3, 2, 1).reshape(2, 3, 2560)
    return (y_prompt, y_sample, s5r_p, s5i_p, hg_p, lru_p, conv_p, s5r_s, s5i_s, hg_s, lru_s, conv_s)
```
